# Optimizing a Trainium2 kernel written in Bass

```python
import jax, jax.numpy as jnp
from jax import lax
import numpy as np

D_MODEL = 1024
BATCH = 8
SEQ = 2048
DEPTH = 1

MLA_HEADS = 8
Q_LORA_RANK = 384
KV_LORA_RANK = 128
QK_NOPE_DIM = 64
QK_ROPE_DIM = 32
V_HEAD_DIM = 64
QK_HEAD_DIM = QK_NOPE_DIM + QK_ROPE_DIM
MLA_WIDTH = MLA_HEADS * V_HEAD_DIM
Q_BLOCK = 128
ROPE_THETA = 10000.0
SGU_GROUPS = 8
SGU_GROUP_DIM = 64
SGU_WIDTH = SGU_GROUPS * SGU_GROUP_DIM
CHUNK = 128
RMS_EPS = 1e-6
LN_EPS = 1e-5
DN_ALPHA = (2.0 * DEPTH) ** 0.25
DN_BETA = (8.0 * DEPTH) ** -0.25
IN_SPLITS = (Q_LORA_RANK, KV_LORA_RANK, QK_ROPE_DIM, MLA_WIDTH,
             SGU_WIDTH, SGU_WIDTH, SGU_WIDTH, D_MODEL, D_MODEL)
IN_WIDTH = sum(IN_SPLITS)

kernel_name = "hybrid_mla_sgu_gated_deepnorm"


def rms_norm(x, g):
    xf = x.astype(jnp.float32)
    y = xf * lax.rsqrt(jnp.mean(xf * xf, axis=-1, keepdims=True) + RMS_EPS)
    return (y * g.astype(jnp.float32)).astype(x.dtype)


def layer_norm(x, g, b):
    xf = x.astype(jnp.float32)
    mu = jnp.mean(xf, axis=-1, keepdims=True)
    xc = xf - mu
    var = jnp.mean(xc * xc, axis=-1, keepdims=True)
    y = xc * lax.rsqrt(var + LN_EPS) * g.astype(jnp.float32) + b.astype(jnp.float32)
    return y.astype(x.dtype)


def rope_tables(positions):
    inv_freq = ROPE_THETA ** (-jnp.arange(0, QK_ROPE_DIM, 2, dtype=jnp.float32) / QK_ROPE_DIM)
    ang = positions.astype(jnp.float32)[..., None] * inv_freq
    return jnp.cos(ang)[:, :, None, :], jnp.sin(ang)[:, :, None, :]


def apply_rope(x, cos, sin):
    xf = x.astype(jnp.float32)
    half = QK_ROPE_DIM // 2
    x1, x2 = xf[..., :half], xf[..., half:]
    return jnp.concatenate([x1 * cos - x2 * sin, x2 * cos + x1 * sin], axis=-1).astype(x.dtype)


def split_columns(h):
    parts, start = [], 0
    for size in IN_SPLITS:
        parts.append(h[..., start:start + size])
        start += size
    return parts


def mla_attention(c_q, c_kv, k_pe, cos, sin, g_q, w_uq, g_kv, w_ukv):
    q = jnp.einsum('bsr,rhd->bshd', rms_norm(c_q, g_q), w_uq)
    q = jnp.concatenate([q[..., :QK_NOPE_DIM], apply_rope(q[..., QK_NOPE_DIM:], cos, sin)], axis=-1)
    kv = jnp.einsum('bsr,rhd->bshd', rms_norm(c_kv, g_kv), w_ukv)
    k_nope, v = kv[..., :QK_NOPE_DIM], kv[..., QK_NOPE_DIM:]
    k_pe = apply_rope(k_pe[:, :, None, :], cos, sin)
    k = jnp.concatenate([k_nope, jnp.broadcast_to(k_pe, k_nope.shape[:-1] + (QK_ROPE_DIM,))], axis=-1)
    scale = QK_HEAD_DIM ** -0.5
    seq = q.shape[1]
    outs = []
    for start in range(0, seq, Q_BLOCK):
        end = start + Q_BLOCK
        s = jnp.einsum('bqhd,bkhd->bhqk', q[:, start:end], k[:, :end]).astype(jnp.float32) * scale
        causal = jnp.arange(end)[None, :] <= (start + jnp.arange(Q_BLOCK))[:, None]
        p = jax.nn.softmax(jnp.where(causal, s, -jnp.inf), axis=-1)
        outs.append(jnp.einsum('bhqk,bkhd->bqhd', p.astype(v.dtype), v[:, :end]))
    o = jnp.concatenate(outs, axis=1)
    return o.reshape(o.shape[0], seq, MLA_WIDTH)


def spatial_gating(u, v, ln_g, ln_b, w_s, b_s):
    u = jax.nn.gelu(u)
    v = layer_norm(jax.nn.gelu(v), ln_g, ln_b)
    bsz, seq, _ = v.shape
    vc = v.reshape(bsz, seq // CHUNK, CHUNK, SGU_GROUPS, SGU_GROUP_DIM)
    mixed = jnp.einsum('gts,bcsgd->bctgd', jnp.tril(w_s), vc) + b_s.T[:, :, None]
    return u * mixed.reshape(bsz, seq, SGU_WIDTH)


def setup_inputs(seed: int = 0) -> dict:
    key = jax.random.key(seed)
    ks = jax.random.split(key, 17)
    nrm = jax.random.normal
    f32 = jnp.float32
    x = nrm(ks[0], (BATCH, SEQ, D_MODEL), f32)
    offset = jax.random.randint(ks[1], (BATCH, 1), 0, 4096, dtype=jnp.int32)
    positions = (offset + jnp.arange(SEQ, dtype=jnp.int32)[None, :]).astype(jnp.int32)
    w_in = nrm(ks[2], (DEPTH, D_MODEL, IN_WIDTH), f32) * D_MODEL ** -0.5
    b_in = 0.02 * nrm(ks[3], (DEPTH, IN_WIDTH), f32)
    g_q = 1.0 + 0.02 * nrm(ks[4], (DEPTH, Q_LORA_RANK), f32)
    w_uq = nrm(ks[5], (DEPTH, Q_LORA_RANK, MLA_HEADS, QK_HEAD_DIM), f32) * Q_LORA_RANK ** -0.5
    g_kv = 1.0 + 0.02 * nrm(ks[6], (DEPTH, KV_LORA_RANK), f32)
    w_ukv = nrm(ks[7], (DEPTH, KV_LORA_RANK, MLA_HEADS, QK_NOPE_DIM + V_HEAD_DIM), f32) * KV_LORA_RANK ** -0.5
    w_oa = nrm(ks[8], (DEPTH, MLA_WIDTH, D_MODEL), f32) * (MLA_WIDTH ** -0.5 * DN_BETA)
    sgu_ln_g = 1.0 + 0.02 * nrm(ks[9], (DEPTH, SGU_WIDTH), f32)
    sgu_ln_b = 0.02 * nrm(ks[10], (DEPTH, SGU_WIDTH), f32)
    w_s = nrm(ks[11], (DEPTH, SGU_GROUPS, CHUNK, CHUNK), f32) * CHUNK ** -0.5
    b_s = 1.0 + 0.02 * nrm(ks[12], (DEPTH, SGU_GROUPS, CHUNK), f32)
    w_ob = nrm(ks[13], (DEPTH, SGU_WIDTH, D_MODEL), f32) * (SGU_WIDTH ** -0.5 * DN_BETA)
    w_out = nrm(ks[14], (DEPTH, D_MODEL, D_MODEL), f32) * (D_MODEL ** -0.5 * DN_BETA)
    ln_g = 1.0 + 0.02 * nrm(ks[15], (DEPTH, D_MODEL), f32)
    ln_b = 0.02 * nrm(ks[16], (DEPTH, D_MODEL), f32)
    return {"x": x, "positions": positions, "w_in": w_in, "b_in": b_in,
            "g_q": g_q, "w_uq": w_uq, "g_kv": g_kv, "w_ukv": w_ukv, "w_oa": w_oa,
            "sgu_ln_g": sgu_ln_g, "sgu_ln_b": sgu_ln_b, "w_s": w_s, "b_s": b_s,
            "w_ob": w_ob, "w_out": w_out, "ln_g": ln_g, "ln_b": ln_b}


def reference(x, positions, w_in, b_in, g_q, w_uq, g_kv, w_ukv, w_oa,
              sgu_ln_g, sgu_ln_b, w_s, b_s, w_ob, w_out, ln_g, ln_b):
    cos, sin = rope_tables(positions)
    for l in range(DEPTH):
        h = jnp.einsum('bsd,dn->bsn', x, w_in[l]) + b_in[l]
        c_q, c_kv, k_pe, z_a, u, v, z_b, g_a, g_b = split_columns(h)
        y_a = mla_attention(c_q, c_kv, k_pe, cos, sin, g_q[l], w_uq[l], g_kv[l], w_ukv[l]) * jax.nn.silu(z_a)
        y_b = spatial_gating(u, v, sgu_ln_g[l], sgu_ln_b[l], w_s[l], b_s[l]) * jax.nn.silu(z_b)
        merged = (jax.nn.sigmoid(g_a) * jnp.einsum('bsc,cd->bsd', y_a, w_oa[l])
                  + jax.nn.sigmoid(g_b) * jnp.einsum('bsc,cd->bsd', y_b, w_ob[l]))
        x = layer_norm(DN_ALPHA * x + jnp.einsum('bsd,de->bse', merged, w_out[l]), ln_g[l], ln_b[l])
    return x
```

```python
import math
import numpy as np
import concourse.bass as bass
import concourse.mybir as mybir
from concourse.bass_utils import run_bass_kernel_spmd
from contextlib import ExitStack

F32 = mybir.dt.float32
BF16 = mybir.dt.bfloat16
I32 = mybir.dt.int32
AF = mybir.ActivationFunctionType
ALU = mybir.AluOpType

S_ = 2048
D_ = 1024
NT = 4
TC = 512
ALPHA = 2.0 ** 0.25
QSCALE = 96.0 ** -0.5
RMS_EPS = 1e-6
LN_EPS = 1e-5
GK = 0.044715 ** 0.5
GS = 2.0 * (2.0 / math.pi) ** 0.5
PI = math.pi
C1 = 6.28125
C2 = 2.0 * math.pi - 6.28125


class Sched:
    CE = ("pe", "act", "dve", "pool")

    def __init__(self, nc, stack):
        self.nc = nc
        self.stack = stack
        self.prog = {e: [] for e in self.CE + ("sp",)}
        self.sem = {e: stack.enter_context(nc.semaphore("prog_" + e)) for e in self.CE}
        self.cnt = {e: 0 for e in self.CE}
        self.dsem = {}
        self.seen = {e: {} for e in self.CE + ("sp",)}
        self.lastw = {}
        self.readers = {}

    def _semh(self, k):
        return self.sem[k] if isinstance(k, str) else self.dsem[k[1]][0]

    def _deps(self, eng, reads, writes):
        need = {}

        def add(ev, kind):
            if ev is None:
                return
            k, v = ev
            if k == eng and eng == "pe":
                return
            if need.get(k, 0) < v:
                need[k] = v

        for r in reads:
            add(self.lastw.get(r), "raw")
        for w in writes:
            add(self.lastw.get(w), "waw")
            for k, v in self.readers.get(w, {}).items():
                add((k, v), "war")
        waits = []
        for k, v in need.items():
            if self.seen[eng].get(k, 0) >= v:
                continue
            self.seen[eng][k] = v
            waits.append((k, v))
        return waits

    def _commit(self, ev, reads, writes):
        k, v = ev
        for r in reads:
            d = self.readers.setdefault(r, {})
            if d.get(k, 0) < v:
                d[k] = v
        for w in writes:
            self.lastw[w] = ev
            self.readers[w] = {}

    def op(self, eng, fn, reads=(), writes=()):
        waits = self._deps(eng, reads, writes)
        self.cnt[eng] += 1
        self.prog[eng].append((waits, [fn], self.sem[eng], 1))
        self._commit((eng, self.cnt[eng]), reads, writes)

    def pe(self, fns, reads=(), writes=()):
        waits = self._deps("pe", reads, writes)
        self.cnt["pe"] += 1
        self.prog["pe"].append((waits, list(fns), self.sem["pe"], 1))
        self._commit(("pe", self.cnt["pe"]), reads, writes)

    def dma(self, chan, pairs, reads=(), writes=(), queue="sp"):
        waits = self._deps(queue, reads, writes)
        if chan not in self.dsem:
            self.dsem[chan] = [self.stack.enter_context(
                self.nc.semaphore("dma_%d" % len(self.dsem))), 0]
        ent = self.dsem[chan]
        first = True
        for (o, i) in pairs:
            fn = (lambda e, o=o, i=i: e.dma_start(out=o, in_=i))
            self.prog[queue].append((waits if first else [], [fn], ent[0], 16))
            first = False
        ent[1] += 16 * len(pairs)
        self._commit((("dma", chan), ent[1]), reads, writes)

    def wait_all_dma(self, eng="sp"):
        waits = []
        for chan, (s, c) in self.dsem.items():
            if c and self.seen[eng].get(("dma", chan), 0) < c:
                waits.append((("dma", chan), c))
                self.seen[eng][("dma", chan)] = c
        self.prog[eng].append((waits, [], None, 0))

    def _replay(self, name, eng):
        for waits, fns, sem, inc in self.prog[name]:
            for k, v in waits:
                eng.wait_ge(self._semh(k), v)
            for i, fn in enumerate(fns):
                ins = fn(eng)
                if i == len(fns) - 1 and sem is not None:
                    ins.then_inc(sem, inc)

    def emit(self):
        with self.nc.Block() as block:
            @block.sync
            def _(e):
                self._replay("sp", e)

            @block.tensor
            def _(e):
                self._replay("pe", e)

            @block.scalar
            def _(e):
                self._replay("act", e)

            @block.vector
            def _(e):
                self._replay("dve", e)

            @block.gpsimd
            def _(e):
                self._replay("pool", e)


def build(dbg=()):
    nc = bass.Bass("TRN2", target_bir_lowering=False)

    def din(name, shape, dt=F32):
        return nc.dram_tensor(name, shape, dt, kind="ExternalInput").ap()

    xT = din("xT", [D_, S_])
    x = din("x", [S_, D_])
    pos = din("pos", [1, S_], I32)
    w_main = din("w_main", [D_, 4096])
    b_main = din("b_main", [128, 32])
    w_v = din("w_v", [D_, 512])
    b_v = din("b_v", [1, 512])
    w_kpe = din("w_kpe", [D_, 192])
    b_kpe = din("b_kpe", [128, 2])
    w_uqa = din("w_uqa", [384, 768])
    w_uqb = din("w_uqb", [384, 768])
    g_q = din("g_q", [128, 3])
    g_kv = din("g_kv", [128, 1])
    w_kn = din("w_kn", [128, 512])
    w_vv = din("w_vv", [128, 512])
    w_oa = din("w_oa", [512, D_])
    w_ob = din("w_ob", [512, D_])
    w_out = din("w_out", [D_, D_])
    sgu_g = din("sgu_g", [128, 4])
    sgu_b = din("sgu_b", [128, 4])
    w_sT = din("w_sT", [128, 1024])
    b_s = din("b_s", [8, 128])
    ln_g = din("ln_g", [1, D_])
    ln_b = din("ln_b", [1, D_])
    consts = din("consts", [128, 388])
    out = nc.dram_tensor("out", [S_, D_], F32, kind="ExternalOutput").ap()
    dbg_out = {}

    with ExitStack() as st:
        S = Sched(nc, st)

        def sb(name, shape, dt):
            return st.enter_context(nc.sbuf_tensor(name, shape, dt))

        xTb = sb("xTb", [128, 8, S_], BF16)
        bufA = sb("bufA", [128, 8, S_], BF16)
        bufB = sb("bufB", [128, 16 * 768], BF16)
        yaT = sb("yaT", [128, 4, S_], BF16)
        bufC = sb("bufC", [128, 2, S_], F32)
        ybT = bufC[:, :, :].rearrange("p a b -> p (a b)").bitcast(BF16).rearrange("p (c n) -> p c n", c=4)
        bufD = sb("bufD", [128, 4, S_], BF16)
        T = [sb("T%d" % i, [128, S_], F32) for i in range(3)]
        wst = [sb("wst%d" % i, [128, 8, 128], F32) for i in range(2)]
        wbf = [sb("wbf%d" % i, [128, 8, 128], BF16) for i in range(2)]
        WB = sb("WB", [128, 8192], BF16)
        wkv = sb("wkv", [128, 1024], BF16)
        wsb = sb("wsb", [128, 1024], BF16)
        PT = [sb("PT%d" % i, [128, 512], BF16) for i in range(4)]
        dmy = sb("dmy", [128, 4], F32)
        rdn = sb("rdn", [128, 512], F32)
        epsq = sb("epsq", [128, 1], F32)
        nhalf = sb("nhalf", [128, 1], F32)
        bmh = sb("bmh", [128, 32], F32)
        cst = sb("cst", [128, 388], F32)
        identb = sb("identb", [128, 128], BF16)
        mnegb = sb("mnegb", [128, 128], BF16)
        onesb = sb("onesb", [128, 128], BF16)
        onesf = sb("onesf", [128, 128], F32)
        bm = sb("bm", [128, 32], F32)
        bmg = sb("bmg", [128, 32], F32)
        bk = sb("bk", [128, 2], F32)
        gq = sb("gq", [128, 3], F32)
        gkv = sb("gkv", [128, 1], F32)
        sg_g = sb("sg_g", [128, 4], F32)
        sg_b = sb("sg_b", [128, 4], F32)
        bvb = sb("bvb", [128, 512], F32)
        bsb = sb("bsb", [128, 4, 128], F32)
        bias2 = sb("bias2", [128, 4, 128], F32)
        small = sb("small", [128, 64], F32)
        ps = [st.enter_context(nc.psum_tensor("ps%d" % i, [128, 512], F32)) for i in range(8)]
        PK = ["ps%d" % i for i in range(8)]

        tri01 = cst[:, 256:384]
        invf2 = cst[:, 384:385]
        sgn = cst[:, 385:386]

        def ACT(out_, in_, func, reads, writes, bias=None, scale=1.0):
            kw = dict(out=out_, in_=in_, func=func, scale=scale)
            if bias is not None:
                kw["bias"] = bias
            S.op("act", lambda e: e.activation(**kw), reads, writes)

        def TS(eng, out_, in0, s1, s2, op0, op1, reads, writes):
            if s2 is None:
                S.op(eng, lambda e: e.tensor_scalar(out=out_, in0=in0, scalar1=s1, scalar2=None, op0=op0), reads, writes)
            else:
                S.op(eng, lambda e: e.tensor_scalar(out=out_, in0=in0, scalar1=s1, scalar2=s2, op0=op0, op1=op1), reads, writes)

        def STT(eng, out_, in0, scalar, in1, op0, op1, reads, writes):
            S.op(eng, lambda e: e.scalar_tensor_tensor(out=out_, in0=in0, scalar=scalar, in1=in1, op0=op0, op1=op1), reads, writes)

        def TT(eng, out_, in0, in1, op, reads, writes):
            S.op(eng, lambda e: e.tensor_tensor(out=out_, in0=in0, in1=in1, op=op), reads, writes)

        def CP(eng, out_, in_, reads, writes):
            S.op(eng, lambda e: e.tensor_copy(out=out_, in_=in_), reads, writes)

        def RCP(out_, in_, reads, writes):
            S.op("dve", lambda e: e.reciprocal(out=out_, in_=in_), reads, writes)

        def MSET(eng, ap, val, writes):
            S.op(eng, lambda e: e.memset(ap, val), (), writes)

        def MM(specs, reads, writes):
            fns = []
            for (o, l, r, s0, s1) in specs:
                fns.append(lambda e, o=o, l=l, r=r, s0=s0, s1=s1: e.matmul(o, lhsT=l, rhs=r, start=s0, stop=s1))
            S.pe(fns, reads, writes)

        cast_rr = [0]

        def cast_eng():
            cast_rr[0] += 1
            return "dve"

        def TK(i, qs=(0, 1, 2, 3)):
            return ["T%d_%d" % (i, q) for q in qs]

        def vpk(a, b):
            return ["Vp%d" % i for i in range(a // 768, (b - 1) // 768 + 1)]

        def DBG(name, ap, shape, keys, dt=F32):
            if name not in dbg:
                return
            d = nc.dram_tensor("dbg_" + name, list(shape), dt, kind="ExternalOutput").ap()
            dbg_out[name] = "dbg_" + name
            S.dma("dbg_" + name, [(d, ap)], reads=keys)

        def skew(n, stages):
            for step in range(n + len(stages) - 1):
                for s_i in reversed(range(len(stages))):
                    it = step - s_i
                    if 0 <= it < n:
                        stages[s_i](it)

        def rekey(eng, old, new):
            S.op(eng, lambda e: e.memset(dmy[0:1, 0:1], 0.0), (), list(old) + list(new) + ["dmy"])

        def CAST(eng, out_, in_, reads, writes, mul=None):
            if eng == "act":
                ACT(out_, in_, AF.Copy, reads, writes, scale=(1.0 if mul is None else mul))
            elif mul is None:
                CP(eng, out_, in_, reads, writes)
            else:
                TS(eng, out_, in_, mul, None, ALU.mult, None, reads, writes)

        S.dma("cst", [(cst[:], consts)], writes=["cst"])
        CP("pool", identb[:], cst[:, 0:128], ["cst"], ["identb"])
        CP("pool", mnegb[:], cst[:, 128:256], ["cst"], ["mnegb"])
        MSET("pool", onesb[:], 1.0, ["onesb"])
        MSET("pool", onesf[:], 1.0, ["onesf"])
        MSET("pool", nhalf[:], -0.5, ["nhalf"])
        MSET("pool", epsq[:], RMS_EPS, ["epsq"])

        lw_rr = [0]
        lw_engs = ["dve", "act"]

        def load_w(dst, src, n, keys_dst, extra_reads=(), dst3=None, mul=None):
            tix = lw_rr[0] % 3
            eng = lw_engs[lw_rr[0] % len(lw_engs)] if mul is None else "act"
            lw_rr[0] += 1
            stg = T[tix][:, 0:n]
            if len(src.shape) == 3:
                stg_v = stg.rearrange("p (a b) -> p a b", a=src.shape[1])
            else:
                stg_v = stg
            S.dma("T%d" % tix, [(stg_v, src)], writes=TK(tix))
            if dst3 is not None:
                if eng == "act":
                    CAST(eng, dst3, stg_v, TK(tix) + list(extra_reads), keys_dst, mul)
                else:
                    for a_ in range(src.shape[1]):
                        CAST(eng, dst3[:, a_, :], stg_v[:, a_, :], TK(tix) + list(extra_reads), keys_dst, mul)
            else:
                CAST(eng, dst, stg, TK(tix) + list(extra_reads), keys_dst, mul)

        w_main_v = w_main.rearrange("(c p) n -> p c n", p=128)

        stream_ix = [0]

        def stream_chunk(j):
            s_ = stream_ix[0] % 2
            stream_ix[0] += 1
            S.dma("wst%d" % s_, [(wst[s_][:, :, :], w_main_v[:, :, j * 128:(j + 1) * 128])], writes=["wst%d" % s_])
            CP(cast_eng(), wbf[s_][:, :, :], wst[s_][:, :, :], ["wst%d" % s_], ["wbf%d" % s_])
            return s_

        def XK(tc):
            return ["xT%d" % tc]

        def stage1(s_, tc, bank):
            tsl_ = slice(tc * TC, (tc + 1) * TC)
            MM([(ps[bank][:, :], wbf[s_][:, dk, :], xTb[:, dk, tsl_], dk == 0, dk == 7) for dk in range(8)],
               XK(tc) + ["wbf%d" % s_], [PK[bank]])

        cos2 = bufC[:, 0, :]
        sinS = bufC[:, 1, :]
        R = slice(64, 96)
        scr = bufA[:, :, :].rearrange("p a b -> p (a b)").bitcast(F32)
        A0, A1, A2 = scr[:, 0:2048], scr[:, 2048:4096], scr[:, 4096:6144]

        def AK(i):
            return ["kT%d_%d" % (h_, t_) for h_ in (2 * i, 2 * i + 1) for t_ in range(4)]

        def emit_rope_chain():
            posi = A0.bitcast(I32)
            kint = A2.bitcast(I32)
            S.dma("pos", [(posi[R, :], pos.partition_broadcast(32))], writes=AK(0))
            CP("dve", A1[R, :], posi[R, :], AK(0), AK(1))
            TS("dve", A1[R, :], A1[R, :], invf2[R, :], None, ALU.mult, None, AK(1) + ["cst"], AK(1))
            for which, shift in ((1, 0.0),):
                r_ = bufC[R, which, :]
                rk_ = ["bufC%d" % which]
                TS("dve", kint[R, :], A1[R, :], shift, 1.0 / (2 * PI), ALU.add, ALU.mult, AK(1), AK(2))
                CP("dve", A0[R, :], kint[R, :], AK(2), AK(0))
                TS("dve", r_, A1[R, :], shift, None, ALU.add, None, AK(1), rk_)
                STT("dve", r_, A0[R, :], -C1, r_, ALU.mult, ALU.add, AK(0) + rk_, rk_)
                STT("dve", r_, A0[R, :], -C2, r_, ALU.mult, ALU.add, AK(0) + rk_, rk_)
                TS("dve", A0[R, :], r_, PI, 2 * PI, ALU.is_gt, ALU.mult, rk_, AK(0))
                TT("dve", r_, r_, A0[R, :], ALU.subtract, AK(0) + rk_, rk_)
                TS("dve", A0[R, :], r_, -PI, 2 * PI, ALU.is_lt, ALU.mult, rk_, AK(0))
                TT("dve", r_, r_, A0[R, :], ALU.add, AK(0) + rk_, rk_)
                TS("dve", r_, r_, -3.1415925, 3.1415925, ALU.max, ALU.min, rk_, rk_)
            rs_, rc_ = bufC[R, 1, :], bufC[R, 0, :]
            TS("dve", rc_, rs_, PI / 2, None, ALU.add, None, ["bufC1"], ["bufC0"])
            TS("dve", A0[R, :], rc_, PI, 2 * PI, ALU.is_gt, ALU.mult, ["bufC0"], AK(0))
            TT("dve", rc_, rc_, A0[R, :], ALU.subtract, AK(0) + ["bufC0"], ["bufC0"])
            TS("dve", rc_, rc_, -3.1415925, 3.1415925, ALU.max, ALU.min, ["bufC0"], ["bufC0"])

        def emit_rope_finish():
            for which in (1, 0):
                ACT(bufC[R, which, :], bufC[R, which, :], AF.Sin, ["bufC%d" % which], ["bufC%d" % which])
            TS("dve", sinS[R, :], sinS[R, :], sgn[R, :], None, ALU.mult, None, ["bufC1", "cst"], ["bufC1"])
            DBG("cos2", bufC[R, 0, :], [32, S_], ["bufC0"])
            DBG("sinS", bufC[R, 1, :], [32, S_], ["bufC1"])

        emit_rope_chain()
        lw_engs[:] = ["act"]
        wA = WB[:, 0:4096].rearrange("p (c n) -> p c n", c=8)
        for half in range(2):
            load_w(WB[:, half * 2048:(half + 1) * 2048], w_main_v[:, half * 4:(half + 1) * 4, 0:512], 2048, ["WB"])
        wK = WB[:, 4096:5632].rearrange("p (c n) -> p c n", c=8)
        xT_v = xT.rearrange("(c p) s -> p c s", p=128)

        def load_xT(tc):
            tsl_ = slice(tc * TC, (tc + 1) * TC)
            for q4 in range(4):
                s_ = stream_ix[0] % 2
                stream_ix[0] += 1
                stg = wst[s_][:, :, :].rearrange("p a b -> p (a b)").rearrange("p (a b) -> p a b", a=2)
                S.dma("wst%d" % s_, [(stg, xT_v[:, 2 * q4:2 * q4 + 2, tsl_])], writes=["wst%d" % s_])
                if tc == 0:
                    CAST("act", xTb[:, 2 * q4:2 * q4 + 2, tsl_], stg, ["wst%d" % s_], XK(tc))
                else:
                    for a_ in range(2):
                        CP("dve", xTb[:, 2 * q4 + a_, tsl_], stg[:, a_, :], ["wst%d" % s_], XK(tc))

        load_xT(0)
        S.dma("bias", [(bm[:], b_main), (bk[:], b_kpe), (gq[:], g_q), (gkv[:], g_kv),
                       (sg_g[:], sgu_g), (sg_b[:], sgu_b),
                       (bvb[:], b_v.partition_broadcast(128))], writes=["bias"])
        S.dma("bsb", [(bsb[(g % 2) * 64:(g % 2) * 64 + 64, g // 2, :],
                       b_s[g:g + 1, :].partition_broadcast(64)) for g in range(8)], writes=["bsb"])
        TS("pool", bmg[:], bm[:], GK, None, ALU.mult, None, ["bias"], ["bmg"])
        TS("pool", bmh[:], bm[:], 0.5, None, ALU.mult, None, ["bias"], ["bmh"])
        load_w(WB[:, 4096:5632], w_kpe.rearrange("(c p) n -> p c n", p=128), 1536, ["WBk"])
        load_w(wkv[:, 0:512], w_kn, 512, ["wkv"])
        load_w(wkv[:, 512:1024], w_vv, 512, ["wkv"])
        lw_engs[:] = ["dve", "act"]
        S.dma("T2", [(T[2][:, 0:1024], w_sT)], writes=TK(2, (0, 1)))
        for g in range(8):
            TT("pool", wsb[:, g * 128:(g + 1) * 128], T[2][:, g * 128:(g + 1) * 128], tri01, ALU.mult,
               TK(2, (0, 1)) + ["cst"], ["wsb"])


        lw_engs[:] = ["dve", "act"]

        cqn = bufD[:, 0:3, :]
        ckvn = bufD[:, 3, :]
        Vp = bufB[:, :].rearrange("k (t c) -> k t c", t=16)
        MSET("pool", bufB[:, :].rearrange("k (g c) -> k g c", c=192)[:, :, 64:128], 1.0, ["VpOnes"])

        def phaseA_lat(tc):
            tsl = slice(tc * TC, (tc + 1) * TC)
            Rw = T[0][:, :].rearrange("p (j n) -> p j n", j=4)
            SQ = T[1][:, :].rearrange("p (j n) -> p j n", j=4)
            def stats_mm(j):
                if j < 3:
                    MM([(ps[2][:, :], onesf[:, :], SQ[:, j, :], j == 0, j == 2)], TK(1, (j,)) + ["onesf"], [PK[2]])
                else:
                    MM([(ps[3][:, :], onesf[:, :], SQ[:, j, :], True, True)], TK(1, (j,)) + ["onesf"], [PK[3]])

            for j in range(4):
                bank = j % 2
                MM([(ps[bank][:, :], wA[:, dk, j * 128:(j + 1) * 128], xTb[:, dk, tsl], dk == 0, dk == 7)
                    for dk in range(8)], XK(tc) + ["WB"], [PK[bank]])
                if j >= 1:
                    stats_mm(j - 1)
                ACT(Rw[:, j, :], ps[bank][:, :], AF.Identity, [PK[bank], "bias"], TK(0, (j,)), bias=bm[:, j:j + 1])
                ACT(SQ[:, j, :], ps[bank][:, :], AF.Square, [PK[bank], "bias"], TK(1, (j,)), bias=bm[:, j:j + 1])
            stats_mm(3)
            rq = T[2][:, 0:512]
            rkv = T[2][:, 512:1024]
            ACT(rq, ps[2][:, :], AF.Ln, [PK[2], "epsq"], TK(2, (0,)), bias=epsq[:, 0:1], scale=1.0 / 384)
            ACT(rkv, ps[3][:, :], AF.Ln, [PK[3], "epsq"], TK(2, (1,)), bias=epsq[:, 0:1], scale=1.0 / 128)
            ACT(rq, rq, AF.Exp, TK(2, (0,)), TK(2, (0,)), scale=-0.5)
            ACT(rkv, rkv, AF.Exp, TK(2, (1,)), TK(2, (1,)), scale=-0.5)
            STT("dve", ckvn[:, tsl], Rw[:, 3, :], gkv[:, 0:1], rkv, ALU.mult, ALU.mult,
                TK(0, (3,)) + TK(2, (1,)) + ["bias"], ["ckvn%d" % tc])
            for j in range(3):
                STT("dve", cqn[:, j, tsl], Rw[:, j, :], gq[:, j:j + 1], rq, ALU.mult, ALU.mult,
                    TK(0, (j,)) + TK(2, (0,)) + ["bias"], ["cqn%d" % tc])

        def phaseA_kpe(tc):
            tsl = slice(tc * TC, (tc + 1) * TC)
            MM([(ps[4][0:96, :], wK[:, dk, 0:96], xTb[:, dk, tsl], dk == 0, dk == 7) for dk in range(8)],
               XK(tc) + ["WBk"], [PK[4]])
            MM([(ps[5][0:96, :], wK[:, dk, 96:192], xTb[:, dk, tsl], dk == 0, dk == 7) for dk in range(8)],
               XK(tc) + ["WBk"], [PK[5]])
            t1 = T[2][:, 1024:1536]
            t2 = T[2][:, 1536:2048]
            STT("dve", t1[R, :], ps[4][R, :], bk[R, 0:1], cos2[R, tsl], ALU.add, ALU.mult,
                [PK[4], "bias", "bufC0"], TK(2, (2,)))
            STT("dve", t2[R, :], ps[5][R, :], bk[R, 1:2], sinS[R, tsl], ALU.add, ALU.mult,
                [PK[5], "bias", "bufC1"], TK(2, (3,)))
            TT("dve", t1[R, :], t1[R, :], t2[R, :], ALU.add, TK(2, (2, 3)), TK(2, (2,)))
            for h in range(8):
                CAST(("act", "dve", "act", "act", "dve", "act", "act", "dve")[h], bufA[R, h, tsl], t1[R, :],
                     TK(2, (2,)), ["kT%d_%d" % (h, tc)])

        def phaseA_up(tc):
            tsl = slice(tc * TC, (tc + 1) * TC)
            for p in range(4):
                bank = 6 + (p % 2)
                MM([(ps[bank][:, :], wkv[:, p * 128:(p + 1) * 128], ckvn[:, tsl], True, True)],
                   ["wkv", "ckvn%d" % tc], [PK[bank]])
                ACT(bufA[0:64, 2 * p, tsl], ps[bank][0:64, :], AF.Copy, [PK[bank]], ["kT%d_%d" % (2 * p, tc)])
                CP("dve", bufA[0:64, 2 * p + 1, tsl], ps[bank][64:128, :], [PK[bank]], ["kT%d_%d" % (2 * p + 1, tc)])
            for tt in range(4):
                t_ = tc * 4 + tt
                bank = 6 + (tt % 2)
                MM([(ps[bank][:, :], ckvn[:, t_ * 128:(t_ + 1) * 128], wkv[:, 512:1024], True, True)],
                   ["wkv", "ckvn%d" % tc], [PK[bank]])
                src = ps[bank][:, :].rearrange("k (p e d) -> k p e d", p=4, e=2)
                dstv = Vp[:, t_, :].rearrange("k (p c) -> k p c", c=192)
                ACT(dstv[:, :, 0:64], src[:, :, 0, :], AF.Copy, [PK[bank]], ["Vp%d" % t_])
                CP("dve", dstv[:, :, 128:192], src[:, :, 1, :], [PK[bank]], ["Vp%d" % t_])

        for tc in range(NT):
            if tc + 1 < NT:
                load_xT(tc + 1)
            phaseA_lat(tc)
            if tc >= 1:
                phaseA_up(tc - 1)
        phaseA_up(NT - 1)
        emit_rope_finish()
        for tc in range(NT):
            phaseA_kpe(tc)
        for bnk in range(2):
            MM([(ps[bnk][:, :], onesb[:, :], wsb[:, bnk * 512:(bnk + 1) * 512], True, True)], ["onesb", "wsb"], [PK[bnk]])
        for g in range(8):
            p = g // 2
            rows = slice((g % 2) * 64, (g % 2) * 64 + 64)
            STT("dve", bias2[rows, p, :], ps[g // 4][rows, (g % 4) * 128:(g % 4 + 1) * 128], sg_b[rows, p:p + 1],
                bsb[rows, p, :], ALU.mult, ALU.add, [PK[g // 4], "bias", "bsb"], ["bias2"])
        DBG("cqn", bufD[:, 0, :], [128, S_], ["cqn%d" % i for i in range(4)], BF16)
        DBG("ckvn", bufD[:, 3, :], [128, S_], ["ckvn%d" % i for i in range(4)], BF16)
        DBG("kT0", bufA[:, 0, :], [128, S_], ["kT0_%d" % i for i in range(4)], BF16)
        DBG("kT3", bufA[:, 3, :], [128, S_], ["kT3_%d" % i for i in range(4)], BF16)
        DBG("Vp", bufB[:, :], [128, 16 * 768], ["Vp%d" % i for i in range(16)] + ["VpOnes"], BF16)

        wqa = WB[:, 0:2304].rearrange("p (c n) -> p c n", c=3)
        wqb = WB[:, 2304:4608].rearrange("p (c n) -> p c n", c=3)
        for kc in range(3):
            load_w(WB[:, kc * 768:(kc + 1) * 768], w_uqa[kc * 128:(kc + 1) * 128, :], 768, ["WB", "WBk"])
            load_w(WB[:, 2304 + kc * 768:2304 + (kc + 1) * 768], w_uqb[kc * 128:(kc + 1) * 128, :], 768, ["WB", "WBk"])
        qT2 = T[1][:, :].bitcast(BF16)
        ZA = T[0]
        qall = ["qT%d_%d" % (s_, t_) for s_ in range(2) for t_ in range(4)]
        rekey("pool", TK(1), qall)
        bsb_bf = bsb[:, :, :].rearrange("p a b -> p (a b)").bitcast(BF16)
        PT.append(bsb_bf[:, 0:512])
        PT.append(bsb_bf[:, 512:1024])
        rekey("pool", ["bsb"], ["PT4", "PT5"])
        ptix = [0]
        za_slot = {}
        SB_ = (1, 6, 7)

        def emit_za(p, tc):
            tsl = slice(tc * TC, (tc + 1) * TC)
            s_ = za_slot[p]
            zb_ = 2 + (tc % 2)
            stage1(s_, tc, zb_)
            th = wst[s_][:, 0:4, :]
            zv = ZA[:, tsl].rearrange("p (a b) -> p a b", a=4)
            ACT(ZA[:, tsl], ps[zb_][:, :], AF.Identity, [PK[zb_], "bias"], TK(0, (tc,)), bias=bm[:, 4 + p:5 + p])
            ACT(th, zv, AF.Tanh, TK(0, (tc,)), ["wst%d" % s_], scale=0.5)
            STT("dve", zv, th, 1.0, zv, ALU.add, ALU.mult, ["wst%d" % s_] + TK(0, (tc,)), TK(0, (tc,)))

        def emit_qproj(h, tc):
            tsl = slice(tc * TC, (tc + 1) * TC)
            sl_ = h % 2
            qT = qT2[:, sl_ * 2048:(sl_ + 1) * 2048]
            qk = ["qT%d_%d" % (sl_, tc)]
            MM([(ps[2][0:96, :], wqa[:, kc, h * 96:(h + 1) * 96], cqn[:, kc, tsl], kc == 0, kc == 2)
                for kc in range(3)], ["WB", "cqn%d" % tc], [PK[2]])
            MM([(ps[3][0:96, :], wqb[:, kc, h * 96:(h + 1) * 96], cqn[:, kc, tsl], kc == 0, kc == 2)
                for kc in range(3)], ["WB", "cqn%d" % tc], [PK[3]])
            t1 = T[2][:, 1024:1536]
            t2 = T[2][:, 1536:2048]
            TT("dve", t1[R, :], ps[2][R, :], cos2[R, tsl], ALU.mult, [PK[2], "bufC0"], TK(2, (2,)))
            TT("dve", t2[R, :], ps[3][R, :], sinS[R, tsl], ALU.mult, [PK[3], "bufC1"], TK(2, (3,)))
            ACT(qT[0:64, tsl], ps[2][0:64, :], AF.Copy, [PK[2]], qk)
            TT("dve", qT[R, tsl], t1[R, :], t2[R, :], ALU.add, TK(2, (2, 3)), qk)

        SB4 = (1, 6, 7, 3)

        def attn_S(h, c, j, slot):
            sl_ = h % 2
            qT = qT2[:, sl_ * 2048:(sl_ + 1) * 2048]
            c0 = max(0, j - 4 * c) * 128
            sbk = SBK[slot % len(SBK)]
            k3 = slot % 6
            specs = [(ps[sbk][:, c0:512], bufA[0:96, h, j * 128:(j + 1) * 128],
                      qT[0:96, c * 512 + c0:(c + 1) * 512], True, j < 4 * c)]
            rd = ["kT%d_%d" % (h, j // 4), "qT%d_%d" % (sl_, c)]
            if j >= 4 * c:
                specs.append((ps[sbk][:, c0:c0 + 128], identb[:, :], mnegb[:, :], False, True))
                rd = rd + ["identb", "mnegb"]
            MM(specs, rd, [PK[sbk]])
            ACT(PT[k3][:, c0:512], ps[sbk][:, c0:512], AF.Exp, [PK[sbk]], ["PT%d" % k3], scale=QSCALE)

        def attn_PV(h, c, j, slot, ob):
            p, hh = divmod(h, 2)
            c0 = max(0, j - 4 * c) * 128
            k3 = slot % 6
            nj = 4 * c + 4
            vsl = slice(p * 192 + hh * 64, p * 192 + hh * 64 + 128)
            MM([(ps[ob][:, c0:512], Vp[:, j, vsl], PT[k3][:, c0:512], j == 0, j == nj - 1)],
               ["Vp%d" % j, "VpOnes", "PT%d" % k3], [PK[ob]])
            if j == nj - 1:
                orow = slice(0, 64) if hh == 0 else slice(64, 128)
                drow = slice(64, 128) if hh == 0 else slice(0, 64)
                csl = slice(c * 512, (c + 1) * 512)
                e2 = ob - 4
                accS = T[2][:, 0:512] if e2 == 0 else T[2][:, 512:1024]
                ak_ = TK(2, (e2,))
                run_deferred(tag=("norm", e2))
                ACT(accS, ps[ob][:, :], AF.Copy, [PK[ob]], ak_)

                def norm_rest(p=p, hh=hh, c=c, orow=orow, drow=drow, csl=csl, accS=accS, ak_=ak_):
                    RCP(rdn[orow, :], accS[drow, :], ak_, ["rdn"])
                    TT("dve", accS[orow, :], accS[orow, :], rdn[orow, :], ALU.mult, ak_ + ["rdn"], ak_)
                    TT("dve", yaT[orow, p, csl], accS[orow, :], ZA[orow, csl], ALU.mult,
                       ak_ + TK(0, (c,)), ["yaT%d_%d" % (p, c)])
                    if hh == 1 and p + 1 < 4:
                        deferred.append((slot_ctr[0] + 10, ("za", c), lambda p=p, c=c: emit_za(p + 1, c)))

                deferred.append((slot_ctr[0] + 5, ("norm", e2), norm_rest))

        SBK = (0, 1, 6, 7)
        slot_ctr = [0]
        LAG = 4

        za_slot[0] = stream_chunk(4)
        for tc in range(NT):
            emit_za(0, tc)
        for tc in range(NT):
            emit_qproj(0, tc)
        chunks = [(h, c) for h in range(8) for c in (3, 0, 2, 1)]
        deferred = []
        nxt = [0]
        pending = []

        def run_deferred(force=False, tag=None):
            keep = []
            items = list(deferred)
            del deferred[:]
            for it_ in items:
                due, tg, fn = it_
                if force or (tag is not None and tg == tag) or (tag is None and due <= slot_ctr[0]):
                    fn()
                else:
                    keep.append(it_)
            deferred[:0] = keep

        w_v_v = w_v.rearrange("(c p) n -> p c n", p=128)

        def prefetch_wv():
            def issue(i):
                s_ = stream_ix[0] % 2
                stream_ix[0] += 1
                stg = wst[s_][:, :, :].rearrange("p a b -> p (a b)")
                S.dma("wst%d" % s_, [(stg.rearrange("p (a b) -> p a b", a=2), w_v_v[:, 2 * i:2 * i + 2, :])],
                      writes=["wst%d" % s_])

                def cast(i=i, s_=s_, stg=stg):
                    CP("dve", WB[:, i * 1024:(i + 1) * 1024], stg, ["wst%d" % s_], ["WB", "WBk"])
                    if i + 2 < 4:
                        issue(i + 2)
                deferred.append((slot_ctr[0] + 8, ("wv", i), cast))
            issue(0)
            issue(1)

        def start_stream(ob):
            if nxt[0] >= len(chunks):
                return None
            h, c = chunks[nxt[0]]
            k = nxt[0] % 4
            nxt[0] += 1
            p, hh = divmod(h, 2)
            if k == 0 and hh == 1 and p + 1 < 4:
                za_slot[p + 1] = stream_chunk(4 + p + 1)
            if h + 1 < 8:
                emit_qproj(h + 1, (3, 0, 2, 1)[k])
            if h == 7 and k == 0:
                prefetch_wv()
            return dict(h=h, c=c, j=0, nj=4 * c + 4, ob=ob)

        streams = [start_stream(4), start_stream(5)]
        while any(s is not None for s in streams):
            for si in range(2):
                s = streams[si]
                if s is None:
                    continue
                slot = slot_ctr[0]
                slot_ctr[0] += 1
                run_deferred()
                attn_S(s["h"], s["c"], s["j"], slot)
                pending.append((s["h"], s["c"], s["j"], slot, s["ob"]))
                if len(pending) > LAG:
                    attn_PV(*pending.pop(0))
                s["j"] += 1
                if s["j"] == s["nj"]:
                    while any(pp[4] == s["ob"] for pp in pending):
                        attn_PV(*pending.pop(0))
                    streams[si] = start_stream(s["ob"])
        while pending:
            attn_PV(*pending.pop(0))
        while deferred:
            run_deferred(force=True)
        DBG("qT3", qT2[:, 2048:4096], [128, S_], ["qT1_%d" % t_ for t_ in range(4)], BF16)
        rekey("pool", qall, TK(1))
        DBG("yaT", yaT[:, 1, :], [128, S_], ["yaT1_%d" % i for i in range(4)], BF16)

        wV = WB[:, 0:4096].rearrange("p (c n) -> p c n", c=8)
        vn = bufB[:, 0:8192].rearrange("k (t f) -> k t f", t=16)
        VSETS = [((T[i][:, (2 * k) * 512:(2 * k + 1) * 512], TK(i, (2 * k,))),
                  (T[i][:, (2 * k + 1) * 512:(2 * k + 2) * 512], TK(i, (2 * k + 1,))))
                 for i in range(3) for k in range(2)]

        def vset(t):
            (vb, kvb), (sq, ksq) = VSETS[t % 6]
            e4 = t % 4
            st6 = small[:, e4 * 6:e4 * 6 + 6]
            mv = small[:, 24 + e4 * 2:26 + e4 * 2]
            rs = small[:, 32 + e4:33 + e4]
            return vb, kvb, sq, ksq, 2 + e4, st6, mv, rs, ["smallV%d" % e4]

        def v1(t):
            vb, kvb, sq, ksq, bank, st6, mv, rs, sk = vset(t)
            MM([(ps[bank][:, :], xTb[:, dk, t * 128:(t + 1) * 128], wV[:, dk, :], dk == 0, dk == 7) for dk in range(8)],
               XK(t // 4) + ["WB"], [PK[bank]])
            TT("dve", vb, ps[bank][:, :], bvb[:, :], ALU.add, [PK[bank], "bias"], kvb)
            ACT(sq, vb, AF.Square, kvb, ksq, scale=GK)

        def v2(t):
            vb, kvb, sq, ksq, bank, st6, mv, rs, sk = vset(t)
            STT("dve", sq, sq, 1.0, vb, ALU.add, ALU.mult, kvb + ksq, ksq)
            ACT(sq, sq, AF.Tanh, ksq, ksq, scale=GS / 2)

        def v3(t):
            vb, kvb, sq, ksq, bank, st6, mv, rs, sk = vset(t)
            STT("dve", vb, sq, 1.0, vb, ALU.add, ALU.mult, kvb + ksq, kvb)
            S.op("dve", lambda en, o=st6, i=vb: en.bn_stats(out=o, in_=i), kvb, sk)
            S.op("dve", lambda en, o=mv, i=st6: en.bn_aggr(out=o, in_=i), sk, sk)
            TS("pool", rs, mv[:, 1:2], 4.0 * LN_EPS, None, ALU.add, None, sk, sk)
            TT("pool", rs, rs, nhalf[:, 0:1], ALU.pow, sk + ["nhalf"], sk)

        def v4(t):
            vb, kvb, sq, ksq, bank, st6, mv, rs, sk = vset(t)
            nm_ = small[:, 40 + (t % 4):41 + (t % 4)]
            STT("dve", nm_, mv[:, 0:1], -1.0, rs, ALU.mult, ALU.mult, sk, sk)
            ACT(vn[:, t, :], vb, AF.Identity, kvb + sk, vpk(512 * t, 512 * t + 512), bias=nm_, scale=rs)

        skew(16, [v1, v2, v3, v4])
        DBG("vn", bufB[:, 0:8192], [128, 8192], vpk(0, 8192), BF16)
        def Q(i, q):
            return T[i][:, q * 512:(q + 1) * 512], TK(i, (q,))
        CSETS = [dict(ub=Q(0, 0), sq=Q(0, 1), sg=Q(0, 2), zb=Q(0, 3), mt=Q(1, 0), thz=Q(1, 1)),
                 dict(ub=Q(1, 2), sq=Q(1, 3), sg=Q(2, 0), zb=Q(2, 1), mt=Q(2, 2), thz=Q(2, 3))]
        pair_slots = {}
        w_oa_v = w_oa.rearrange("(c p) n -> p c n", p=128)
        w_ob_v = w_ob.rearrange("(c p) n -> p c n", p=128)
        pre_pieces = []
        for kc in range(4):
            for hf in range(2):
                pre_pieces.append((WB[:, kc * 1024 + hf * 512:kc * 1024 + (hf + 1) * 512], w_oa_v[:, kc, hf * 512:(hf + 1) * 512], 0.5))
        for kc in range(4):
            for hf in range(2):
                pre_pieces.append((WB[:, 4096 + kc * 1024 + hf * 512:4096 + kc * 1024 + (hf + 1) * 512],
                                   w_ob_v[:, kc, hf * 512:(hf + 1) * 512], 0.25))

        def piece_dma(i, pieces):
            dst, src, mul = pieces[i]
            S.dma("rdn", [(rdn[:, :], src)], writes=["rdn"])

        def piece_cast(i, pieces, keys):
            dst, src, mul = pieces[i]
            CAST("act", dst, rdn[:, :], ["rdn"], keys, mul)
        CS3 = CSETS + [None]

        def pset(it):
            cs = CSETS[it % 2]
            return (cs["ub"], cs["sq"], cs["sg"], cs["zb"], cs["mt"], cs["thz"])

        def pA(it):
            p, tc = divmod(it, NT)
            if tc == 0:
                pair_slots[p] = (stream_chunk(8 + p), stream_chunk(12 + p))
            su, sz = pair_slots[p]
            ju = 8 + p
            (ub, kub), (sq, ksq), (sg, ksg), (zb, kzb), (mt, kmt), (thz, kthz) = pset(it)
            e = it % 2
            bu, bz = e, 2 + e
            piece_dma(it, pre_pieces)
            stage1(su, tc, bu)
            stage1(sz, tc, bz)
            ACT(ub, ps[bu][:, :], AF.Identity, [PK[bu], "bias"], kub, bias=bm[:, ju:ju + 1])
            ACT(sq, ps[bu][:, :], AF.Square, [PK[bu], "bmg"], ksq, bias=bmg[:, ju:ju + 1], scale=GK)
            ACT(zb, ps[bz][:, :], AF.Identity, [PK[bz], "bias"], kzb, bias=bm[:, 12 + p:13 + p])
            ACT(thz, ps[bz][:, :], AF.Tanh, [PK[bz], "bmh"], kthz, bias=bmh[:, 12 + p:13 + p], scale=0.5)
            STT("dve", sq, sq, 1.0, ub, ALU.add, ALU.mult, ksq + kub, ksq)
            ACT(sg, sq, AF.Tanh, ksq, ksg, scale=GS / 2)
            STT("dve", zb, thz, 1.0, zb, ALU.add, ALU.mult, kzb + kthz, kzb)

        def pB(it):
            p, tc = divmod(it, NT)
            (ub, kub), (sq, ksq), (sg, ksg), (zb, kzb), (mt, kmt), (thz, kthz) = pset(it)
            piece_cast(it, pre_pieces, ["WB"])
            for half in range(2):
                cc0 = tc * 4 + 2 * half
                bank = 4 + 2 * (it % 2) + half
                MM([(ps[bank][:, k2 * 256:(k2 + 1) * 256], vn[:, cc0 + k2, p * 128:(p + 1) * 128],
                     wsb[:, 2 * p * 128:(2 * p + 2) * 128], True, True) for k2 in range(2)],
                   vpk(512 * cc0, 512 * cc0 + 1024) + ["wsb"], [PK[bank]])
                for rows, cs_ in ((slice(0, 64), slice(0, 128)), (slice(64, 128), slice(128, 256))):
                    in0 = ps[bank][rows, :].rearrange("f (c x) -> f c x", c=2)[:, :, cs_]
                    out_ = mt[rows, half * 256:(half + 1) * 256].rearrange("f (c x) -> f c x", c=2)
                    STT("dve", out_, in0, sg_g[rows, p:p + 1], bias2[rows, p:p + 1, :].broadcast_to([64, 2, 128]),
                        ALU.mult, ALU.add, [PK[bank], "bias", "bias2"], kmt)
            STT("dve", ub, sg, 1.0, ub, ALU.add, ALU.mult, kub + ksg, kub)

        def pC(it):
            p, tc = divmod(it, NT)
            tsl = slice(tc * TC, (tc + 1) * TC)
            (ub, kub), (sq, ksq), (sg, ksg), (zb, kzb), (mt, kmt), (thz, kthz) = pset(it)
            TT("dve", mt, mt, ub, ALU.mult, kmt + kub, kmt)
            TT("dve", ybT[:, p, tsl], mt, zb, ALU.mult, kmt + kzb, ["ybT%d_%d" % (p, tc), "bufC%d" % (p // 2)])

        skew(16, [pA, pB, pC])
        DBG("ybT", ybT[:, 1, :], [128, S_], ["ybT1_%d" % i for i in range(4)], BF16)

        woa = WB[:, 0:4096].rearrange("p (c n) -> p c n", c=4)
        wob = WB[:, 4096:8192].rearrange("p (c n) -> p c n", c=4)
        wOut = bufB[:, 0:8192].rearrange("p (c n) -> p c n", c=8)
        w_out_v = w_out.rearrange("(c p) n -> p c n", p=128)
        out_pieces = []
        for kc in range(8):
            for hf in range(2):
                out_pieces.append((bufB[:, kc * 1024 + hf * 512:kc * 1024 + (hf + 1) * 512], w_out_v[:, kc, hf * 512:(hf + 1) * 512], 0.5))
        d_it = [0]
        for m in range(8):
            sa = stream_chunk(16 + m)
            sb_ = stream_chunk(24 + m)
            SA = T[0]
            SBt = T[1]
            for tc in range(NT):
                tsl = slice(tc * TC, (tc + 1) * TC)
                di = d_it[0]
                d_it[0] += 1
                if 1 <= di <= 16:
                    dst_, _, _ = out_pieces[di - 1]
                    o0 = (di - 1) * 512
                    piece_cast(di - 1, out_pieces, vpk(o0, o0 + 512))
                if di < 16:
                    piece_dma(di, out_pieces)
                stage1(sa, tc, 0)
                ACT(SA[:, tsl], ps[0][:, :], AF.Tanh, [PK[0], "bmh"], TK(0, (tc,)), bias=bmh[:, 16 + m:17 + m], scale=0.5)
                stage1(sb_, tc, 1)
                ACT(SBt[:, tsl], ps[1][:, :], AF.Tanh, [PK[1], "bmh"], TK(1, (tc,)), bias=bmh[:, 24 + m:25 + m], scale=0.5)
                e = tc % 2
                ba, bb = 2 + 2 * e, 3 + 2 * e
                MM([(ps[ba][:, :], woa[:, kc, m * 128:(m + 1) * 128], yaT[:, kc, tsl], kc == 0, kc == 3) for kc in range(4)],
                   ["WB"] + ["yaT%d_%d" % (kc, tc) for kc in range(4)], [PK[ba]])
                MM([(ps[bb][:, :], wob[:, kc, m * 128:(m + 1) * 128], ybT[:, kc, tsl], kc == 0, kc == 3) for kc in range(4)],
                   ["WB", "bufC0", "bufC1"] + ["ybT%d_%d" % (kc, tc) for kc in range(4)], [PK[bb]])
                ta = T[2][:, (2 * e) * 512:(2 * e + 1) * 512]
                tb = T[2][:, (2 * e + 1) * 512:(2 * e + 2) * 512]
                STT("dve", ta, SA[:, tsl], 1.0, ps[ba][:, :], ALU.add, ALU.mult, [PK[ba]] + TK(0, (tc,)), TK(2, (2 * e,)))
                STT("dve", tb, SBt[:, tsl], 1.0, ps[bb][:, :], ALU.add, ALU.mult, [PK[bb]] + TK(1, (tc,)), TK(2, (2 * e + 1,)))
                TT("dve", bufA[:, m, tsl], ta, tb, ALU.add, TK(2, (2 * e, 2 * e + 1)), ["kT%d_%d" % (m, tc)])
        DBG("mergedT", bufA[:, 2, :], [128, S_], ["kT2_%d" % i for i in range(4)], BF16)

        lnv = bufD[:, :, :].rearrange("p a b -> p (a b)").bitcast(F32)
        lnG = lnv[:, 0:1024]
        lnB = lnv[:, 1024:2048]
        dkeys = ["cqn%d" % i for i in range(4)] + ["ckvn%d" % i for i in range(4)]
        S.dma("lnp", [(lnG, ln_g.partition_broadcast(128)), (lnB, ln_b.partition_broadcast(128))], writes=dkeys)
        rekey("pool", ["xT%d" % i for i in range(4)], ["xs%d" % i for i in range(8)])
        xsl = xTb[:, :, :].rearrange("p a b -> p (a b)").bitcast(F32)
        OTs = [T[1][:, 0:1024], T[1][:, 1024:2048], T[2][:, 0:1024], T[2][:, 1024:2048]]
        OTk = [TK(1, (0, 1)), TK(1, (2, 3)), TK(2, (0, 1)), TK(2, (2, 3))]

        def emit_xload(t):
            sl_ = t % 8
            S.dma("xs%d" % sl_, [(xsl[:, sl_ * 1024:(sl_ + 1) * 1024], x[t * 128:(t + 1) * 128, :])], writes=["xs%d" % sl_])

        for t in range(8):
            emit_xload(t)
        def eA(t):
            e = t % 2
            sl_ = t % 8
            o4 = t % 4
            XT = xsl[:, sl_ * 1024:(sl_ + 1) * 1024]
            OT = OTs[o4]
            r = T[0][:, e * 1024:(e + 1) * 1024]
            rk = TK(0, (2 * e, 2 * e + 1))
            st12 = small[:, 32 + e * 12:32 + e * 12 + 12]
            mv = small[:, 56 + e * 2:58 + e * 2]
            rs = small[:, 60 + e:61 + e]
            nmr = small[:, 62 + e:63 + e]
            sk = ["smallE%d" % e]
            for half in range(2):
                bank = 2 * e + half
                hs = slice(half * 512, (half + 1) * 512)
                MM([(ps[bank][:, :], bufA[:, kc, t * 128:(t + 1) * 128], wOut[:, kc, hs], kc == 0, kc == 7)
                    for kc in range(8)], ["kT%d_%d" % (kc, t // 4) for kc in range(8)] + vpk(0, 8192), [PK[bank]])
                STT("dve", r[:, hs], XT[:, hs], ALPHA, ps[bank][:, :], ALU.mult, ALU.add, ["xs%d" % sl_, PK[bank]], TK(0, (2 * e + half,)))
                S.op("dve", lambda en, o=st12[:, half * 6:half * 6 + 6], i=r[:, hs]: en.bn_stats(out=o, in_=i),
                     TK(0, (2 * e + half,)), sk)

        def eB(t):
            e = t % 2
            sl_ = t % 8
            o4 = t % 4
            XT = xsl[:, sl_ * 1024:(sl_ + 1) * 1024]
            OT = OTs[o4]
            r = T[0][:, e * 1024:(e + 1) * 1024]
            rk = TK(0, (2 * e, 2 * e + 1))
            st12 = small[:, 32 + e * 12:32 + e * 12 + 12]
            mv = small[:, 56 + e * 2:58 + e * 2]
            rs = small[:, 60 + e:61 + e]
            nmr = small[:, 62 + e:63 + e]
            sk = ["smallE%d" % e]
            S.op("dve", lambda en, o=mv, i=st12: en.bn_aggr(out=o, in_=i), sk, sk)
            TS("pool", rs, mv[:, 1:2], LN_EPS, None, ALU.add, None, sk, sk)
            TT("pool", rs, rs, nhalf[:, 0:1], ALU.pow, sk + ["nhalf"], sk)
            STT("dve", nmr, mv[:, 0:1], -1.0, rs, ALU.mult, ALU.mult, sk, sk)
            ACT(OT, r, AF.Identity, rk + sk, OTk[o4], bias=nmr, scale=rs)

        def eC(t):
            e = t % 2
            sl_ = t % 8
            o4 = t % 4
            XT = xsl[:, sl_ * 1024:(sl_ + 1) * 1024]
            OT = OTs[o4]
            r = T[0][:, e * 1024:(e + 1) * 1024]
            rk = TK(0, (2 * e, 2 * e + 1))
            st12 = small[:, 32 + e * 12:32 + e * 12 + 12]
            mv = small[:, 56 + e * 2:58 + e * 2]
            rs = small[:, 60 + e:61 + e]
            nmr = small[:, 62 + e:63 + e]
            sk = ["smallE%d" % e]
            TT("dve", OT, OT, lnG, ALU.mult, OTk[o4] + ["cqn0"], OTk[o4])
            TT("dve", OT, OT, lnB, ALU.add, OTk[o4] + ["cqn0"], OTk[o4])
            S.dma("ot%d" % o4, [(out[t * 128:(t + 1) * 128, :], OT)], reads=OTk[o4])
            if t + 8 < 16:
                emit_xload(t + 8)

        skew(16, [eA, eB, eC])
        S.wait_all_dma()
        S.emit()
    return nc, dbg_out


def _prep(x, positions, w_in, b_in, g_q, w_uq, g_kv, w_ukv, w_oa, sgu_ln_g, sgu_ln_b, w_s, b_s,
          w_ob, w_out, ln_g, ln_b):
    f = lambda a: np.ascontiguousarray(np.asarray(a), dtype=np.float32)
    W = f(w_in)[0]
    b = f(b_in)[0]
    cols = np.r_[0:512, 544:1568, 2080:4640]
    perm = np.r_[528:544, 512:528]
    w_kpe = np.zeros((D_, 2, 96), np.float32)
    w_kpe[:, 0, 64:96] = W[:, 512:544]
    w_kpe[:, 1, 64:96] = W[:, perm]
    b_kpe = np.zeros((128, 2), np.float32)
    b_kpe[64:96, 0] = b[512:544]
    b_kpe[64:96, 1] = b[perm]
    wq = f(w_uq)[0]
    w_uqb = np.zeros((384, 8, 96), np.float32)
    w_uqb[:, :, 64:96] = wq[:, :, 64 + np.r_[16:32, 0:16]]
    wkv_ = f(w_ukv)[0]
    consts = np.zeros((128, 388), np.float32)
    consts[:, 0:128] = np.eye(128, dtype=np.float32)
    kk = np.arange(128)[:, None]
    qq = np.arange(128)[None, :]
    consts[:, 128:256] = np.where(kk <= qq, 0.0, -30000.0)
    consts[:, 256:384] = np.where(kk <= qq, 1.0, 0.0)
    inv_freq = (np.float32(10000.0) ** (-np.arange(0, 32, 2, dtype=np.float32) / np.float32(32))).astype(np.float32)
    consts[64:96, 384] = np.concatenate([inv_freq, inv_freq])
    consts[64:80, 385] = -1.0
    consts[80:96, 385] = 1.0
    shared = {
        "w_main": np.ascontiguousarray(W[:, cols]),
        "b_main": np.ascontiguousarray(b[cols].reshape(32, 128).T),
        "w_v": np.ascontiguousarray(W[:, 1568:2080]),
        "b_v": np.ascontiguousarray(b[1568:2080][None, :]),
        "w_kpe": np.ascontiguousarray(w_kpe.reshape(D_, 192)),
        "b_kpe": b_kpe,
        "w_uqa": np.ascontiguousarray(wq.reshape(384, 768)),
        "w_uqb": np.ascontiguousarray(w_uqb.reshape(384, 768)),
        "g_q": np.ascontiguousarray(f(g_q)[0].reshape(3, 128).T),
        "g_kv": np.ascontiguousarray(f(g_kv)[0].reshape(1, 128).T),
        "w_kn": np.ascontiguousarray(wkv_[:, :, 0:64].reshape(128, 512)),
        "w_vv": np.ascontiguousarray(wkv_[:, :, 64:128].reshape(128, 512)),
        "w_oa": f(w_oa)[0],
        "w_ob": f(w_ob)[0],
        "w_out": f(w_out)[0],
        "sgu_g": np.ascontiguousarray(f(sgu_ln_g)[0].reshape(4, 128).T),
        "sgu_b": np.ascontiguousarray(f(sgu_ln_b)[0].reshape(4, 128).T),
        "w_sT": np.ascontiguousarray(np.transpose(f(w_s)[0], (2, 0, 1)).reshape(128, 1024)),
        "b_s": f(b_s)[0],
        "ln_g": f(ln_g)[0][None, :],
        "ln_b": f(ln_b)[0][None, :],
        "consts": consts,
    }
    xs = f(x)
    ps_ = np.ascontiguousarray(np.asarray(positions), dtype=np.int32)
    in_maps = []
    for bi in range(xs.shape[0]):
        m = dict(shared)
        m["x"] = xs[bi]
        m["xT"] = np.ascontiguousarray(xs[bi].T)
        m["pos"] = ps_[bi][None, :]
        in_maps.append(m)
    return in_maps


def kernel(**inputs):
    in_maps = _prep(**inputs)
    nc, _ = build()
    res = run_bass_kernel_spmd(nc, in_maps, core_ids=list(range(8)))
    return np.stack([np.asarray(r["out"], dtype=np.float32) for r in res.results], axis=0)
```

```python
import math
import numpy as np
import concourse.bass as bass
import concourse.mybir as mybir
from concourse.bass_utils import run_bass_kernel_spmd
from contextlib import ExitStack

F32 = mybir.dt.float32
BF16 = mybir.dt.bfloat16
I32 = mybir.dt.int32
AF = mybir.ActivationFunctionType
ALU = mybir.AluOpType

S_ = 2048
D_ = 1024
NT = 4
TC = 512
ALPHA = 2.0 ** 0.25
QSCALE = 96.0 ** -0.5
RMS_EPS = 1e-6
LN_EPS = 1e-5
GK = 0.044715 ** 0.5
GS = 2.0 * (2.0 / math.pi) ** 0.5
PI = math.pi
C1 = 6.28125
C2 = 2.0 * math.pi - 6.28125


class Sched:
    CE = ("pe", "act", "dve", "pool")

    def __init__(self, nc, stack):
        self.nc = nc
        self.stack = stack
        self.prog = {e: [] for e in self.CE + ("sp",)}
        self.sem = {e: stack.enter_context(nc.semaphore("prog_" + e)) for e in self.CE}
        self.cnt = {e: 0 for e in self.CE}
        self.dsem = {}
        self.seen = {e: {} for e in self.CE + ("sp",)}
        self.lastw = {}
        self.readers = {}

    def _semh(self, k):
        return self.sem[k] if isinstance(k, str) else self.dsem[k[1]][0]

    def _deps(self, eng, reads, writes):
        need = {}

        def add(ev, kind):
            if ev is None:
                return
            k, v = ev
            if k == eng and eng == "pe":
                return
            if need.get(k, 0) < v:
                need[k] = v

        for r in reads:
            add(self.lastw.get(r), "raw")
        for w in writes:
            add(self.lastw.get(w), "waw")
            for k, v in self.readers.get(w, {}).items():
                add((k, v), "war")
        waits = []
        for k, v in need.items():
            if self.seen[eng].get(k, 0) >= v:
                continue
            self.seen[eng][k] = v
            waits.append((k, v))
        return waits

    def _commit(self, ev, reads, writes):
        k, v = ev
        for r in reads:
            d = self.readers.setdefault(r, {})
            if d.get(k, 0) < v:
                d[k] = v
        for w in writes:
            self.lastw[w] = ev
            self.readers[w] = {}

    def op(self, eng, fn, reads=(), writes=()):
        waits = self._deps(eng, reads, writes)
        self.cnt[eng] += 1
        self.prog[eng].append((waits, [fn], self.sem[eng], 1))
        self._commit((eng, self.cnt[eng]), reads, writes)

    def pe(self, fns, reads=(), writes=()):
        waits = self._deps("pe", reads, writes)
        self.cnt["pe"] += 1
        self.prog["pe"].append((waits, list(fns), self.sem["pe"], 1))
        self._commit(("pe", self.cnt["pe"]), reads, writes)

    def dma(self, chan, pairs, reads=(), writes=(), queue="sp"):
        waits = self._deps(queue, reads, writes)
        if chan not in self.dsem:
            self.dsem[chan] = [self.stack.enter_context(
                self.nc.semaphore("dma_%d" % len(self.dsem))), 0]
        ent = self.dsem[chan]
        first = True
        for (o, i) in pairs:
            fn = (lambda e, o=o, i=i: e.dma_start(out=o, in_=i))
            self.prog[queue].append((waits if first else [], [fn], ent[0], 16))
            first = False
        ent[1] += 16 * len(pairs)
        self._commit((("dma", chan), ent[1]), reads, writes)

    def wait_all_dma(self, eng="sp"):
        waits = []
        for chan, (s, c) in self.dsem.items():
            if c and self.seen[eng].get(("dma", chan), 0) < c:
                waits.append((("dma", chan), c))
                self.seen[eng][("dma", chan)] = c
        self.prog[eng].append((waits, [], None, 0))

    def _replay(self, name, eng):
        for waits, fns, sem, inc in self.prog[name]:
            for k, v in waits:
                eng.wait_ge(self._semh(k), v)
            for i, fn in enumerate(fns):
                ins = fn(eng)
                if i == len(fns) - 1 and sem is not None:
                    ins.then_inc(sem, inc)

    def emit(self):
        with self.nc.Block() as block:
            @block.sync
            def _(e):
                self._replay("sp", e)

            @block.tensor
            def _(e):
                self._replay("pe", e)

            @block.scalar
            def _(e):
                self._replay("act", e)

            @block.vector
            def _(e):
                self._replay("dve", e)

            @block.gpsimd
            def _(e):
                self._replay("pool", e)


def build(dbg=()):
    nc = bass.Bass("TRN2", target_bir_lowering=False)

    def din(name, shape, dt=F32):
        return nc.dram_tensor(name, shape, dt, kind="ExternalInput").ap()

    xT = din("xT", [D_, S_])
    x = din("x", [S_, D_])
    pos = din("pos", [1, S_], I32)
    w_main = din("w_main", [D_, 4096])
    b_main = din("b_main", [128, 32])
    w_v = din("w_v", [D_, 512])
    b_v = din("b_v", [1, 512])
    w_kpe = din("w_kpe", [D_, 192])
    b_kpe = din("b_kpe", [128, 2])
    w_uqa = din("w_uqa", [384, 768])
    w_uqb = din("w_uqb", [384, 768])
    g_q = din("g_q", [128, 3])
    g_kv = din("g_kv", [128, 1])
    w_kn = din("w_kn", [128, 512])
    w_vv = din("w_vv", [128, 512])
    w_oa = din("w_oa", [512, D_])
    w_ob = din("w_ob", [512, D_])
    w_out = din("w_out", [D_, D_])
    sgu_g = din("sgu_g", [128, 4])
    sgu_b = din("sgu_b", [128, 4])
    w_sT = din("w_sT", [128, 1024])
    b_s = din("b_s", [8, 128])
    ln_g = din("ln_g", [1, D_])
    ln_b = din("ln_b", [1, D_])
    consts = din("consts", [128, 388])
    out = nc.dram_tensor("out", [S_, D_], F32, kind="ExternalOutput").ap()
    dbg_out = {}

    with ExitStack() as st:
        S = Sched(nc, st)

        def sb(name, shape, dt):
            return st.enter_context(nc.sbuf_tensor(name, shape, dt))

        xTb = sb("xTb", [128, 8, S_], BF16)
        bufA = sb("bufA", [128, 8, S_], BF16)
        bufB = sb("bufB", [128, 16 * 768], BF16)
        yaT = sb("yaT", [128, 4, S_], BF16)
        bufC = sb("bufC", [128, 2, S_], F32)
        ybT = bufC[:, :, :].rearrange("p a b -> p (a b)").bitcast(BF16).rearrange("p (c n) -> p c n", c=4)
        bufD = sb("bufD", [128, 4, S_], BF16)
        T = [sb("T%d" % i, [128, S_], F32) for i in range(3)]
        wst = [sb("wst%d" % i, [128, 8, 128], F32) for i in range(2)]
        wbf = [sb("wbf%d" % i, [128, 8, 128], BF16) for i in range(2)]
        WB = sb("WB", [128, 8192], BF16)
        wkv = sb("wkv", [128, 1024], BF16)
        wsb = sb("wsb", [128, 1024], BF16)
        PT = [sb("PT%d" % i, [128, 512], BF16) for i in range(4)]
        dmy = sb("dmy", [128, 4], F32)
        rdn = sb("rdn", [128, 512], F32)
        epsq = sb("epsq", [128, 1], F32)
        nhalf = sb("nhalf", [128, 1], F32)
        bmh = sb("bmh", [128, 32], F32)
        cst = sb("cst", [128, 388], F32)
        identb = sb("identb", [128, 128], BF16)
        mnegb = sb("mnegb", [128, 128], BF16)
        onesb = sb("onesb", [128, 128], BF16)
        onesf = sb("onesf", [128, 128], F32)
        bm = sb("bm", [128, 32], F32)
        bmg = sb("bmg", [128, 32], F32)
        bk = sb("bk", [128, 2], F32)
        gq = sb("gq", [128, 3], F32)
        gkv = sb("gkv", [128, 1], F32)
        sg_g = sb("sg_g", [128, 4], F32)
        sg_b = sb("sg_b", [128, 4], F32)
        bvb = sb("bvb", [128, 512], F32)
        bsb = sb("bsb", [128, 4, 128], F32)
        bias2 = sb("bias2", [128, 4, 128], F32)
        small = sb("small", [128, 64], F32)
        ps = [st.enter_context(nc.psum_tensor("ps%d" % i, [128, 512], F32)) for i in range(8)]
        PK = ["ps%d" % i for i in range(8)]

        tri01 = cst[:, 256:384]
        invf2 = cst[:, 384:385]
        sgn = cst[:, 385:386]

        def ACT(out_, in_, func, reads, writes, bias=None, scale=1.0):
            kw = dict(out=out_, in_=in_, func=func, scale=scale)
            if bias is not None:
                kw["bias"] = bias
            S.op("act", lambda e: e.activation(**kw), reads, writes)

        def TS(eng, out_, in0, s1, s2, op0, op1, reads, writes):
            if s2 is None:
                S.op(eng, lambda e: e.tensor_scalar(out=out_, in0=in0, scalar1=s1, scalar2=None, op0=op0), reads, writes)
            else:
                S.op(eng, lambda e: e.tensor_scalar(out=out_, in0=in0, scalar1=s1, scalar2=s2, op0=op0, op1=op1), reads, writes)

        def STT(eng, out_, in0, scalar, in1, op0, op1, reads, writes):
            S.op(eng, lambda e: e.scalar_tensor_tensor(out=out_, in0=in0, scalar=scalar, in1=in1, op0=op0, op1=op1), reads, writes)

        def TT(eng, out_, in0, in1, op, reads, writes):
            S.op(eng, lambda e: e.tensor_tensor(out=out_, in0=in0, in1=in1, op=op), reads, writes)

        def CP(eng, out_, in_, reads, writes):
            S.op(eng, lambda e: e.tensor_copy(out=out_, in_=in_), reads, writes)

        def RCP(out_, in_, reads, writes):
            S.op("dve", lambda e: e.reciprocal(out=out_, in_=in_), reads, writes)

        def MSET(eng, ap, val, writes):
            S.op(eng, lambda e: e.memset(ap, val), (), writes)

        def MM(specs, reads, writes):
            fns = []
            for (o, l, r, s0, s1) in specs:
                fns.append(lambda e, o=o, l=l, r=r, s0=s0, s1=s1: e.matmul(o, lhsT=l, rhs=r, start=s0, stop=s1))
            S.pe(fns, reads, writes)

        cast_rr = [0]

        def cast_eng():
            cast_rr[0] += 1
            return "dve"

        def TK(i, qs=(0, 1, 2, 3)):
            return ["T%d_%d" % (i, q) for q in qs]

        def vpk(a, b):
            return ["Vp%d" % i for i in range(a // 768, (b - 1) // 768 + 1)]

        def DBG(name, ap, shape, keys, dt=F32):
            if name not in dbg:
                return
            d = nc.dram_tensor("dbg_" + name, list(shape), dt, kind="ExternalOutput").ap()
            dbg_out[name] = "dbg_" + name
            S.dma("dbg_" + name, [(d, ap)], reads=keys)

        def skew(n, stages):
            for step in range(n + len(stages) - 1):
                for s_i in reversed(range(len(stages))):
                    it = step - s_i
                    if 0 <= it < n:
                        stages[s_i](it)

        def rekey(eng, old, new):
            S.op(eng, lambda e: e.memset(dmy[0:1, 0:1], 0.0), (), list(old) + list(new) + ["dmy"])

        def CAST(eng, out_, in_, reads, writes, mul=None):
            if eng == "act":
                ACT(out_, in_, AF.Copy, reads, writes, scale=(1.0 if mul is None else mul))
            elif mul is None:
                CP(eng, out_, in_, reads, writes)
            else:
                TS(eng, out_, in_, mul, None, ALU.mult, None, reads, writes)

        S.dma("cst", [(cst[:], consts)], writes=["cst"])
        CP("pool", identb[:], cst[:, 0:128], ["cst"], ["identb"])
        CP("pool", mnegb[:], cst[:, 128:256], ["cst"], ["mnegb"])
        MSET("pool", onesb[:], 1.0, ["onesb"])
        MSET("pool", onesf[:], 1.0, ["onesf"])
        MSET("pool", nhalf[:], -0.5, ["nhalf"])
        MSET("pool", epsq[:], RMS_EPS, ["epsq"])

        lw_rr = [0]
        lw_engs = ["dve", "act"]

        def load_w(dst, src, n, keys_dst, extra_reads=(), dst3=None, mul=None):
            tix = lw_rr[0] % 3
            eng = lw_engs[lw_rr[0] % len(lw_engs)] if mul is None else "act"
            lw_rr[0] += 1
            stg = T[tix][:, 0:n]
            if len(src.shape) == 3:
                stg_v = stg.rearrange("p (a b) -> p a b", a=src.shape[1])
            else:
                stg_v = stg
            S.dma("T%d" % tix, [(stg_v, src)], writes=TK(tix))
            if dst3 is not None:
                if eng == "act":
                    CAST(eng, dst3, stg_v, TK(tix) + list(extra_reads), keys_dst, mul)
                else:
                    for a_ in range(src.shape[1]):
                        CAST(eng, dst3[:, a_, :], stg_v[:, a_, :], TK(tix) + list(extra_reads), keys_dst, mul)
            else:
                CAST(eng, dst, stg, TK(tix) + list(extra_reads), keys_dst, mul)

        w_main_v = w_main.rearrange("(c p) n -> p c n", p=128)

        stream_ix = [0]

        def stream_chunk(j):
            s_ = stream_ix[0] % 2
            stream_ix[0] += 1
            S.dma("wst%d" % s_, [(wst[s_][:, :, :], w_main_v[:, :, j * 128:(j + 1) * 128])], writes=["wst%d" % s_])
            CP(cast_eng(), wbf[s_][:, :, :], wst[s_][:, :, :], ["wst%d" % s_], ["wbf%d" % s_])
            return s_

        def XK(tc):
            return ["xT%d" % tc]

        def stage1(s_, tc, bank):
            tsl_ = slice(tc * TC, (tc + 1) * TC)
            MM([(ps[bank][:, :], wbf[s_][:, dk, :], xTb[:, dk, tsl_], dk == 0, dk == 7) for dk in range(8)],
               XK(tc) + ["wbf%d" % s_], [PK[bank]])

        cos2 = bufC[:, 0, :]
        sinS = bufC[:, 1, :]
        R = slice(64, 96)
        scr = bufA[:, :, :].rearrange("p a b -> p (a b)").bitcast(F32)
        A0, A1, A2 = scr[:, 0:2048], scr[:, 2048:4096], scr[:, 4096:6144]

        def AK(i):
            return ["kT%d_%d" % (h_, t_) for h_ in (2 * i, 2 * i + 1) for t_ in range(4)]

        def emit_rope_chain():
            posi = A0.bitcast(I32)
            kint = A2.bitcast(I32)
            S.dma("pos", [(posi[R, :], pos.partition_broadcast(32))], writes=AK(0))
            CP("dve", A1[R, :], posi[R, :], AK(0), AK(1))
            TS("dve", A1[R, :], A1[R, :], invf2[R, :], None, ALU.mult, None, AK(1) + ["cst"], AK(1))
            for which, shift in ((1, 0.0),):
                r_ = bufC[R, which, :]
                rk_ = ["bufC%d" % which]
                TS("dve", kint[R, :], A1[R, :], shift, 1.0 / (2 * PI), ALU.add, ALU.mult, AK(1), AK(2))
                CP("dve", A0[R, :], kint[R, :], AK(2), AK(0))
                TS("dve", r_, A1[R, :], shift, None, ALU.add, None, AK(1), rk_)
                STT("dve", r_, A0[R, :], -C1, r_, ALU.mult, ALU.add, AK(0) + rk_, rk_)
                STT("dve", r_, A0[R, :], -C2, r_, ALU.mult, ALU.add, AK(0) + rk_, rk_)
                TS("dve", A0[R, :], r_, PI, 2 * PI, ALU.is_gt, ALU.mult, rk_, AK(0))
                TT("dve", r_, r_, A0[R, :], ALU.subtract, AK(0) + rk_, rk_)
                TS("dve", A0[R, :], r_, -PI, 2 * PI, ALU.is_lt, ALU.mult, rk_, AK(0))
                TT("dve", r_, r_, A0[R, :], ALU.add, AK(0) + rk_, rk_)
                TS("dve", r_, r_, -3.1415925, 3.1415925, ALU.max, ALU.min, rk_, rk_)
            rs_, rc_ = bufC[R, 1, :], bufC[R, 0, :]
            TS("dve", rc_, rs_, PI / 2, None, ALU.add, None, ["bufC1"], ["bufC0"])
            TS("dve", A0[R, :], rc_, PI, 2 * PI, ALU.is_gt, ALU.mult, ["bufC0"], AK(0))
            TT("dve", rc_, rc_, A0[R, :], ALU.subtract, AK(0) + ["bufC0"], ["bufC0"])
            TS("dve", rc_, rc_, -3.1415925, 3.1415925, ALU.max, ALU.min, ["bufC0"], ["bufC0"])

        def emit_rope_finish():
            for which in (1, 0):
                ACT(bufC[R, which, :], bufC[R, which, :], AF.Sin, ["bufC%d" % which], ["bufC%d" % which])
            TS("dve", sinS[R, :], sinS[R, :], sgn[R, :], None, ALU.mult, None, ["bufC1", "cst"], ["bufC1"])
            DBG("cos2", bufC[R, 0, :], [32, S_], ["bufC0"])
            DBG("sinS", bufC[R, 1, :], [32, S_], ["bufC1"])

        emit_rope_chain()
        lw_engs[:] = ["act"]
        wA = WB[:, 0:4096].rearrange("p (c n) -> p c n", c=8)
        for half in range(2):
            load_w(WB[:, half * 2048:(half + 1) * 2048], w_main_v[:, half * 4:(half + 1) * 4, 0:512], 2048, ["WB"])
        wK = WB[:, 4096:5632].rearrange("p (c n) -> p c n", c=8)
        xT_v = xT.rearrange("(c p) s -> p c s", p=128)

        def load_xT(tc):
            tsl_ = slice(tc * TC, (tc + 1) * TC)
            for q4 in range(4):
                s_ = stream_ix[0] % 2
                stream_ix[0] += 1
                stg = wst[s_][:, :, :].rearrange("p a b -> p (a b)").rearrange("p (a b) -> p a b", a=2)
                S.dma("wst%d" % s_, [(stg, xT_v[:, 2 * q4:2 * q4 + 2, tsl_])], writes=["wst%d" % s_])
                if tc == 0:
                    CAST("act", xTb[:, 2 * q4:2 * q4 + 2, tsl_], stg, ["wst%d" % s_], XK(tc))
                else:
                    for a_ in range(2):
                        CP("dve", xTb[:, 2 * q4 + a_, tsl_], stg[:, a_, :], ["wst%d" % s_], XK(tc))

        load_xT(0)
        S.dma("bias", [(bm[:], b_main), (bk[:], b_kpe), (gq[:], g_q), (gkv[:], g_kv),
                       (sg_g[:], sgu_g), (sg_b[:], sgu_b),
                       (bvb[:], b_v.partition_broadcast(128))], writes=["bias"])
        S.dma("bsb", [(bsb[(g % 2) * 64:(g % 2) * 64 + 64, g // 2, :],
                       b_s[g:g + 1, :].partition_broadcast(64)) for g in range(8)], writes=["bsb"])
        TS("pool", bmg[:], bm[:], GK, None, ALU.mult, None, ["bias"], ["bmg"])
        TS("pool", bmh[:], bm[:], 0.5, None, ALU.mult, None, ["bias"], ["bmh"])
        load_w(WB[:, 4096:5632], w_kpe.rearrange("(c p) n -> p c n", p=128), 1536, ["WBk"])
        load_w(wkv[:, 0:512], w_kn, 512, ["wkv"])
        load_w(wkv[:, 512:1024], w_vv, 512, ["wkv"])
        lw_engs[:] = ["dve", "act"]
        S.dma("T2", [(T[2][:, 0:1024], w_sT)], writes=TK(2, (0, 1)))
        for g in range(8):
            TT("pool", wsb[:, g * 128:(g + 1) * 128], T[2][:, g * 128:(g + 1) * 128], tri01, ALU.mult,
               TK(2, (0, 1)) + ["cst"], ["wsb"])


        lw_engs[:] = ["dve", "act"]

        cqn = bufD[:, 0:3, :]
        ckvn = bufD[:, 3, :]
        Vp = bufB[:, :].rearrange("k (t c) -> k t c", t=16)
        MSET("pool", bufB[:, :].rearrange("k (g c) -> k g c", c=192)[:, :, 64:128], 1.0, ["VpOnes"])

        def phaseA_lat(tc):
            tsl = slice(tc * TC, (tc + 1) * TC)
            Rw = T[0][:, :].rearrange("p (j n) -> p j n", j=4)
            SQ = T[1][:, :].rearrange("p (j n) -> p j n", j=4)
            def stats_mm(j):
                if j < 3:
                    MM([(ps[2][:, :], onesf[:, :], SQ[:, j, :], j == 0, j == 2)], TK(1, (j,)) + ["onesf"], [PK[2]])
                else:
                    MM([(ps[3][:, :], onesf[:, :], SQ[:, j, :], True, True)], TK(1, (j,)) + ["onesf"], [PK[3]])

            for j in range(4):
                bank = j % 2
                MM([(ps[bank][:, :], wA[:, dk, j * 128:(j + 1) * 128], xTb[:, dk, tsl], dk == 0, dk == 7)
                    for dk in range(8)], XK(tc) + ["WB"], [PK[bank]])
                if j >= 1:
                    stats_mm(j - 1)
                ACT(Rw[:, j, :], ps[bank][:, :], AF.Identity, [PK[bank], "bias"], TK(0, (j,)), bias=bm[:, j:j + 1])
                ACT(SQ[:, j, :], ps[bank][:, :], AF.Square, [PK[bank], "bias"], TK(1, (j,)), bias=bm[:, j:j + 1])
            stats_mm(3)
            rq = T[2][:, 0:512]
            rkv = T[2][:, 512:1024]
            ACT(rq, ps[2][:, :], AF.Ln, [PK[2], "epsq"], TK(2, (0,)), bias=epsq[:, 0:1], scale=1.0 / 384)
            ACT(rkv, ps[3][:, :], AF.Ln, [PK[3], "epsq"], TK(2, (1,)), bias=epsq[:, 0:1], scale=1.0 / 128)
            ACT(rq, rq, AF.Exp, TK(2, (0,)), TK(2, (0,)), scale=-0.5)
            ACT(rkv, rkv, AF.Exp, TK(2, (1,)), TK(2, (1,)), scale=-0.5)
            STT("dve", ckvn[:, tsl], Rw[:, 3, :], gkv[:, 0:1], rkv, ALU.mult, ALU.mult,
                TK(0, (3,)) + TK(2, (1,)) + ["bias"], ["ckvn%d" % tc])
            for j in range(3):
                STT("dve", cqn[:, j, tsl], Rw[:, j, :], gq[:, j:j + 1], rq, ALU.mult, ALU.mult,
                    TK(0, (j,)) + TK(2, (0,)) + ["bias"], ["cqn%d" % tc])

        def phaseA_kpe(tc):
            tsl = slice(tc * TC, (tc + 1) * TC)
            MM([(ps[4][0:96, :], wK[:, dk, 0:96], xTb[:, dk, tsl], dk == 0, dk == 7) for dk in range(8)],
               XK(tc) + ["WBk"], [PK[4]])
            MM([(ps[5][0:96, :], wK[:, dk, 96:192], xTb[:, dk, tsl], dk == 0, dk == 7) for dk in range(8)],
               XK(tc) + ["WBk"], [PK[5]])
            t1 = T[2][:, 1024:1536]
            t2 = T[2][:, 1536:2048]
            STT("dve", t1[R, :], ps[4][R, :], bk[R, 0:1], cos2[R, tsl], ALU.add, ALU.mult,
                [PK[4], "bias", "bufC0"], TK(2, (2,)))
            STT("dve", t2[R, :], ps[5][R, :], bk[R, 1:2], sinS[R, tsl], ALU.add, ALU.mult,
                [PK[5], "bias", "bufC1"], TK(2, (3,)))
            TT("dve", t1[R, :], t1[R, :], t2[R, :], ALU.add, TK(2, (2, 3)), TK(2, (2,)))
            for h in range(8):
                CAST(("act", "dve", "act", "act", "dve", "act", "act", "dve")[h], bufA[R, h, tsl], t1[R, :],
                     TK(2, (2,)), ["kT%d_%d" % (h, tc)])

        def phaseA_up(tc):
            tsl = slice(tc * TC, (tc + 1) * TC)
            for p in range(4):
                bank = 6 + (p % 2)
                MM([(ps[bank][:, :], wkv[:, p * 128:(p + 1) * 128], ckvn[:, tsl], True, True)],
                   ["wkv", "ckvn%d" % tc], [PK[bank]])
                ACT(bufA[0:64, 2 * p, tsl], ps[bank][0:64, :], AF.Copy, [PK[bank]], ["kT%d_%d" % (2 * p, tc)])
                CP("dve", bufA[0:64, 2 * p + 1, tsl], ps[bank][64:128, :], [PK[bank]], ["kT%d_%d" % (2 * p + 1, tc)])
            for tt in range(4):
                t_ = tc * 4 + tt
                bank = 6 + (tt % 2)
                MM([(ps[bank][:, :], ckvn[:, t_ * 128:(t_ + 1) * 128], wkv[:, 512:1024], True, True)],
                   ["wkv", "ckvn%d" % tc], [PK[bank]])
                src = ps[bank][:, :].rearrange("k (p e d) -> k p e d", p=4, e=2)
                dstv = Vp[:, t_, :].rearrange("k (p c) -> k p c", c=192)
                ACT(dstv[:, :, 0:64], src[:, :, 0, :], AF.Copy, [PK[bank]], ["Vp%d" % t_])
                CP("dve", dstv[:, :, 128:192], src[:, :, 1, :], [PK[bank]], ["Vp%d" % t_])

        for tc in range(NT):
            if tc + 1 < NT:
                load_xT(tc + 1)
            phaseA_lat(tc)
            if tc >= 1:
                phaseA_up(tc - 1)
        phaseA_up(NT - 1)
        emit_rope_finish()
        for tc in range(NT):
            phaseA_kpe(tc)
        for bnk in range(2):
            MM([(ps[bnk][:, :], onesb[:, :], wsb[:, bnk * 512:(bnk + 1) * 512], True, True)], ["onesb", "wsb"], [PK[bnk]])
        for g in range(8):
            p = g // 2
            rows = slice((g % 2) * 64, (g % 2) * 64 + 64)
            STT("dve", bias2[rows, p, :], ps[g // 4][rows, (g % 4) * 128:(g % 4 + 1) * 128], sg_b[rows, p:p + 1],
                bsb[rows, p, :], ALU.mult, ALU.add, [PK[g // 4], "bias", "bsb"], ["bias2"])
        DBG("cqn", bufD[:, 0, :], [128, S_], ["cqn%d" % i for i in range(4)], BF16)
        DBG("ckvn", bufD[:, 3, :], [128, S_], ["ckvn%d" % i for i in range(4)], BF16)
        DBG("kT0", bufA[:, 0, :], [128, S_], ["kT0_%d" % i for i in range(4)], BF16)
        DBG("kT3", bufA[:, 3, :], [128, S_], ["kT3_%d" % i for i in range(4)], BF16)
        DBG("Vp", bufB[:, :], [128, 16 * 768], ["Vp%d" % i for i in range(16)] + ["VpOnes"], BF16)

        wqa = WB[:, 0:2304].rearrange("p (c n) -> p c n", c=3)
        wqb = WB[:, 2304:4608].rearrange("p (c n) -> p c n", c=3)
        for kc in range(3):
            load_w(WB[:, kc * 768:(kc + 1) * 768], w_uqa[kc * 128:(kc + 1) * 128, :], 768, ["WB", "WBk"])
            load_w(WB[:, 2304 + kc * 768:2304 + (kc + 1) * 768], w_uqb[kc * 128:(kc + 1) * 128, :], 768, ["WB", "WBk"])
        qT2 = T[1][:, :].bitcast(BF16)
        ZA = T[0]
        qall = ["qT%d_%d" % (s_, t_) for s_ in range(2) for t_ in range(4)]
        rekey("pool", TK(1), qall)
        bsb_bf = bsb[:, :, :].rearrange("p a b -> p (a b)").bitcast(BF16)
        PT.append(bsb_bf[:, 0:512])
        PT.append(bsb_bf[:, 512:1024])
        rekey("pool", ["bsb"], ["PT4", "PT5"])
        ptix = [0]
        za_slot = {}
        SB_ = (1, 6, 7)

        def emit_za(p, tc):
            tsl = slice(tc * TC, (tc + 1) * TC)
            s_ = za_slot[p]
            zb_ = 2 + (tc % 2)
            stage1(s_, tc, zb_)
            th = wst[s_][:, 0:4, :]
            zv = ZA[:, tsl].rearrange("p (a b) -> p a b", a=4)
            ACT(ZA[:, tsl], ps[zb_][:, :], AF.Identity, [PK[zb_], "bias"], TK(0, (tc,)), bias=bm[:, 4 + p:5 + p])
            ACT(th, zv, AF.Tanh, TK(0, (tc,)), ["wst%d" % s_], scale=0.5)
            STT("dve", zv, th, 1.0, zv, ALU.add, ALU.mult, ["wst%d" % s_] + TK(0, (tc,)), TK(0, (tc,)))

        def emit_qproj(h, tc):
            tsl = slice(tc * TC, (tc + 1) * TC)
            sl_ = h % 2
            qT = qT2[:, sl_ * 2048:(sl_ + 1) * 2048]
            qk = ["qT%d_%d" % (sl_, tc)]
            MM([(ps[2][0:96, :], wqa[:, kc, h * 96:(h + 1) * 96], cqn[:, kc, tsl], kc == 0, kc == 2)
                for kc in range(3)], ["WB", "cqn%d" % tc], [PK[2]])
            MM([(ps[3][0:96, :], wqb[:, kc, h * 96:(h + 1) * 96], cqn[:, kc, tsl], kc == 0, kc == 2)
                for kc in range(3)], ["WB", "cqn%d" % tc], [PK[3]])
            t1 = T[2][:, 1024:1536]
            t2 = T[2][:, 1536:2048]
            TT("dve", t1[R, :], ps[2][R, :], cos2[R, tsl], ALU.mult, [PK[2], "bufC0"], TK(2, (2,)))
            TT("dve", t2[R, :], ps[3][R, :], sinS[R, tsl], ALU.mult, [PK[3], "bufC1"], TK(2, (3,)))
            ACT(qT[0:64, tsl], ps[2][0:64, :], AF.Copy, [PK[2]], qk)
            TT("dve", qT[R, tsl], t1[R, :], t2[R, :], ALU.add, TK(2, (2, 3)), qk)

        SB4 = (1, 6, 7, 3)

        def attn_S(h, c, j, slot):
            sl_ = h % 2
            qT = qT2[:, sl_ * 2048:(sl_ + 1) * 2048]
            c0 = max(0, j - 4 * c) * 128
            sbk = SBK[slot % len(SBK)]
            k3 = slot % 6
            specs = [(ps[sbk][:, c0:512], bufA[0:96, h, j * 128:(j + 1) * 128],
                      qT[0:96, c * 512 + c0:(c + 1) * 512], True, j < 4 * c)]
            rd = ["kT%d_%d" % (h, j // 4), "qT%d_%d" % (sl_, c)]
            if j >= 4 * c:
                specs.append((ps[sbk][:, c0:c0 + 128], identb[:, :], mnegb[:, :], False, True))
                rd = rd + ["identb", "mnegb"]
            MM(specs, rd, [PK[sbk]])
            ACT(PT[k3][:, c0:512], ps[sbk][:, c0:512], AF.Exp, [PK[sbk]], ["PT%d" % k3], scale=QSCALE)

        def attn_PV(h, c, j, slot, ob):
            p, hh = divmod(h, 2)
            c0 = max(0, j - 4 * c) * 128
            k3 = slot % 6
            nj = 4 * c + 4
            vsl = slice(p * 192 + hh * 64, p * 192 + hh * 64 + 128)
            MM([(ps[ob][:, c0:512], Vp[:, j, vsl], PT[k3][:, c0:512], j == 0, j == nj - 1)],
               ["Vp%d" % j, "VpOnes", "PT%d" % k3], [PK[ob]])
            if j == nj - 1:
                orow = slice(0, 64) if hh == 0 else slice(64, 128)
                drow = slice(64, 128) if hh == 0 else slice(0, 64)
                csl = slice(c * 512, (c + 1) * 512)
                e2 = ob - 4
                accS = T[2][:, 0:512] if e2 == 0 else T[2][:, 512:1024]
                ak_ = TK(2, (e2,))
                run_deferred(tag=("norm", e2))
                ACT(accS, ps[ob][:, :], AF.Copy, [PK[ob]], ak_)

                def norm_rest(p=p, hh=hh, c=c, orow=orow, drow=drow, csl=csl, accS=accS, ak_=ak_):
                    RCP(rdn[orow, :], accS[drow, :], ak_, ["rdn"])
                    TT("dve", accS[orow, :], accS[orow, :], rdn[orow, :], ALU.mult, ak_ + ["rdn"], ak_)
                    TT("dve", yaT[orow, p, csl], accS[orow, :], ZA[orow, csl], ALU.mult,
                       ak_ + TK(0, (c,)), ["yaT%d_%d" % (p, c)])
                    if hh == 1 and p + 1 < 4:
                        deferred.append((slot_ctr[0] + 10, ("za", c), lambda p=p, c=c: emit_za(p + 1, c)))

                deferred.append((slot_ctr[0] + 3, ("norm", e2), norm_rest))

        SBK = (0, 1, 6, 7)
        slot_ctr = [0]
        LAG = 4

        za_slot[0] = stream_chunk(4)
        for tc in range(NT):
            emit_za(0, tc)
        for tc in range(NT):
            emit_qproj(0, tc)
        chunks = [(h, c) for h in range(8) for c in (3, 0, 2, 1)]
        deferred = []
        nxt = [0]
        pending = []

        def run_deferred(force=False, tag=None):
            keep = []
            items = list(deferred)
            del deferred[:]
            for it_ in items:
                due, tg, fn = it_
                if force or (tag is not None and tg == tag) or (tag is None and due <= slot_ctr[0]):
                    fn()
                else:
                    keep.append(it_)
            deferred[:0] = keep

        w_v_v = w_v.rearrange("(c p) n -> p c n", p=128)

        def prefetch_wv():
            def issue(i):
                s_ = stream_ix[0] % 2
                stream_ix[0] += 1
                stg = wst[s_][:, :, :].rearrange("p a b -> p (a b)")
                S.dma("wst%d" % s_, [(stg.rearrange("p (a b) -> p a b", a=2), w_v_v[:, 2 * i:2 * i + 2, :])],
                      writes=["wst%d" % s_])

                def cast(i=i, s_=s_, stg=stg):
                    CP("dve", WB[:, i * 1024:(i + 1) * 1024], stg, ["wst%d" % s_], ["WB", "WBk"])
                    if i + 2 < 4:
                        issue(i + 2)
                deferred.append((slot_ctr[0] + 8, ("wv", i), cast))
            issue(0)
            issue(1)

        def start_stream(ob):
            if nxt[0] >= len(chunks):
                return None
            h, c = chunks[nxt[0]]
            k = nxt[0] % 4
            nxt[0] += 1
            p, hh = divmod(h, 2)
            if k == 0 and hh == 1 and p + 1 < 4:
                za_slot[p + 1] = stream_chunk(4 + p + 1)
            if h + 1 < 8:
                emit_qproj(h + 1, (3, 0, 2, 1)[k])
            if h == 7 and k == 0:
                prefetch_wv()
            return dict(h=h, c=c, j=0, nj=4 * c + 4, ob=ob)

        streams = [start_stream(4), start_stream(5)]
        while any(s is not None for s in streams):
            for si in range(2):
                s = streams[si]
                if s is None:
                    continue
                slot = slot_ctr[0]
                slot_ctr[0] += 1
                run_deferred()
                attn_S(s["h"], s["c"], s["j"], slot)
                pending.append((s["h"], s["c"], s["j"], slot, s["ob"]))
                if len(pending) > LAG:
                    attn_PV(*pending.pop(0))
                s["j"] += 1
                if s["j"] == s["nj"]:
                    while any(pp[4] == s["ob"] for pp in pending):
                        attn_PV(*pending.pop(0))
                    streams[si] = start_stream(s["ob"])
        while pending:
            attn_PV(*pending.pop(0))
        while deferred:
            run_deferred(force=True)
        DBG("qT3", qT2[:, 2048:4096], [128, S_], ["qT1_%d" % t_ for t_ in range(4)], BF16)
        rekey("pool", qall, TK(1))
        DBG("yaT", yaT[:, 1, :], [128, S_], ["yaT1_%d" % i for i in range(4)], BF16)

        wV = WB[:, 0:4096].rearrange("p (c n) -> p c n", c=8)
        vn = bufB[:, 0:8192].rearrange("k (t f) -> k t f", t=16)
        VSETS = [((T[i][:, (2 * k) * 512:(2 * k + 1) * 512], TK(i, (2 * k,))),
                  (T[i][:, (2 * k + 1) * 512:(2 * k + 2) * 512], TK(i, (2 * k + 1,))))
                 for i in range(3) for k in range(2)]

        def vset(t):
            (vb, kvb), (sq, ksq) = VSETS[t % 6]
            e4 = t % 4
            st6 = small[:, e4 * 6:e4 * 6 + 6]
            mv = small[:, 24 + e4 * 2:26 + e4 * 2]
            rs = small[:, 32 + e4:33 + e4]
            return vb, kvb, sq, ksq, 2 + e4, st6, mv, rs, ["smallV%d" % e4]

        def v1(t):
            vb, kvb, sq, ksq, bank, st6, mv, rs, sk = vset(t)
            MM([(ps[bank][:, :], xTb[:, dk, t * 128:(t + 1) * 128], wV[:, dk, :], dk == 0, dk == 7) for dk in range(8)],
               XK(t // 4) + ["WB"], [PK[bank]])
            TT("dve", vb, ps[bank][:, :], bvb[:, :], ALU.add, [PK[bank], "bias"], kvb)
            ACT(sq, vb, AF.Square, kvb, ksq, scale=GK)

        def v2(t):
            vb, kvb, sq, ksq, bank, st6, mv, rs, sk = vset(t)
            STT("dve", sq, sq, 1.0, vb, ALU.add, ALU.mult, kvb + ksq, ksq)
            ACT(sq, sq, AF.Tanh, ksq, ksq, scale=GS / 2)

        def v3(t):
            vb, kvb, sq, ksq, bank, st6, mv, rs, sk = vset(t)
            STT("dve", vb, sq, 1.0, vb, ALU.add, ALU.mult, kvb + ksq, kvb)
            S.op("dve", lambda en, o=st6, i=vb: en.bn_stats(out=o, in_=i), kvb, sk)
            S.op("dve", lambda en, o=mv, i=st6: en.bn_aggr(out=o, in_=i), sk, sk)
            TS("pool", rs, mv[:, 1:2], 4.0 * LN_EPS, None, ALU.add, None, sk, sk)
            TT("pool", rs, rs, nhalf[:, 0:1], ALU.pow, sk + ["nhalf"], sk)

        def v4(t):
            vb, kvb, sq, ksq, bank, st6, mv, rs, sk = vset(t)
            nm_ = small[:, 40 + (t % 4):41 + (t % 4)]
            STT("dve", nm_, mv[:, 0:1], -1.0, rs, ALU.mult, ALU.mult, sk, sk)
            ACT(vn[:, t, :], vb, AF.Identity, kvb + sk, vpk(512 * t, 512 * t + 512), bias=nm_, scale=rs)

        skew(16, [v1, v2, v3, v4])
        DBG("vn", bufB[:, 0:8192], [128, 8192], vpk(0, 8192), BF16)
        def Q(i, q):
            return T[i][:, q * 512:(q + 1) * 512], TK(i, (q,))
        CSETS = [dict(ub=Q(0, 0), sq=Q(0, 1), sg=Q(0, 2), zb=Q(0, 3), mt=Q(1, 0), thz=Q(1, 1)),
                 dict(ub=Q(1, 2), sq=Q(1, 3), sg=Q(2, 0), zb=Q(2, 1), mt=Q(2, 2), thz=Q(2, 3))]
        pair_slots = {}
        w_oa_v = w_oa.rearrange("(c p) n -> p c n", p=128)
        w_ob_v = w_ob.rearrange("(c p) n -> p c n", p=128)
        pre_pieces = []
        for kc in range(4):
            for hf in range(2):
                pre_pieces.append((WB[:, kc * 1024 + hf * 512:kc * 1024 + (hf + 1) * 512], w_oa_v[:, kc, hf * 512:(hf + 1) * 512], 0.5))
        for kc in range(4):
            for hf in range(2):
                pre_pieces.append((WB[:, 4096 + kc * 1024 + hf * 512:4096 + kc * 1024 + (hf + 1) * 512],
                                   w_ob_v[:, kc, hf * 512:(hf + 1) * 512], 0.25))

        def piece_dma(i, pieces):
            dst, src, mul = pieces[i]
            S.dma("rdn", [(rdn[:, :], src)], writes=["rdn"])

        def piece_cast(i, pieces, keys):
            dst, src, mul = pieces[i]
            CAST("act", dst, rdn[:, :], ["rdn"], keys, mul)
        CS3 = CSETS + [None]

        def pset(it):
            cs = CSETS[it % 2]
            return (cs["ub"], cs["sq"], cs["sg"], cs["zb"], cs["mt"], cs["thz"])

        def pA(it):
            p, tc = divmod(it, NT)
            if tc == 0:
                pair_slots[p] = (stream_chunk(8 + p), stream_chunk(12 + p))
            su, sz = pair_slots[p]
            ju = 8 + p
            (ub, kub), (sq, ksq), (sg, ksg), (zb, kzb), (mt, kmt), (thz, kthz) = pset(it)
            e = it % 2
            bu, bz = e, 2 + e
            piece_dma(it, pre_pieces)
            stage1(su, tc, bu)
            stage1(sz, tc, bz)
            ACT(ub, ps[bu][:, :], AF.Identity, [PK[bu], "bias"], kub, bias=bm[:, ju:ju + 1])
            ACT(sq, ps[bu][:, :], AF.Square, [PK[bu], "bmg"], ksq, bias=bmg[:, ju:ju + 1], scale=GK)
            ACT(zb, ps[bz][:, :], AF.Identity, [PK[bz], "bias"], kzb, bias=bm[:, 12 + p:13 + p])
            ACT(thz, ps[bz][:, :], AF.Tanh, [PK[bz], "bmh"], kthz, bias=bmh[:, 12 + p:13 + p], scale=0.5)
            STT("dve", sq, sq, 1.0, ub, ALU.add, ALU.mult, ksq + kub, ksq)
            ACT(sg, sq, AF.Tanh, ksq, ksg, scale=GS / 2)
            STT("dve", zb, thz, 1.0, zb, ALU.add, ALU.mult, kzb + kthz, kzb)

        def pB(it):
            p, tc = divmod(it, NT)
            (ub, kub), (sq, ksq), (sg, ksg), (zb, kzb), (mt, kmt), (thz, kthz) = pset(it)
            piece_cast(it, pre_pieces, ["WB"])
            for half in range(2):
                cc0 = tc * 4 + 2 * half
                bank = 4 + 2 * (it % 2) + half
                MM([(ps[bank][:, k2 * 256:(k2 + 1) * 256], vn[:, cc0 + k2, p * 128:(p + 1) * 128],
                     wsb[:, 2 * p * 128:(2 * p + 2) * 128], True, True) for k2 in range(2)],
                   vpk(512 * cc0, 512 * cc0 + 1024) + ["wsb"], [PK[bank]])
                for rows, cs_ in ((slice(0, 64), slice(0, 128)), (slice(64, 128), slice(128, 256))):
                    in0 = ps[bank][rows, :].rearrange("f (c x) -> f c x", c=2)[:, :, cs_]
                    out_ = mt[rows, half * 256:(half + 1) * 256].rearrange("f (c x) -> f c x", c=2)
                    STT("dve", out_, in0, sg_g[rows, p:p + 1], bias2[rows, p:p + 1, :].broadcast_to([64, 2, 128]),
                        ALU.mult, ALU.add, [PK[bank], "bias", "bias2"], kmt)
            STT("dve", ub, sg, 1.0, ub, ALU.add, ALU.mult, kub + ksg, kub)

        def pC(it):
            p, tc = divmod(it, NT)
            tsl = slice(tc * TC, (tc + 1) * TC)
            (ub, kub), (sq, ksq), (sg, ksg), (zb, kzb), (mt, kmt), (thz, kthz) = pset(it)
            TT("dve", mt, mt, ub, ALU.mult, kmt + kub, kmt)
            TT("dve", ybT[:, p, tsl], mt, zb, ALU.mult, kmt + kzb, ["ybT%d_%d" % (p, tc), "bufC%d" % (p // 2)])

        skew(16, [pA, pB, pC])
        DBG("ybT", ybT[:, 1, :], [128, S_], ["ybT1_%d" % i for i in range(4)], BF16)

        woa = WB[:, 0:4096].rearrange("p (c n) -> p c n", c=4)
        wob = WB[:, 4096:8192].rearrange("p (c n) -> p c n", c=4)
        wOut = bufB[:, 0:8192].rearrange("p (c n) -> p c n", c=8)
        w_out_v = w_out.rearrange("(c p) n -> p c n", p=128)
        out_pieces = []
        for kc in range(8):
            for hf in range(2):
                out_pieces.append((bufB[:, kc * 1024 + hf * 512:kc * 1024 + (hf + 1) * 512], w_out_v[:, kc, hf * 512:(hf + 1) * 512], 0.5))
        d_it = [0]
        for m in range(8):
            sa = stream_chunk(16 + m)
            sb_ = stream_chunk(24 + m)
            SA = T[0]
            SBt = T[1]
            for tc in range(NT):
                tsl = slice(tc * TC, (tc + 1) * TC)
                di = d_it[0]
                d_it[0] += 1
                if 1 <= di <= 16:
                    dst_, _, _ = out_pieces[di - 1]
                    o0 = (di - 1) * 512
                    piece_cast(di - 1, out_pieces, vpk(o0, o0 + 512))
                if di < 16:
                    piece_dma(di, out_pieces)
                ga_, gb_ = tc % 2, 6 + (tc % 2)
                stage1(sa, tc, ga_)
                stage1(sb_, tc, gb_)
                ACT(SA[:, tsl], ps[ga_][:, :], AF.Tanh, [PK[ga_], "bmh"], TK(0, (tc,)), bias=bmh[:, 16 + m:17 + m], scale=0.5)
                ACT(SBt[:, tsl], ps[gb_][:, :], AF.Tanh, [PK[gb_], "bmh"], TK(1, (tc,)), bias=bmh[:, 24 + m:25 + m], scale=0.5)
                e = tc % 2
                ba, bb = 2 + 2 * e, 3 + 2 * e
                MM([(ps[ba][:, :], woa[:, kc, m * 128:(m + 1) * 128], yaT[:, kc, tsl], kc == 0, kc == 3) for kc in range(4)],
                   ["WB"] + ["yaT%d_%d" % (kc, tc) for kc in range(4)], [PK[ba]])
                MM([(ps[bb][:, :], wob[:, kc, m * 128:(m + 1) * 128], ybT[:, kc, tsl], kc == 0, kc == 3) for kc in range(4)],
                   ["WB", "bufC0", "bufC1"] + ["ybT%d_%d" % (kc, tc) for kc in range(4)], [PK[bb]])
                ta = T[2][:, (2 * e) * 512:(2 * e + 1) * 512]
                tb = T[2][:, (2 * e + 1) * 512:(2 * e + 2) * 512]
                STT("dve", ta, SA[:, tsl], 1.0, ps[ba][:, :], ALU.add, ALU.mult, [PK[ba]] + TK(0, (tc,)), TK(2, (2 * e,)))
                STT("dve", tb, SBt[:, tsl], 1.0, ps[bb][:, :], ALU.add, ALU.mult, [PK[bb]] + TK(1, (tc,)), TK(2, (2 * e + 1,)))
                TT("dve", bufA[:, m, tsl], ta, tb, ALU.add, TK(2, (2 * e, 2 * e + 1)), ["kT%d_%d" % (m, tc)])
        DBG("mergedT", bufA[:, 2, :], [128, S_], ["kT2_%d" % i for i in range(4)], BF16)

        lnv = bufD[:, :, :].rearrange("p a b -> p (a b)").bitcast(F32)
        lnG = lnv[:, 0:1024]
        lnB = lnv[:, 1024:2048]
        dkeys = ["cqn%d" % i for i in range(4)] + ["ckvn%d" % i for i in range(4)]
        S.dma("lnp", [(lnG, ln_g.partition_broadcast(128)), (lnB, ln_b.partition_broadcast(128))], writes=dkeys)
        rekey("pool", ["xT%d" % i for i in range(4)], ["xs%d" % i for i in range(8)])
        xsl = xTb[:, :, :].rearrange("p a b -> p (a b)").bitcast(F32)
        OTs = [T[1][:, 0:1024], T[1][:, 1024:2048], T[2][:, 0:1024], T[2][:, 1024:2048]]
        OTk = [TK(1, (0, 1)), TK(1, (2, 3)), TK(2, (0, 1)), TK(2, (2, 3))]

        def emit_xload(t):
            sl_ = t % 8
            S.dma("xs%d" % sl_, [(xsl[:, sl_ * 1024:(sl_ + 1) * 1024], x[t * 128:(t + 1) * 128, :])], writes=["xs%d" % sl_])

        for t in range(8):
            emit_xload(t)
        def eA(t):
            e = t % 2
            sl_ = t % 8
            o4 = t % 4
            XT = xsl[:, sl_ * 1024:(sl_ + 1) * 1024]
            OT = OTs[o4]
            r = T[0][:, e * 1024:(e + 1) * 1024]
            rk = TK(0, (2 * e, 2 * e + 1))
            st12 = small[:, 32 + e * 12:32 + e * 12 + 12]
            mv = small[:, 56 + e * 2:58 + e * 2]
            rs = small[:, 60 + e:61 + e]
            nmr = small[:, 62 + e:63 + e]
            sk = ["smallE%d" % e]
            for half in range(2):
                bank = 2 * e + half
                hs = slice(half * 512, (half + 1) * 512)
                MM([(ps[bank][:, :], bufA[:, kc, t * 128:(t + 1) * 128], wOut[:, kc, hs], kc == 0, kc == 7)
                    for kc in range(8)], ["kT%d_%d" % (kc, t // 4) for kc in range(8)] + vpk(0, 8192), [PK[bank]])
                STT("dve", r[:, hs], XT[:, hs], ALPHA, ps[bank][:, :], ALU.mult, ALU.add, ["xs%d" % sl_, PK[bank]], TK(0, (2 * e + half,)))
                S.op("dve", lambda en, o=st12[:, half * 6:half * 6 + 6], i=r[:, hs]: en.bn_stats(out=o, in_=i),
                     TK(0, (2 * e + half,)), sk)

        def eB(t):
            e = t % 2
            sl_ = t % 8
            o4 = t % 4
            XT = xsl[:, sl_ * 1024:(sl_ + 1) * 1024]
            OT = OTs[o4]
            r = T[0][:, e * 1024:(e + 1) * 1024]
            rk = TK(0, (2 * e, 2 * e + 1))
            st12 = small[:, 32 + e * 12:32 + e * 12 + 12]
            mv = small[:, 56 + e * 2:58 + e * 2]
            rs = small[:, 60 + e:61 + e]
            nmr = small[:, 62 + e:63 + e]
            sk = ["smallE%d" % e]
            S.op("dve", lambda en, o=mv, i=st12: en.bn_aggr(out=o, in_=i), sk, sk)
            TS("pool", rs, mv[:, 1:2], LN_EPS, None, ALU.add, None, sk, sk)
            TT("pool", rs, rs, nhalf[:, 0:1], ALU.pow, sk + ["nhalf"], sk)
            STT("dve", nmr, mv[:, 0:1], -1.0, rs, ALU.mult, ALU.mult, sk, sk)
            ACT(OT, r, AF.Identity, rk + sk, OTk[o4], bias=nmr, scale=rs)

        def eC(t):
            e = t % 2
            sl_ = t % 8
            o4 = t % 4
            XT = xsl[:, sl_ * 1024:(sl_ + 1) * 1024]
            OT = OTs[o4]
            r = T[0][:, e * 1024:(e + 1) * 1024]
            rk = TK(0, (2 * e, 2 * e + 1))
            st12 = small[:, 32 + e * 12:32 + e * 12 + 12]
            mv = small[:, 56 + e * 2:58 + e * 2]
            rs = small[:, 60 + e:61 + e]
            nmr = small[:, 62 + e:63 + e]
            sk = ["smallE%d" % e]
            TT("dve", OT, OT, lnG, ALU.mult, OTk[o4] + ["cqn0"], OTk[o4])
            TT("dve", OT, OT, lnB, ALU.add, OTk[o4] + ["cqn0"], OTk[o4])
            S.dma("ot%d" % o4, [(out[t * 128:(t + 1) * 128, :], OT)], reads=OTk[o4])
            if t + 8 < 16:
                emit_xload(t + 8)

        skew(16, [eA, eB, eC])
        S.wait_all_dma()
        S.emit()
    return nc, dbg_out


def _prep(x, positions, w_in, b_in, g_q, w_uq, g_kv, w_ukv, w_oa, sgu_ln_g, sgu_ln_b, w_s, b_s,
          w_ob, w_out, ln_g, ln_b):
    f = lambda a: np.ascontiguousarray(np.asarray(a), dtype=np.float32)
    W = f(w_in)[0]
    b = f(b_in)[0]
    cols = np.r_[0:512, 544:1568, 2080:4640]
    perm = np.r_[528:544, 512:528]
    w_kpe = np.zeros((D_, 2, 96), np.float32)
    w_kpe[:, 0, 64:96] = W[:, 512:544]
    w_kpe[:, 1, 64:96] = W[:, perm]
    b_kpe = np.zeros((128, 2), np.float32)
    b_kpe[64:96, 0] = b[512:544]
    b_kpe[64:96, 1] = b[perm]
    wq = f(w_uq)[0]
    w_uqb = np.zeros((384, 8, 96), np.float32)
    w_uqb[:, :, 64:96] = wq[:, :, 64 + np.r_[16:32, 0:16]]
    wkv_ = f(w_ukv)[0]
    consts = np.zeros((128, 388), np.float32)
    consts[:, 0:128] = np.eye(128, dtype=np.float32)
    kk = np.arange(128)[:, None]
    qq = np.arange(128)[None, :]
    consts[:, 128:256] = np.where(kk <= qq, 0.0, -30000.0)
    consts[:, 256:384] = np.where(kk <= qq, 1.0, 0.0)
    inv_freq = (np.float32(10000.0) ** (-np.arange(0, 32, 2, dtype=np.float32) / np.float32(32))).astype(np.float32)
    consts[64:96, 384] = np.concatenate([inv_freq, inv_freq])
    consts[64:80, 385] = -1.0
    consts[80:96, 385] = 1.0
    shared = {
        "w_main": np.ascontiguousarray(W[:, cols]),
        "b_main": np.ascontiguousarray(b[cols].reshape(32, 128).T),
        "w_v": np.ascontiguousarray(W[:, 1568:2080]),
        "b_v": np.ascontiguousarray(b[1568:2080][None, :]),
        "w_kpe": np.ascontiguousarray(w_kpe.reshape(D_, 192)),
        "b_kpe": b_kpe,
        "w_uqa": np.ascontiguousarray(wq.reshape(384, 768)),
        "w_uqb": np.ascontiguousarray(w_uqb.reshape(384, 768)),
        "g_q": np.ascontiguousarray(f(g_q)[0].reshape(3, 128).T),
        "g_kv": np.ascontiguousarray(f(g_kv)[0].reshape(1, 128).T),
        "w_kn": np.ascontiguousarray(wkv_[:, :, 0:64].reshape(128, 512)),
        "w_vv": np.ascontiguousarray(wkv_[:, :, 64:128].reshape(128, 512)),
        "w_oa": f(w_oa)[0],
        "w_ob": f(w_ob)[0],
        "w_out": f(w_out)[0],
        "sgu_g": np.ascontiguousarray(f(sgu_ln_g)[0].reshape(4, 128).T),
        "sgu_b": np.ascontiguousarray(f(sgu_ln_b)[0].reshape(4, 128).T),
        "w_sT": np.ascontiguousarray(np.transpose(f(w_s)[0], (2, 0, 1)).reshape(128, 1024)),
        "b_s": f(b_s)[0],
        "ln_g": f(ln_g)[0][None, :],
        "ln_b": f(ln_b)[0][None, :],
        "consts": consts,
    }
    xs = f(x)
    ps_ = np.ascontiguousarray(np.asarray(positions), dtype=np.int32)
    in_maps = []
    for bi in range(xs.shape[0]):
        m = dict(shared)
        m["x"] = xs[bi]
        m["xT"] = np.ascontiguousarray(xs[bi].T)
        m["pos"] = ps_[bi][None, :]
        in_maps.append(m)
    return in_maps


def kernel(**inputs):
    in_maps = _prep(**inputs)
    nc, _ = build()
    res = run_bass_kernel_spmd(nc, in_maps, core_ids=list(range(8)))
    return np.stack([np.asarray(r["out"], dtype=np.float32) for r in res.results], axis=0)
```

```python
import math
import numpy as np
import concourse.bass as bass
import concourse.mybir as mybir
from concourse.bass_utils import run_bass_kernel_spmd
from contextlib import ExitStack

F32 = mybir.dt.float32
BF16 = mybir.dt.bfloat16
I32 = mybir.dt.int32
AF = mybir.ActivationFunctionType
ALU = mybir.AluOpType

S_ = 2048
D_ = 1024
NT = 4
TC = 512
ALPHA = 2.0 ** 0.25
QSCALE = 96.0 ** -0.5
RMS_EPS = 1e-6
LN_EPS = 1e-5
GK = 0.044715 ** 0.5
GS = 2.0 * (2.0 / math.pi) ** 0.5
PI = math.pi
C1 = 6.28125
C2 = 2.0 * math.pi - 6.28125


class Sched:
    CE = ("pe", "act", "dve", "pool")

    def __init__(self, nc, stack):
        self.nc = nc
        self.stack = stack
        self.prog = {e: [] for e in self.CE + ("sp",)}
        self.sem = {e: stack.enter_context(nc.semaphore("prog_" + e)) for e in self.CE}
        self.cnt = {e: 0 for e in self.CE}
        self.dsem = {}
        self.seen = {e: {} for e in self.CE + ("sp",)}
        self.lastw = {}
        self.readers = {}

    def _semh(self, k):
        return self.sem[k] if isinstance(k, str) else self.dsem[k[1]][0]

    def _deps(self, eng, reads, writes):
        need = {}

        def add(ev, kind):
            if ev is None:
                return
            k, v = ev
            if k == eng and eng == "pe":
                return
            if need.get(k, 0) < v:
                need[k] = v

        for r in reads:
            add(self.lastw.get(r), "raw")
        for w in writes:
            add(self.lastw.get(w), "waw")
            for k, v in self.readers.get(w, {}).items():
                add((k, v), "war")
        waits = []
        for k, v in need.items():
            if self.seen[eng].get(k, 0) >= v:
                continue
            self.seen[eng][k] = v
            waits.append((k, v))
        return waits

    def _commit(self, ev, reads, writes):
        k, v = ev
        for r in reads:
            d = self.readers.setdefault(r, {})
            if d.get(k, 0) < v:
                d[k] = v
        for w in writes:
            self.lastw[w] = ev
            self.readers[w] = {}

    def op(self, eng, fn, reads=(), writes=()):
        waits = self._deps(eng, reads, writes)
        self.cnt[eng] += 1
        self.prog[eng].append((waits, [fn], self.sem[eng], 1))
        self._commit((eng, self.cnt[eng]), reads, writes)

    def pe(self, fns, reads=(), writes=()):
        waits = self._deps("pe", reads, writes)
        self.cnt["pe"] += 1
        self.prog["pe"].append((waits, list(fns), self.sem["pe"], 1))
        self._commit(("pe", self.cnt["pe"]), reads, writes)

    def dma(self, chan, pairs, reads=(), writes=(), queue="sp"):
        waits = self._deps(queue, reads, writes)
        if chan not in self.dsem:
            self.dsem[chan] = [self.stack.enter_context(
                self.nc.semaphore("dma_%d" % len(self.dsem))), 0]
        ent = self.dsem[chan]
        first = True
        for (o, i) in pairs:
            fn = (lambda e, o=o, i=i: e.dma_start(out=o, in_=i))
            self.prog[queue].append((waits if first else [], [fn], ent[0], 16))
            first = False
        ent[1] += 16 * len(pairs)
        self._commit((("dma", chan), ent[1]), reads, writes)

    def wait_all_dma(self, eng="sp"):
        waits = []
        for chan, (s, c) in self.dsem.items():
            if c and self.seen[eng].get(("dma", chan), 0) < c:
                waits.append((("dma", chan), c))
                self.seen[eng][("dma", chan)] = c
        self.prog[eng].append((waits, [], None, 0))

    def _replay(self, name, eng):
        for waits, fns, sem, inc in self.prog[name]:
            for k, v in waits:
                eng.wait_ge(self._semh(k), v)
            for i, fn in enumerate(fns):
                ins = fn(eng)
                if i == len(fns) - 1 and sem is not None:
                    ins.then_inc(sem, inc)

    def emit(self):
        with self.nc.Block() as block:
            @block.sync
            def _(e):
                self._replay("sp", e)

            @block.tensor
            def _(e):
                self._replay("pe", e)

            @block.scalar
            def _(e):
                self._replay("act", e)

            @block.vector
            def _(e):
                self._replay("dve", e)

            @block.gpsimd
            def _(e):
                self._replay("pool", e)


def build(dbg=()):
    nc = bass.Bass("TRN2", target_bir_lowering=False)

    def din(name, shape, dt=F32):
        return nc.dram_tensor(name, shape, dt, kind="ExternalInput").ap()

    xT = din("xT", [D_, S_])
    x = din("x", [S_, D_])
    pos = din("pos", [1, S_], I32)
    w_main = din("w_main", [D_, 4096])
    b_main = din("b_main", [128, 32])
    w_v = din("w_v", [D_, 512])
    b_v = din("b_v", [1, 512])
    w_kpe = din("w_kpe", [D_, 192])
    b_kpe = din("b_kpe", [128, 2])
    w_uqa = din("w_uqa", [384, 768])
    w_uqb = din("w_uqb", [384, 768])
    g_q = din("g_q", [128, 3])
    g_kv = din("g_kv", [128, 1])
    w_kn = din("w_kn", [128, 512])
    w_vv = din("w_vv", [128, 512])
    w_oa = din("w_oa", [512, D_])
    w_ob = din("w_ob", [512, D_])
    w_out = din("w_out", [D_, D_])
    sgu_g = din("sgu_g", [128, 4])
    sgu_b = din("sgu_b", [128, 4])
    w_sT = din("w_sT", [128, 1024])
    b_s = din("b_s", [8, 128])
    ln_g = din("ln_g", [1, D_])
    ln_b = din("ln_b", [1, D_])
    consts = din("consts", [128, 388])
    out = nc.dram_tensor("out", [S_, D_], F32, kind="ExternalOutput").ap()
    dbg_out = {}

    with ExitStack() as st:
        S = Sched(nc, st)

        def sb(name, shape, dt):
            return st.enter_context(nc.sbuf_tensor(name, shape, dt))

        xTb = sb("xTb", [128, 8, S_], BF16)
        bufA = sb("bufA", [128, 8, S_], BF16)
        bufB = sb("bufB", [128, 16 * 768], BF16)
        yaT = sb("yaT", [128, 4, S_], BF16)
        bufC = sb("bufC", [128, 2, S_], F32)
        ybT = bufC[:, :, :].rearrange("p a b -> p (a b)").bitcast(BF16).rearrange("p (c n) -> p c n", c=4)
        bufD = sb("bufD", [128, 4, S_], BF16)
        T = [sb("T%d" % i, [128, S_], F32) for i in range(3)]
        wst = [sb("wst%d" % i, [128, 8, 128], F32) for i in range(2)]
        wbf = [sb("wbf%d" % i, [128, 8, 128], BF16) for i in range(2)]
        WB = sb("WB", [128, 8192], BF16)
        wkv = sb("wkv", [128, 1024], BF16)
        wsb = sb("wsb", [128, 1024], BF16)
        PT = [sb("PT%d" % i, [128, 512], BF16) for i in range(4)]
        dmy = sb("dmy", [128, 4], F32)
        rdn = sb("rdn", [128, 512], F32)
        epsq = sb("epsq", [128, 1], F32)
        nhalf = sb("nhalf", [128, 1], F32)
        bmh = sb("bmh", [128, 32], F32)
        cst = sb("cst", [128, 388], F32)
        identb = sb("identb", [128, 128], BF16)
        mnegb = sb("mnegb", [128, 128], BF16)
        onesb = sb("onesb", [128, 128], BF16)
        onesf = sb("onesf", [128, 128], F32)
        bm = sb("bm", [128, 32], F32)
        bmg = sb("bmg", [128, 32], F32)
        bk = sb("bk", [128, 2], F32)
        gq = sb("gq", [128, 3], F32)
        gkv = sb("gkv", [128, 1], F32)
        sg_g = sb("sg_g", [128, 4], F32)
        sg_b = sb("sg_b", [128, 4], F32)
        bvb = sb("bvb", [128, 512], F32)
        bsb = sb("bsb", [128, 4, 128], F32)
        bias2 = sb("bias2", [128, 4, 128], F32)
        small = sb("small", [128, 64], F32)
        ps = [st.enter_context(nc.psum_tensor("ps%d" % i, [128, 512], F32)) for i in range(8)]
        PK = ["ps%d" % i for i in range(8)]

        tri01 = cst[:, 256:384]
        invf2 = cst[:, 384:385]
        sgn = cst[:, 385:386]

        def ACT(out_, in_, func, reads, writes, bias=None, scale=1.0):
            kw = dict(out=out_, in_=in_, func=func, scale=scale)
            if bias is not None:
                kw["bias"] = bias
            S.op("act", lambda e: e.activation(**kw), reads, writes)

        def TS(eng, out_, in0, s1, s2, op0, op1, reads, writes):
            if s2 is None:
                S.op(eng, lambda e: e.tensor_scalar(out=out_, in0=in0, scalar1=s1, scalar2=None, op0=op0), reads, writes)
            else:
                S.op(eng, lambda e: e.tensor_scalar(out=out_, in0=in0, scalar1=s1, scalar2=s2, op0=op0, op1=op1), reads, writes)

        def STT(eng, out_, in0, scalar, in1, op0, op1, reads, writes):
            S.op(eng, lambda e: e.scalar_tensor_tensor(out=out_, in0=in0, scalar=scalar, in1=in1, op0=op0, op1=op1), reads, writes)

        def TT(eng, out_, in0, in1, op, reads, writes):
            S.op(eng, lambda e: e.tensor_tensor(out=out_, in0=in0, in1=in1, op=op), reads, writes)

        def CP(eng, out_, in_, reads, writes):
            S.op(eng, lambda e: e.tensor_copy(out=out_, in_=in_), reads, writes)

        def RCP(out_, in_, reads, writes):
            S.op("dve", lambda e: e.reciprocal(out=out_, in_=in_), reads, writes)

        def MSET(eng, ap, val, writes):
            S.op(eng, lambda e: e.memset(ap, val), (), writes)

        def MM(specs, reads, writes):
            fns = []
            for (o, l, r, s0, s1) in specs:
                fns.append(lambda e, o=o, l=l, r=r, s0=s0, s1=s1: e.matmul(o, lhsT=l, rhs=r, start=s0, stop=s1))
            S.pe(fns, reads, writes)

        cast_rr = [0]

        def cast_eng():
            cast_rr[0] += 1
            return "dve"

        def TK(i, qs=(0, 1, 2, 3)):
            return ["T%d_%d" % (i, q) for q in qs]

        def vpk(a, b):
            return ["Vp%d" % i for i in range(a // 768, (b - 1) // 768 + 1)]

        def DBG(name, ap, shape, keys, dt=F32):
            if name not in dbg:
                return
            d = nc.dram_tensor("dbg_" + name, list(shape), dt, kind="ExternalOutput").ap()
            dbg_out[name] = "dbg_" + name
            S.dma("dbg_" + name, [(d, ap)], reads=keys)

        def skew(n, stages):
            for step in range(n + len(stages) - 1):
                for s_i in reversed(range(len(stages))):
                    it = step - s_i
                    if 0 <= it < n:
                        stages[s_i](it)

        def rekey(eng, old, new):
            S.op(eng, lambda e: e.memset(dmy[0:1, 0:1], 0.0), (), list(old) + list(new) + ["dmy"])

        def CAST(eng, out_, in_, reads, writes, mul=None):
            if eng == "act":
                ACT(out_, in_, AF.Copy, reads, writes, scale=(1.0 if mul is None else mul))
            elif mul is None:
                CP(eng, out_, in_, reads, writes)
            else:
                TS(eng, out_, in_, mul, None, ALU.mult, None, reads, writes)

        S.dma("cst", [(cst[:], consts)], writes=["cst"])
        CP("pool", identb[:], cst[:, 0:128], ["cst"], ["identb"])
        CP("pool", mnegb[:], cst[:, 128:256], ["cst"], ["mnegb"])
        MSET("pool", onesb[:], 1.0, ["onesb"])
        MSET("pool", onesf[:], 1.0, ["onesf"])
        MSET("pool", nhalf[:], -0.5, ["nhalf"])
        MSET("pool", epsq[:], RMS_EPS, ["epsq"])

        lw_rr = [0]
        lw_engs = ["dve", "act"]

        def load_w(dst, src, n, keys_dst, extra_reads=(), dst3=None, mul=None):
            tix = lw_rr[0] % 3
            eng = lw_engs[lw_rr[0] % len(lw_engs)] if mul is None else "act"
            lw_rr[0] += 1
            stg = T[tix][:, 0:n]
            if len(src.shape) == 3:
                stg_v = stg.rearrange("p (a b) -> p a b", a=src.shape[1])
            else:
                stg_v = stg
            S.dma("T%d" % tix, [(stg_v, src)], writes=TK(tix))
            if dst3 is not None:
                if eng == "act":
                    CAST(eng, dst3, stg_v, TK(tix) + list(extra_reads), keys_dst, mul)
                else:
                    for a_ in range(src.shape[1]):
                        CAST(eng, dst3[:, a_, :], stg_v[:, a_, :], TK(tix) + list(extra_reads), keys_dst, mul)
            else:
                CAST(eng, dst, stg, TK(tix) + list(extra_reads), keys_dst, mul)

        w_main_v = w_main.rearrange("(c p) n -> p c n", p=128)

        stream_ix = [0]

        def stream_dma(j):
            s_ = stream_ix[0] % 2
            stream_ix[0] += 1
            S.dma("wst%d" % s_, [(wst[s_][:, :, :], w_main_v[:, :, j * 128:(j + 1) * 128])], writes=["wst%d" % s_])
            return s_

        def stream_cast(s_):
            CP(cast_eng(), wbf[s_][:, :, :], wst[s_][:, :, :], ["wst%d" % s_], ["wbf%d" % s_])

        def stream_chunk(j):
            s_ = stream_dma(j)
            stream_cast(s_)
            return s_

        def XK(tc):
            return ["xT%d" % tc]

        def stage1(s_, tc, bank):
            tsl_ = slice(tc * TC, (tc + 1) * TC)
            MM([(ps[bank][:, :], wbf[s_][:, dk, :], xTb[:, dk, tsl_], dk == 0, dk == 7) for dk in range(8)],
               XK(tc) + ["wbf%d" % s_], [PK[bank]])

        cos2 = bufC[:, 0, :]
        sinS = bufC[:, 1, :]
        R = slice(64, 96)
        scr = bufA[:, :, :].rearrange("p a b -> p (a b)").bitcast(F32)
        A0, A1, A2 = scr[:, 0:2048], scr[:, 2048:4096], scr[:, 4096:6144]

        def AK(i):
            return ["kT%d_%d" % (h_, t_) for h_ in (2 * i, 2 * i + 1) for t_ in range(4)]

        def emit_rope_chain():
            posi = A0.bitcast(I32)
            kint = A2.bitcast(I32)
            S.dma("pos", [(posi[R, :], pos.partition_broadcast(32))], writes=AK(0))
            CP("dve", A1[R, :], posi[R, :], AK(0), AK(1))
            TS("dve", A1[R, :], A1[R, :], invf2[R, :], None, ALU.mult, None, AK(1) + ["cst"], AK(1))
            for which, shift in ((1, 0.0),):
                r_ = bufC[R, which, :]
                rk_ = ["bufC%d" % which]
                TS("dve", kint[R, :], A1[R, :], shift, 1.0 / (2 * PI), ALU.add, ALU.mult, AK(1), AK(2))
                CP("dve", A0[R, :], kint[R, :], AK(2), AK(0))
                TS("dve", r_, A1[R, :], shift, None, ALU.add, None, AK(1), rk_)
                STT("dve", r_, A0[R, :], -C1, r_, ALU.mult, ALU.add, AK(0) + rk_, rk_)
                STT("dve", r_, A0[R, :], -C2, r_, ALU.mult, ALU.add, AK(0) + rk_, rk_)
                TS("dve", A0[R, :], r_, PI, 2 * PI, ALU.is_gt, ALU.mult, rk_, AK(0))
                TT("dve", r_, r_, A0[R, :], ALU.subtract, AK(0) + rk_, rk_)
                TS("dve", A0[R, :], r_, -PI, 2 * PI, ALU.is_lt, ALU.mult, rk_, AK(0))
                TT("dve", r_, r_, A0[R, :], ALU.add, AK(0) + rk_, rk_)
                TS("dve", r_, r_, -3.1415925, 3.1415925, ALU.max, ALU.min, rk_, rk_)
            rs_, rc_ = bufC[R, 1, :], bufC[R, 0, :]
            TS("dve", rc_, rs_, PI / 2, None, ALU.add, None, ["bufC1"], ["bufC0"])
            TS("dve", A0[R, :], rc_, PI, 2 * PI, ALU.is_gt, ALU.mult, ["bufC0"], AK(0))
            TT("dve", rc_, rc_, A0[R, :], ALU.subtract, AK(0) + ["bufC0"], ["bufC0"])
            TS("dve", rc_, rc_, -3.1415925, 3.1415925, ALU.max, ALU.min, ["bufC0"], ["bufC0"])

        def emit_rope_finish():
            for which in (1, 0):
                ACT(bufC[R, which, :], bufC[R, which, :], AF.Sin, ["bufC%d" % which], ["bufC%d" % which])
            TS("dve", sinS[R, :], sinS[R, :], sgn[R, :], None, ALU.mult, None, ["bufC1", "cst"], ["bufC1"])
            DBG("cos2", bufC[R, 0, :], [32, S_], ["bufC0"])
            DBG("sinS", bufC[R, 1, :], [32, S_], ["bufC1"])

        emit_rope_chain()
        lw_engs[:] = ["act"]
        wA = WB[:, 0:4096].rearrange("p (c n) -> p c n", c=8)
        for half in range(2):
            load_w(WB[:, half * 2048:(half + 1) * 2048], w_main_v[:, half * 4:(half + 1) * 4, 0:512], 2048, ["WB"])
        wK = WB[:, 4096:5632].rearrange("p (c n) -> p c n", c=8)
        xT_v = xT.rearrange("(c p) s -> p c s", p=128)

        def load_xT(tc):
            tsl_ = slice(tc * TC, (tc + 1) * TC)
            for q4 in range(4):
                s_ = stream_ix[0] % 2
                stream_ix[0] += 1
                stg = wst[s_][:, :, :].rearrange("p a b -> p (a b)").rearrange("p (a b) -> p a b", a=2)
                S.dma("wst%d" % s_, [(stg, xT_v[:, 2 * q4:2 * q4 + 2, tsl_])], writes=["wst%d" % s_])
                if tc == 0:
                    CAST("act", xTb[:, 2 * q4:2 * q4 + 2, tsl_], stg, ["wst%d" % s_], XK(tc))
                else:
                    for a_ in range(2):
                        CP("dve", xTb[:, 2 * q4 + a_, tsl_], stg[:, a_, :], ["wst%d" % s_], XK(tc))

        load_xT(0)
        S.dma("bias", [(bm[:], b_main), (bk[:], b_kpe), (gq[:], g_q), (gkv[:], g_kv),
                       (sg_g[:], sgu_g), (sg_b[:], sgu_b),
                       (bvb[:], b_v.partition_broadcast(128))], writes=["bias"])
        S.dma("bsb", [(bsb[(g % 2) * 64:(g % 2) * 64 + 64, g // 2, :],
                       b_s[g:g + 1, :].partition_broadcast(64)) for g in range(8)], writes=["bsb"])
        TS("pool", bmg[:], bm[:], GK, None, ALU.mult, None, ["bias"], ["bmg"])
        TS("pool", bmh[:], bm[:], 0.5, None, ALU.mult, None, ["bias"], ["bmh"])
        load_w(WB[:, 4096:5632], w_kpe.rearrange("(c p) n -> p c n", p=128), 1536, ["WBk"])
        load_w(wkv[:, 0:512], w_kn, 512, ["wkv"])
        load_w(wkv[:, 512:1024], w_vv, 512, ["wkv"])
        lw_engs[:] = ["dve", "act"]
        S.dma("T2", [(T[2][:, 0:1024], w_sT)], writes=TK(2, (0, 1)))
        for g in range(8):
            TT("pool", wsb[:, g * 128:(g + 1) * 128], T[2][:, g * 128:(g + 1) * 128], tri01, ALU.mult,
               TK(2, (0, 1)) + ["cst"], ["wsb"])


        lw_engs[:] = ["dve", "act"]

        cqn = bufD[:, 0:3, :]
        ckvn = bufD[:, 3, :]
        Vp = bufB[:, :].rearrange("k (t c) -> k t c", t=16)
        MSET("pool", bufB[:, :].rearrange("k (g c) -> k g c", c=192)[:, :, 64:128], 1.0, ["VpOnes"])

        def phaseA_lat(tc):
            tsl = slice(tc * TC, (tc + 1) * TC)
            Rw = T[0][:, :].rearrange("p (j n) -> p j n", j=4)
            SQ = T[1][:, :].rearrange("p (j n) -> p j n", j=4)
            def stats_mm(j):
                if j < 3:
                    MM([(ps[2][:, :], onesf[:, :], SQ[:, j, :], j == 0, j == 2)], TK(1, (j,)) + ["onesf"], [PK[2]])
                else:
                    MM([(ps[3][:, :], onesf[:, :], SQ[:, j, :], True, True)], TK(1, (j,)) + ["onesf"], [PK[3]])

            for j in range(4):
                bank = j % 2
                MM([(ps[bank][:, :], wA[:, dk, j * 128:(j + 1) * 128], xTb[:, dk, tsl], dk == 0, dk == 7)
                    for dk in range(8)], XK(tc) + ["WB"], [PK[bank]])
                if j >= 1:
                    stats_mm(j - 1)
                ACT(Rw[:, j, :], ps[bank][:, :], AF.Identity, [PK[bank], "bias"], TK(0, (j,)), bias=bm[:, j:j + 1])
                ACT(SQ[:, j, :], ps[bank][:, :], AF.Square, [PK[bank], "bias"], TK(1, (j,)), bias=bm[:, j:j + 1])
            stats_mm(3)
            rq = T[2][:, 0:512]
            rkv = T[2][:, 512:1024]
            ACT(rq, ps[2][:, :], AF.Ln, [PK[2], "epsq"], TK(2, (0,)), bias=epsq[:, 0:1], scale=1.0 / 384)
            ACT(rkv, ps[3][:, :], AF.Ln, [PK[3], "epsq"], TK(2, (1,)), bias=epsq[:, 0:1], scale=1.0 / 128)
            ACT(rq, rq, AF.Exp, TK(2, (0,)), TK(2, (0,)), scale=-0.5)
            ACT(rkv, rkv, AF.Exp, TK(2, (1,)), TK(2, (1,)), scale=-0.5)
            STT("dve", ckvn[:, tsl], Rw[:, 3, :], gkv[:, 0:1], rkv, ALU.mult, ALU.mult,
                TK(0, (3,)) + TK(2, (1,)) + ["bias"], ["ckvn%d" % tc])
            for j in range(3):
                STT("dve", cqn[:, j, tsl], Rw[:, j, :], gq[:, j:j + 1], rq, ALU.mult, ALU.mult,
                    TK(0, (j,)) + TK(2, (0,)) + ["bias"], ["cqn%d" % tc])

        def phaseA_kpe(tc):
            tsl = slice(tc * TC, (tc + 1) * TC)
            MM([(ps[4][0:96, :], wK[:, dk, 0:96], xTb[:, dk, tsl], dk == 0, dk == 7) for dk in range(8)],
               XK(tc) + ["WBk"], [PK[4]])
            MM([(ps[5][0:96, :], wK[:, dk, 96:192], xTb[:, dk, tsl], dk == 0, dk == 7) for dk in range(8)],
               XK(tc) + ["WBk"], [PK[5]])
            t1 = T[2][:, 1024:1536]
            t2 = T[2][:, 1536:2048]
            STT("dve", t1[R, :], ps[4][R, :], bk[R, 0:1], cos2[R, tsl], ALU.add, ALU.mult,
                [PK[4], "bias", "bufC0"], TK(2, (2,)))
            STT("dve", t2[R, :], ps[5][R, :], bk[R, 1:2], sinS[R, tsl], ALU.add, ALU.mult,
                [PK[5], "bias", "bufC1"], TK(2, (3,)))
            TT("dve", t1[R, :], t1[R, :], t2[R, :], ALU.add, TK(2, (2, 3)), TK(2, (2,)))
            for h in range(8):
                CAST(("act", "dve", "act", "act", "dve", "act", "act", "dve")[h], bufA[R, h, tsl], t1[R, :],
                     TK(2, (2,)), ["kT%d_%d" % (h, tc)])

        def phaseA_up(tc):
            tsl = slice(tc * TC, (tc + 1) * TC)
            for p in range(4):
                bank = 6 + (p % 2)
                MM([(ps[bank][:, :], wkv[:, p * 128:(p + 1) * 128], ckvn[:, tsl], True, True)],
                   ["wkv", "ckvn%d" % tc], [PK[bank]])
                ACT(bufA[0:64, 2 * p, tsl], ps[bank][0:64, :], AF.Copy, [PK[bank]], ["kT%d_%d" % (2 * p, tc)])
                CP("dve", bufA[0:64, 2 * p + 1, tsl], ps[bank][64:128, :], [PK[bank]], ["kT%d_%d" % (2 * p + 1, tc)])
            for tt in range(4):
                t_ = tc * 4 + tt
                bank = 6 + (tt % 2)
                MM([(ps[bank][:, :], ckvn[:, t_ * 128:(t_ + 1) * 128], wkv[:, 512:1024], True, True)],
                   ["wkv", "ckvn%d" % tc], [PK[bank]])
                src = ps[bank][:, :].rearrange("k (p e d) -> k p e d", p=4, e=2)
                dstv = Vp[:, t_, :].rearrange("k (p c) -> k p c", c=192)
                ACT(dstv[:, :, 0:64], src[:, :, 0, :], AF.Copy, [PK[bank]], ["Vp%d" % t_])
                CP("dve", dstv[:, :, 128:192], src[:, :, 1, :], [PK[bank]], ["Vp%d" % t_])

        for tc in range(NT):
            if tc + 1 < NT:
                load_xT(tc + 1)
            phaseA_lat(tc)
            if tc >= 1:
                phaseA_up(tc - 1)
        phaseA_up(NT - 1)
        emit_rope_finish()
        for tc in range(NT):
            phaseA_kpe(tc)
        for bnk in range(2):
            MM([(ps[bnk][:, :], onesb[:, :], wsb[:, bnk * 512:(bnk + 1) * 512], True, True)], ["onesb", "wsb"], [PK[bnk]])
        for g in range(8):
            p = g // 2
            rows = slice((g % 2) * 64, (g % 2) * 64 + 64)
            STT("dve", bias2[rows, p, :], ps[g // 4][rows, (g % 4) * 128:(g % 4 + 1) * 128], sg_b[rows, p:p + 1],
                bsb[rows, p, :], ALU.mult, ALU.add, [PK[g // 4], "bias", "bsb"], ["bias2"])
        DBG("cqn", bufD[:, 0, :], [128, S_], ["cqn%d" % i for i in range(4)], BF16)
        DBG("ckvn", bufD[:, 3, :], [128, S_], ["ckvn%d" % i for i in range(4)], BF16)
        DBG("kT0", bufA[:, 0, :], [128, S_], ["kT0_%d" % i for i in range(4)], BF16)
        DBG("kT3", bufA[:, 3, :], [128, S_], ["kT3_%d" % i for i in range(4)], BF16)
        DBG("Vp", bufB[:, :], [128, 16 * 768], ["Vp%d" % i for i in range(16)] + ["VpOnes"], BF16)

        wqa = WB[:, 0:2304].rearrange("p (c n) -> p c n", c=3)
        wqb = WB[:, 2304:4608].rearrange("p (c n) -> p c n", c=3)
        for kc in range(3):
            load_w(WB[:, kc * 768:(kc + 1) * 768], w_uqa[kc * 128:(kc + 1) * 128, :], 768, ["WB", "WBk"])
            load_w(WB[:, 2304 + kc * 768:2304 + (kc + 1) * 768], w_uqb[kc * 128:(kc + 1) * 128, :], 768, ["WB", "WBk"])
        qT2 = T[1][:, :].bitcast(BF16)
        ZA = T[0]
        qall = ["qT%d_%d" % (s_, t_) for s_ in range(2) for t_ in range(4)]
        rekey("pool", TK(1), qall)
        bsb_bf = bsb[:, :, :].rearrange("p a b -> p (a b)").bitcast(BF16)
        PT.append(bsb_bf[:, 0:512])
        PT.append(bsb_bf[:, 512:1024])
        rekey("pool", ["bsb"], ["PT4", "PT5"])
        ptix = [0]
        za_slot = {}
        SB_ = (1, 6, 7)

        def emit_za(p, tc):
            tsl = slice(tc * TC, (tc + 1) * TC)
            s_ = za_slot[p]
            zb_ = 2 + (tc % 2)
            stage1(s_, tc, zb_)
            th = wst[s_][:, 0:4, :]
            zv = ZA[:, tsl].rearrange("p (a b) -> p a b", a=4)
            ACT(ZA[:, tsl], ps[zb_][:, :], AF.Identity, [PK[zb_], "bias"], TK(0, (tc,)), bias=bm[:, 4 + p:5 + p])
            ACT(th, zv, AF.Tanh, TK(0, (tc,)), ["wst%d" % s_], scale=0.5)
            STT("dve", zv, th, 1.0, zv, ALU.add, ALU.mult, ["wst%d" % s_] + TK(0, (tc,)), TK(0, (tc,)))

        def emit_qproj(h, tc):
            tsl = slice(tc * TC, (tc + 1) * TC)
            sl_ = h % 2
            qT = qT2[:, sl_ * 2048:(sl_ + 1) * 2048]
            qk = ["qT%d_%d" % (sl_, tc)]
            MM([(ps[2][0:96, :], wqa[:, kc, h * 96:(h + 1) * 96], cqn[:, kc, tsl], kc == 0, kc == 2)
                for kc in range(3)], ["WB", "cqn%d" % tc], [PK[2]])
            MM([(ps[3][0:96, :], wqb[:, kc, h * 96:(h + 1) * 96], cqn[:, kc, tsl], kc == 0, kc == 2)
                for kc in range(3)], ["WB", "cqn%d" % tc], [PK[3]])
            t1 = T[2][:, 1024:1536]
            t2 = T[2][:, 1536:2048]
            TT("dve", t1[R, :], ps[2][R, :], cos2[R, tsl], ALU.mult, [PK[2], "bufC0"], TK(2, (2,)))
            TT("dve", t2[R, :], ps[3][R, :], sinS[R, tsl], ALU.mult, [PK[3], "bufC1"], TK(2, (3,)))
            ACT(qT[0:64, tsl], ps[2][0:64, :], AF.Copy, [PK[2]], qk)
            TT("dve", qT[R, tsl], t1[R, :], t2[R, :], ALU.add, TK(2, (2, 3)), qk)

        SB4 = (1, 6, 7, 3)

        def attn_S(h, c, j, slot):
            sl_ = h % 2
            qT = qT2[:, sl_ * 2048:(sl_ + 1) * 2048]
            c0 = max(0, j - 4 * c) * 128
            sbk = SBK[slot % len(SBK)]
            k3 = slot % 6
            specs = [(ps[sbk][:, c0:512], bufA[0:96, h, j * 128:(j + 1) * 128],
                      qT[0:96, c * 512 + c0:(c + 1) * 512], True, j < 4 * c)]
            rd = ["kT%d_%d" % (h, j // 4), "qT%d_%d" % (sl_, c)]
            if j >= 4 * c:
                specs.append((ps[sbk][:, c0:c0 + 128], identb[:, :], mnegb[:, :], False, True))
                rd = rd + ["identb", "mnegb"]
            MM(specs, rd, [PK[sbk]])
            ACT(PT[k3][:, c0:512], ps[sbk][:, c0:512], AF.Exp, [PK[sbk]], ["PT%d" % k3], scale=QSCALE)

        def attn_PV(h, c, j, slot, ob):
            p, hh = divmod(h, 2)
            c0 = max(0, j - 4 * c) * 128
            k3 = slot % 6
            nj = 4 * c + 4
            vsl = slice(p * 192 + hh * 64, p * 192 + hh * 64 + 128)
            MM([(ps[ob][:, c0:512], Vp[:, j, vsl], PT[k3][:, c0:512], j == 0, j == nj - 1)],
               ["Vp%d" % j, "VpOnes", "PT%d" % k3], [PK[ob]])
            if j == nj - 1:
                orow = slice(0, 64) if hh == 0 else slice(64, 128)
                drow = slice(64, 128) if hh == 0 else slice(0, 64)
                csl = slice(c * 512, (c + 1) * 512)
                e2 = ob - 4
                accS = T[2][:, 0:512] if e2 == 0 else T[2][:, 512:1024]
                ak_ = TK(2, (e2,))
                run_deferred(tag=("norm", e2))
                ACT(accS, ps[ob][:, :], AF.Copy, [PK[ob]], ak_)

                def norm_rest(p=p, hh=hh, c=c, orow=orow, drow=drow, csl=csl, accS=accS, ak_=ak_):
                    RCP(rdn[orow, :], accS[drow, :], ak_, ["rdn"])
                    TT("dve", accS[orow, :], accS[orow, :], rdn[orow, :], ALU.mult, ak_ + ["rdn"], ak_)
                    TT("dve", yaT[orow, p, csl], accS[orow, :], ZA[orow, csl], ALU.mult,
                       ak_ + TK(0, (c,)), ["yaT%d_%d" % (p, c)])
                    if hh == 1 and p + 1 < 4:
                        deferred.append((slot_ctr[0] + 10, ("za", c), lambda p=p, c=c: emit_za(p + 1, c)))

                deferred.append((slot_ctr[0] + 3, ("norm", e2), norm_rest))

        SBK = (0, 1, 6, 7)
        slot_ctr = [0]
        LAG = 4

        za_slot[0] = stream_chunk(4)
        for tc in range(NT):
            emit_za(0, tc)
        for tc in range(NT):
            emit_qproj(0, tc)
        chunks = [(h, c) for h in range(8) for c in (3, 0, 2, 1)]
        deferred = []
        nxt = [0]
        pending = []

        def run_deferred(force=False, tag=None):
            keep = []
            items = list(deferred)
            del deferred[:]
            for it_ in items:
                due, tg, fn = it_
                if force or (tag is not None and tg == tag) or (tag is None and due <= slot_ctr[0]):
                    fn()
                else:
                    keep.append(it_)
            deferred[:0] = keep

        w_v_v = w_v.rearrange("(c p) n -> p c n", p=128)

        def prefetch_wv():
            def issue(i):
                s_ = stream_ix[0] % 2
                stream_ix[0] += 1
                stg = wst[s_][:, :, :].rearrange("p a b -> p (a b)")
                S.dma("wst%d" % s_, [(stg.rearrange("p (a b) -> p a b", a=2), w_v_v[:, 2 * i:2 * i + 2, :])],
                      writes=["wst%d" % s_])

                def cast(i=i, s_=s_, stg=stg):
                    CP("dve", WB[:, i * 1024:(i + 1) * 1024], stg, ["wst%d" % s_], ["WB", "WBk"])
                    if i + 2 < 4:
                        issue(i + 2)
                deferred.append((slot_ctr[0] + 8, ("wv", i), cast))
            issue(0)
            issue(1)

        def start_stream(ob):
            if nxt[0] >= len(chunks):
                return None
            h, c = chunks[nxt[0]]
            k = nxt[0] % 4
            nxt[0] += 1
            p, hh = divmod(h, 2)
            if k == 0 and hh == 1 and p + 1 < 4:
                za_slot[p + 1] = stream_chunk(4 + p + 1)
            if h + 1 < 8:
                emit_qproj(h + 1, (3, 0, 2, 1)[k])
            if h == 7 and k == 0:
                prefetch_wv()
            return dict(h=h, c=c, j=0, nj=4 * c + 4, ob=ob)

        streams = [start_stream(4), start_stream(5)]
        while any(s is not None for s in streams):
            for si in range(2):
                s = streams[si]
                if s is None:
                    continue
                slot = slot_ctr[0]
                slot_ctr[0] += 1
                run_deferred()
                attn_S(s["h"], s["c"], s["j"], slot)
                pending.append((s["h"], s["c"], s["j"], slot, s["ob"]))
                if len(pending) > LAG:
                    attn_PV(*pending.pop(0))
                s["j"] += 1
                if s["j"] == s["nj"]:
                    while any(pp[4] == s["ob"] for pp in pending):
                        attn_PV(*pending.pop(0))
                    streams[si] = start_stream(s["ob"])
        while pending:
            attn_PV(*pending.pop(0))
        while deferred:
            run_deferred(force=True)
        DBG("qT3", qT2[:, 2048:4096], [128, S_], ["qT1_%d" % t_ for t_ in range(4)], BF16)
        rekey("pool", qall, TK(1))
        DBG("yaT", yaT[:, 1, :], [128, S_], ["yaT1_%d" % i for i in range(4)], BF16)

        wV = WB[:, 0:4096].rearrange("p (c n) -> p c n", c=8)
        vn = bufB[:, 0:8192].rearrange("k (t f) -> k t f", t=16)
        VSETS = [((T[i][:, (2 * k) * 512:(2 * k + 1) * 512], TK(i, (2 * k,))),
                  (T[i][:, (2 * k + 1) * 512:(2 * k + 2) * 512], TK(i, (2 * k + 1,))))
                 for i in range(3) for k in range(2)]

        def vset(t):
            (vb, kvb), (sq, ksq) = VSETS[t % 6]
            e4 = t % 4
            st6 = small[:, e4 * 6:e4 * 6 + 6]
            mv = small[:, 24 + e4 * 2:26 + e4 * 2]
            rs = small[:, 32 + e4:33 + e4]
            return vb, kvb, sq, ksq, 2 + e4, st6, mv, rs, ["smallV%d" % e4]

        def v1(t):
            vb, kvb, sq, ksq, bank, st6, mv, rs, sk = vset(t)
            MM([(ps[bank][:, :], xTb[:, dk, t * 128:(t + 1) * 128], wV[:, dk, :], dk == 0, dk == 7) for dk in range(8)],
               XK(t // 4) + ["WB"], [PK[bank]])
            TT("dve", vb, ps[bank][:, :], bvb[:, :], ALU.add, [PK[bank], "bias"], kvb)
            ACT(sq, vb, AF.Square, kvb, ksq, scale=GK)

        def v2(t):
            vb, kvb, sq, ksq, bank, st6, mv, rs, sk = vset(t)
            STT("dve", sq, sq, 1.0, vb, ALU.add, ALU.mult, kvb + ksq, ksq)
            ACT(sq, sq, AF.Tanh, ksq, ksq, scale=GS / 2)

        def v3(t):
            vb, kvb, sq, ksq, bank, st6, mv, rs, sk = vset(t)
            STT("dve", vb, sq, 1.0, vb, ALU.add, ALU.mult, kvb + ksq, kvb)
            S.op("dve", lambda en, o=st6, i=vb: en.bn_stats(out=o, in_=i), kvb, sk)
            S.op("dve", lambda en, o=mv, i=st6: en.bn_aggr(out=o, in_=i), sk, sk)
            TS("pool", rs, mv[:, 1:2], 4.0 * LN_EPS, None, ALU.add, None, sk, sk)
            TT("pool", rs, rs, nhalf[:, 0:1], ALU.pow, sk + ["nhalf"], sk)

        def v4(t):
            vb, kvb, sq, ksq, bank, st6, mv, rs, sk = vset(t)
            nm_ = small[:, 40 + (t % 4):41 + (t % 4)]
            STT("dve", nm_, mv[:, 0:1], -1.0, rs, ALU.mult, ALU.mult, sk, sk)
            ACT(vn[:, t, :], vb, AF.Identity, kvb + sk, vpk(512 * t, 512 * t + 512), bias=nm_, scale=rs)

        skew(16, [v1, v2, v3, v4])
        DBG("vn", bufB[:, 0:8192], [128, 8192], vpk(0, 8192), BF16)
        def Q(i, q):
            return T[i][:, q * 512:(q + 1) * 512], TK(i, (q,))
        CSETS = [dict(ub=Q(0, 0), sq=Q(0, 1), sg=Q(0, 2), zb=Q(0, 3), mt=Q(1, 0), thz=Q(1, 1)),
                 dict(ub=Q(1, 2), sq=Q(1, 3), sg=Q(2, 0), zb=Q(2, 1), mt=Q(2, 2), thz=Q(2, 3))]
        pair_slots = {}
        pair_dma = {}
        w_oa_v = w_oa.rearrange("(c p) n -> p c n", p=128)
        w_ob_v = w_ob.rearrange("(c p) n -> p c n", p=128)
        pre_pieces = []
        for kc in range(4):
            for hf in range(2):
                pre_pieces.append((WB[:, kc * 1024 + hf * 512:kc * 1024 + (hf + 1) * 512], w_oa_v[:, kc, hf * 512:(hf + 1) * 512], 0.5))
        for kc in range(4):
            for hf in range(2):
                pre_pieces.append((WB[:, 4096 + kc * 1024 + hf * 512:4096 + kc * 1024 + (hf + 1) * 512],
                                   w_ob_v[:, kc, hf * 512:(hf + 1) * 512], 0.25))

        def piece_dma(i, pieces):
            dst, src, mul = pieces[i]
            S.dma("rdn", [(rdn[:, :], src)], writes=["rdn"])

        def piece_cast(i, pieces, keys):
            dst, src, mul = pieces[i]
            CAST("act", dst, rdn[:, :], ["rdn"], keys, mul)
        CS3 = CSETS + [None]

        def pset(it):
            cs = CSETS[it % 2]
            return (cs["ub"], cs["sq"], cs["sg"], cs["zb"], cs["mt"], cs["thz"])

        def pA(it):
            p, tc = divmod(it, NT)
            if tc == 0:
                if p == 0:
                    pair_dma[0] = (stream_dma(8), stream_dma(12))
                su_, sz_ = pair_dma[p]
                stream_cast(su_)
                stream_cast(sz_)
                pair_slots[p] = (su_, sz_)
                if p + 1 < 4:
                    pair_dma[p + 1] = (stream_dma(8 + p + 1), stream_dma(12 + p + 1))
            su, sz = pair_slots[p]
            ju = 8 + p
            (ub, kub), (sq, ksq), (sg, ksg), (zb, kzb), (mt, kmt), (thz, kthz) = pset(it)
            e = it % 2
            bu, bz = e, 2 + e
            piece_dma(it, pre_pieces)
            stage1(su, tc, bu)
            stage1(sz, tc, bz)
            ACT(ub, ps[bu][:, :], AF.Identity, [PK[bu], "bias"], kub, bias=bm[:, ju:ju + 1])
            ACT(sq, ps[bu][:, :], AF.Square, [PK[bu], "bmg"], ksq, bias=bmg[:, ju:ju + 1], scale=GK)
            ACT(zb, ps[bz][:, :], AF.Identity, [PK[bz], "bias"], kzb, bias=bm[:, 12 + p:13 + p])
            ACT(thz, ps[bz][:, :], AF.Tanh, [PK[bz], "bmh"], kthz, bias=bmh[:, 12 + p:13 + p], scale=0.5)
            STT("dve", sq, sq, 1.0, ub, ALU.add, ALU.mult, ksq + kub, ksq)
            ACT(sg, sq, AF.Tanh, ksq, ksg, scale=GS / 2)
            STT("dve", zb, thz, 1.0, zb, ALU.add, ALU.mult, kzb + kthz, kzb)

        def pB(it):
            p, tc = divmod(it, NT)
            (ub, kub), (sq, ksq), (sg, ksg), (zb, kzb), (mt, kmt), (thz, kthz) = pset(it)
            piece_cast(it, pre_pieces, ["WB"])
            for half in range(2):
                cc0 = tc * 4 + 2 * half
                bank = 4 + 2 * (it % 2) + half
                MM([(ps[bank][:, k2 * 256:(k2 + 1) * 256], vn[:, cc0 + k2, p * 128:(p + 1) * 128],
                     wsb[:, 2 * p * 128:(2 * p + 2) * 128], True, True) for k2 in range(2)],
                   vpk(512 * cc0, 512 * cc0 + 1024) + ["wsb"], [PK[bank]])
                for rows, cs_ in ((slice(0, 64), slice(0, 128)), (slice(64, 128), slice(128, 256))):
                    in0 = ps[bank][rows, :].rearrange("f (c x) -> f c x", c=2)[:, :, cs_]
                    out_ = mt[rows, half * 256:(half + 1) * 256].rearrange("f (c x) -> f c x", c=2)
                    STT("dve", out_, in0, sg_g[rows, p:p + 1], bias2[rows, p:p + 1, :].broadcast_to([64, 2, 128]),
                        ALU.mult, ALU.add, [PK[bank], "bias", "bias2"], kmt)
            STT("dve", ub, sg, 1.0, ub, ALU.add, ALU.mult, kub + ksg, kub)

        def pC(it):
            p, tc = divmod(it, NT)
            tsl = slice(tc * TC, (tc + 1) * TC)
            (ub, kub), (sq, ksq), (sg, ksg), (zb, kzb), (mt, kmt), (thz, kthz) = pset(it)
            TT("dve", mt, mt, ub, ALU.mult, kmt + kub, kmt)
            TT("dve", ybT[:, p, tsl], mt, zb, ALU.mult, kmt + kzb, ["ybT%d_%d" % (p, tc), "bufC%d" % (p // 2)])

        skew(16, [pA, pB, pC])
        DBG("ybT", ybT[:, 1, :], [128, S_], ["ybT1_%d" % i for i in range(4)], BF16)

        woa = WB[:, 0:4096].rearrange("p (c n) -> p c n", c=4)
        wob = WB[:, 4096:8192].rearrange("p (c n) -> p c n", c=4)
        wOut = bufB[:, 0:8192].rearrange("p (c n) -> p c n", c=8)
        w_out_v = w_out.rearrange("(c p) n -> p c n", p=128)
        out_pieces = []
        for kc in range(8):
            for hf in range(2):
                out_pieces.append((bufB[:, kc * 1024 + hf * 512:kc * 1024 + (hf + 1) * 512], w_out_v[:, kc, hf * 512:(hf + 1) * 512], 0.5))
        d_it = [0]
        nxt_d = (stream_dma(16), stream_dma(24))
        for m in range(8):
            sa, sb_ = nxt_d
            stream_cast(sa)
            stream_cast(sb_)
            if m + 1 < 8:
                nxt_d = (stream_dma(16 + m + 1), stream_dma(24 + m + 1))
            SA = T[0]
            SBt = T[1]
            for tc in range(NT):
                tsl = slice(tc * TC, (tc + 1) * TC)
                di = d_it[0]
                d_it[0] += 1
                if 1 <= di <= 16:
                    dst_, _, _ = out_pieces[di - 1]
                    o0 = (di - 1) * 512
                    piece_cast(di - 1, out_pieces, vpk(o0, o0 + 512))
                if di < 16:
                    piece_dma(di, out_pieces)
                stage1(sa, tc, 0)
                ACT(SA[:, tsl], ps[0][:, :], AF.Tanh, [PK[0], "bmh"], TK(0, (tc,)), bias=bmh[:, 16 + m:17 + m], scale=0.5)
                stage1(sb_, tc, 1)
                ACT(SBt[:, tsl], ps[1][:, :], AF.Tanh, [PK[1], "bmh"], TK(1, (tc,)), bias=bmh[:, 24 + m:25 + m], scale=0.5)
                e = tc % 2
                ba, bb = 2 + 2 * e, 3 + 2 * e
                MM([(ps[ba][:, :], woa[:, kc, m * 128:(m + 1) * 128], yaT[:, kc, tsl], kc == 0, kc == 3) for kc in range(4)],
                   ["WB"] + ["yaT%d_%d" % (kc, tc) for kc in range(4)], [PK[ba]])
                MM([(ps[bb][:, :], wob[:, kc, m * 128:(m + 1) * 128], ybT[:, kc, tsl], kc == 0, kc == 3) for kc in range(4)],
                   ["WB", "bufC0", "bufC1"] + ["ybT%d_%d" % (kc, tc) for kc in range(4)], [PK[bb]])
                ta = T[2][:, (2 * e) * 512:(2 * e + 1) * 512]
                tb = T[2][:, (2 * e + 1) * 512:(2 * e + 2) * 512]
                STT("dve", ta, SA[:, tsl], 1.0, ps[ba][:, :], ALU.add, ALU.mult, [PK[ba]] + TK(0, (tc,)), TK(2, (2 * e,)))
                STT("dve", tb, SBt[:, tsl], 1.0, ps[bb][:, :], ALU.add, ALU.mult, [PK[bb]] + TK(1, (tc,)), TK(2, (2 * e + 1,)))
                TT("dve", bufA[:, m, tsl], ta, tb, ALU.add, TK(2, (2 * e, 2 * e + 1)), ["kT%d_%d" % (m, tc)])
        DBG("mergedT", bufA[:, 2, :], [128, S_], ["kT2_%d" % i for i in range(4)], BF16)

        lnv = bufD[:, :, :].rearrange("p a b -> p (a b)").bitcast(F32)
        lnG = lnv[:, 0:1024]
        lnB = lnv[:, 1024:2048]
        dkeys = ["cqn%d" % i for i in range(4)] + ["ckvn%d" % i for i in range(4)]
        S.dma("lnp", [(lnG, ln_g.partition_broadcast(128)), (lnB, ln_b.partition_broadcast(128))], writes=dkeys)
        rekey("pool", ["xT%d" % i for i in range(4)], ["xs%d" % i for i in range(8)])
        xsl = xTb[:, :, :].rearrange("p a b -> p (a b)").bitcast(F32)
        OTs = [T[1][:, 0:1024], T[1][:, 1024:2048], T[2][:, 0:1024], T[2][:, 1024:2048]]
        OTk = [TK(1, (0, 1)), TK(1, (2, 3)), TK(2, (0, 1)), TK(2, (2, 3))]

        def emit_xload(t):
            sl_ = t % 8
            S.dma("xs%d" % sl_, [(xsl[:, sl_ * 1024:(sl_ + 1) * 1024], x[t * 128:(t + 1) * 128, :])], writes=["xs%d" % sl_])

        for t in range(8):
            emit_xload(t)
        def eA(t):
            e = t % 2
            sl_ = t % 8
            o4 = t % 4
            XT = xsl[:, sl_ * 1024:(sl_ + 1) * 1024]
            OT = OTs[o4]
            r = T[0][:, e * 1024:(e + 1) * 1024]
            rk = TK(0, (2 * e, 2 * e + 1))
            st12 = small[:, 32 + e * 12:32 + e * 12 + 12]
            mv = small[:, 56 + e * 2:58 + e * 2]
            rs = small[:, 60 + e:61 + e]
            nmr = small[:, 62 + e:63 + e]
            sk = ["smallE%d" % e]
            for half in range(2):
                bank = 2 * e + half
                hs = slice(half * 512, (half + 1) * 512)
                MM([(ps[bank][:, :], bufA[:, kc, t * 128:(t + 1) * 128], wOut[:, kc, hs], kc == 0, kc == 7)
                    for kc in range(8)], ["kT%d_%d" % (kc, t // 4) for kc in range(8)] + vpk(0, 8192), [PK[bank]])
                STT("dve", r[:, hs], XT[:, hs], ALPHA, ps[bank][:, :], ALU.mult, ALU.add, ["xs%d" % sl_, PK[bank]], TK(0, (2 * e + half,)))
                S.op("dve", lambda en, o=st12[:, half * 6:half * 6 + 6], i=r[:, hs]: en.bn_stats(out=o, in_=i),
                     TK(0, (2 * e + half,)), sk)

        def eB(t):
            e = t % 2
            sl_ = t % 8
            o4 = t % 4
            XT = xsl[:, sl_ * 1024:(sl_ + 1) * 1024]
            OT = OTs[o4]
            r = T[0][:, e * 1024:(e + 1) * 1024]
            rk = TK(0, (2 * e, 2 * e + 1))
            st12 = small[:, 32 + e * 12:32 + e * 12 + 12]
            mv = small[:, 56 + e * 2:58 + e * 2]
            rs = small[:, 60 + e:61 + e]
            nmr = small[:, 62 + e:63 + e]
            sk = ["smallE%d" % e]
            S.op("dve", lambda en, o=mv, i=st12: en.bn_aggr(out=o, in_=i), sk, sk)
            TS("pool", rs, mv[:, 1:2], LN_EPS, None, ALU.add, None, sk, sk)
            TT("pool", rs, rs, nhalf[:, 0:1], ALU.pow, sk + ["nhalf"], sk)
            STT("dve", nmr, mv[:, 0:1], -1.0, rs, ALU.mult, ALU.mult, sk, sk)
            ACT(OT, r, AF.Identity, rk + sk, OTk[o4], bias=nmr, scale=rs)

        def eC(t):
            e = t % 2
            sl_ = t % 8
            o4 = t % 4
            XT = xsl[:, sl_ * 1024:(sl_ + 1) * 1024]
            OT = OTs[o4]
            r = T[0][:, e * 1024:(e + 1) * 1024]
            rk = TK(0, (2 * e, 2 * e + 1))
            st12 = small[:, 32 + e * 12:32 + e * 12 + 12]
            mv = small[:, 56 + e * 2:58 + e * 2]
            rs = small[:, 60 + e:61 + e]
            nmr = small[:, 62 + e:63 + e]
            sk = ["smallE%d" % e]
            TT("dve", OT, OT, lnG, ALU.mult, OTk[o4] + ["cqn0"], OTk[o4])
            TT("dve", OT, OT, lnB, ALU.add, OTk[o4] + ["cqn0"], OTk[o4])
            S.dma("ot%d" % o4, [(out[t * 128:(t + 1) * 128, :], OT)], reads=OTk[o4])
            if t + 8 < 16:
                emit_xload(t + 8)

        skew(16, [eA, eB, eC])
        S.wait_all_dma()
        S.emit()
    return nc, dbg_out


def _prep(x, positions, w_in, b_in, g_q, w_uq, g_kv, w_ukv, w_oa, sgu_ln_g, sgu_ln_b, w_s, b_s,
          w_ob, w_out, ln_g, ln_b):
    f = lambda a: np.ascontiguousarray(np.asarray(a), dtype=np.float32)
    W = f(w_in)[0]
    b = f(b_in)[0]
    cols = np.r_[0:512, 544:1568, 2080:4640]
    perm = np.r_[528:544, 512:528]
    w_kpe = np.zeros((D_, 2, 96), np.float32)
    w_kpe[:, 0, 64:96] = W[:, 512:544]
    w_kpe[:, 1, 64:96] = W[:, perm]
    b_kpe = np.zeros((128, 2), np.float32)
    b_kpe[64:96, 0] = b[512:544]
    b_kpe[64:96, 1] = b[perm]
    wq = f(w_uq)[0]
    w_uqb = np.zeros((384, 8, 96), np.float32)
    w_uqb[:, :, 64:96] = wq[:, :, 64 + np.r_[16:32, 0:16]]
    wkv_ = f(w_ukv)[0]
    consts = np.zeros((128, 388), np.float32)
    consts[:, 0:128] = np.eye(128, dtype=np.float32)
    kk = np.arange(128)[:, None]
    qq = np.arange(128)[None, :]
    consts[:, 128:256] = np.where(kk <= qq, 0.0, -30000.0)
    consts[:, 256:384] = np.where(kk <= qq, 1.0, 0.0)
    inv_freq = (np.float32(10000.0) ** (-np.arange(0, 32, 2, dtype=np.float32) / np.float32(32))).astype(np.float32)
    consts[64:96, 384] = np.concatenate([inv_freq, inv_freq])
    consts[64:80, 385] = -1.0
    consts[80:96, 385] = 1.0
    shared = {
        "w_main": np.ascontiguousarray(W[:, cols]),
        "b_main": np.ascontiguousarray(b[cols].reshape(32, 128).T),
        "w_v": np.ascontiguousarray(W[:, 1568:2080]),
        "b_v": np.ascontiguousarray(b[1568:2080][None, :]),
        "w_kpe": np.ascontiguousarray(w_kpe.reshape(D_, 192)),
        "b_kpe": b_kpe,
        "w_uqa": np.ascontiguousarray(wq.reshape(384, 768)),
        "w_uqb": np.ascontiguousarray(w_uqb.reshape(384, 768)),
        "g_q": np.ascontiguousarray(f(g_q)[0].reshape(3, 128).T),
        "g_kv": np.ascontiguousarray(f(g_kv)[0].reshape(1, 128).T),
        "w_kn": np.ascontiguousarray(wkv_[:, :, 0:64].reshape(128, 512)),
        "w_vv": np.ascontiguousarray(wkv_[:, :, 64:128].reshape(128, 512)),
        "w_oa": f(w_oa)[0],
        "w_ob": f(w_ob)[0],
        "w_out": f(w_out)[0],
        "sgu_g": np.ascontiguousarray(f(sgu_ln_g)[0].reshape(4, 128).T),
        "sgu_b": np.ascontiguousarray(f(sgu_ln_b)[0].reshape(4, 128).T),
        "w_sT": np.ascontiguousarray(np.transpose(f(w_s)[0], (2, 0, 1)).reshape(128, 1024)),
        "b_s": f(b_s)[0],
        "ln_g": f(ln_g)[0][None, :],
        "ln_b": f(ln_b)[0][None, :],
        "consts": consts,
    }
    xs = f(x)
    ps_ = np.ascontiguousarray(np.asarray(positions), dtype=np.int32)
    in_maps = []
    for bi in range(xs.shape[0]):
        m = dict(shared)
        m["x"] = xs[bi]
        m["xT"] = np.ascontiguousarray(xs[bi].T)
        m["pos"] = ps_[bi][None, :]
        in_maps.append(m)
    return in_maps


def kernel(**inputs):
    in_maps = _prep(**inputs)
    nc, _ = build()
    res = run_bass_kernel_spmd(nc, in_maps, core_ids=list(range(8)))
    return np.stack([np.asarray(r["out"], dtype=np.float32) for r in res.results], axis=0)
```

```python
import math
import numpy as np
import concourse.bass as bass
import concourse.mybir as mybir
from concourse.bass_utils import run_bass_kernel_spmd
from contextlib import ExitStack

F32 = mybir.dt.float32
BF16 = mybir.dt.bfloat16
I32 = mybir.dt.int32
AF = mybir.ActivationFunctionType
ALU = mybir.AluOpType

S_ = 2048
D_ = 1024
NT = 4
TC = 512
ALPHA = 2.0 ** 0.25
QSCALE = 96.0 ** -0.5
RMS_EPS = 1e-6
LN_EPS = 1e-5
GK = 0.044715 ** 0.5
GS = 2.0 * (2.0 / math.pi) ** 0.5
PI = math.pi
C1 = 6.28125
C2 = 2.0 * math.pi - 6.28125


class Sched:
    CE = ("pe", "act", "dve", "pool")

    def __init__(self, nc, stack):
        self.nc = nc
        self.stack = stack
        self.prog = {e: [] for e in self.CE + ("sp",)}
        self.sem = {e: stack.enter_context(nc.semaphore("prog_" + e)) for e in self.CE}
        self.cnt = {e: 0 for e in self.CE}
        self.dsem = {}
        self.seen = {e: {} for e in self.CE + ("sp",)}
        self.lastw = {}
        self.readers = {}

    def _semh(self, k):
        return self.sem[k] if isinstance(k, str) else self.dsem[k[1]][0]

    def _deps(self, eng, reads, writes):
        need = {}

        def add(ev, kind):
            if ev is None:
                return
            k, v = ev
            if k == eng and eng == "pe":
                return
            if need.get(k, 0) < v:
                need[k] = v

        for r in reads:
            add(self.lastw.get(r), "raw")
        for w in writes:
            add(self.lastw.get(w), "waw")
            for k, v in self.readers.get(w, {}).items():
                add((k, v), "war")
        waits = []
        for k, v in need.items():
            if self.seen[eng].get(k, 0) >= v:
                continue
            self.seen[eng][k] = v
            waits.append((k, v))
        return waits

    def _commit(self, ev, reads, writes):
        k, v = ev
        for r in reads:
            d = self.readers.setdefault(r, {})
            if d.get(k, 0) < v:
                d[k] = v
        for w in writes:
            self.lastw[w] = ev
            self.readers[w] = {}

    def op(self, eng, fn, reads=(), writes=()):
        waits = self._deps(eng, reads, writes)
        self.cnt[eng] += 1
        self.prog[eng].append((waits, [fn], self.sem[eng], 1))
        self._commit((eng, self.cnt[eng]), reads, writes)

    def pe(self, fns, reads=(), writes=()):
        waits = self._deps("pe", reads, writes)
        self.cnt["pe"] += 1
        self.prog["pe"].append((waits, list(fns), self.sem["pe"], 1))
        self._commit(("pe", self.cnt["pe"]), reads, writes)

    def dma(self, chan, pairs, reads=(), writes=(), queue="sp"):
        waits = self._deps(queue, reads, writes)
        if chan not in self.dsem:
            self.dsem[chan] = [self.stack.enter_context(
                self.nc.semaphore("dma_%d" % len(self.dsem))), 0]
        ent = self.dsem[chan]
        first = True
        for (o, i) in pairs:
            fn = (lambda e, o=o, i=i: e.dma_start(out=o, in_=i))
            self.prog[queue].append((waits if first else [], [fn], ent[0], 16))
            first = False
        ent[1] += 16 * len(pairs)
        self._commit((("dma", chan), ent[1]), reads, writes)

    def wait_all_dma(self, eng="sp"):
        waits = []
        for chan, (s, c) in self.dsem.items():
            if c and self.seen[eng].get(("dma", chan), 0) < c:
                waits.append((("dma", chan), c))
                self.seen[eng][("dma", chan)] = c
        self.prog[eng].append((waits, [], None, 0))

    def _replay(self, name, eng):
        for waits, fns, sem, inc in self.prog[name]:
            for k, v in waits:
                eng.wait_ge(self._semh(k), v)
            for i, fn in enumerate(fns):
                ins = fn(eng)
                if i == len(fns) - 1 and sem is not None:
                    ins.then_inc(sem, inc)

    def emit(self):
        with self.nc.Block() as block:
            @block.sync
            def _(e):
                self._replay("sp", e)

            @block.tensor
            def _(e):
                self._replay("pe", e)

            @block.scalar
            def _(e):
                self._replay("act", e)

            @block.vector
            def _(e):
                self._replay("dve", e)

            @block.gpsimd
            def _(e):
                self._replay("pool", e)


def build(dbg=()):
    nc = bass.Bass("TRN2", target_bir_lowering=False)

    def din(name, shape, dt=F32):
        return nc.dram_tensor(name, shape, dt, kind="ExternalInput").ap()

    xT = din("xT", [D_, S_])
    x = din("x", [S_, D_])
    pos = din("pos", [1, S_], I32)
    w_main = din("w_main", [D_, 4096])
    b_main = din("b_main", [128, 32])
    w_v = din("w_v", [D_, 512])
    b_v = din("b_v", [1, 512])
    w_kpe = din("w_kpe", [D_, 192])
    b_kpe = din("b_kpe", [128, 2])
    w_uqa = din("w_uqa", [384, 768])
    w_uqb = din("w_uqb", [384, 768])
    g_q = din("g_q", [128, 3])
    g_kv = din("g_kv", [128, 1])
    w_kn = din("w_kn", [128, 512])
    w_vv = din("w_vv", [128, 512])
    w_oa = din("w_oa", [512, D_])
    w_ob = din("w_ob", [512, D_])
    w_out = din("w_out", [D_, D_])
    sgu_g = din("sgu_g", [128, 4])
    sgu_b = din("sgu_b", [128, 4])
    w_sT = din("w_sT", [128, 1024])
    b_s = din("b_s", [8, 128])
    ln_g = din("ln_g", [1, D_])
    ln_b = din("ln_b", [1, D_])
    consts = din("consts", [128, 388])
    out = nc.dram_tensor("out", [S_, D_], F32, kind="ExternalOutput").ap()
    dbg_out = {}

    with ExitStack() as st:
        S = Sched(nc, st)

        def sb(name, shape, dt):
            return st.enter_context(nc.sbuf_tensor(name, shape, dt))

        xTb = sb("xTb", [128, 8, S_], BF16)
        bufA = sb("bufA", [128, 8, S_], BF16)
        bufB = sb("bufB", [128, 16 * 768], BF16)
        yaT = sb("yaT", [128, 4, S_], BF16)
        bufC = sb("bufC", [128, 2, S_], F32)
        ybT = bufC[:, :, :].rearrange("p a b -> p (a b)").bitcast(BF16).rearrange("p (c n) -> p c n", c=4)
        bufD = sb("bufD", [128, 4, S_], BF16)
        T = [sb("T%d" % i, [128, S_], F32) for i in range(3)]
        wst = [sb("wst%d" % i, [128, 8, 128], F32) for i in range(2)]
        wbf = [sb("wbf%d" % i, [128, 8, 128], BF16) for i in range(2)]
        WB = sb("WB", [128, 8192], BF16)
        wkv = sb("wkv", [128, 1024], BF16)
        wsb = sb("wsb", [128, 1024], BF16)
        PT = [sb("PT%d" % i, [128, 512], BF16) for i in range(4)]
        dmy = sb("dmy", [128, 4], F32)
        rdn = sb("rdn", [128, 512], F32)
        epsq = sb("epsq", [128, 1], F32)
        nhalf = sb("nhalf", [128, 1], F32)
        bmh = sb("bmh", [128, 32], F32)
        cst = sb("cst", [128, 388], F32)
        identb = sb("identb", [128, 128], BF16)
        mnegb = sb("mnegb", [128, 128], BF16)
        onesb = sb("onesb", [128, 128], BF16)
        onesf = sb("onesf", [128, 128], F32)
        bm = sb("bm", [128, 32], F32)
        bmg = sb("bmg", [128, 32], F32)
        bk = sb("bk", [128, 2], F32)
        gq = sb("gq", [128, 3], F32)
        gkv = sb("gkv", [128, 1], F32)
        sg_g = sb("sg_g", [128, 4], F32)
        sg_b = sb("sg_b", [128, 4], F32)
        bvb = sb("bvb", [128, 512], F32)
        bsb = sb("bsb", [128, 4, 128], F32)
        bias2 = sb("bias2", [128, 4, 128], F32)
        small = sb("small", [128, 64], F32)
        ps = [st.enter_context(nc.psum_tensor("ps%d" % i, [128, 512], F32)) for i in range(8)]
        PK = ["ps%d" % i for i in range(8)]

        tri01 = cst[:, 256:384]
        invf2 = cst[:, 384:385]
        sgn = cst[:, 385:386]

        def ACT(out_, in_, func, reads, writes, bias=None, scale=1.0):
            kw = dict(out=out_, in_=in_, func=func, scale=scale)
            if bias is not None:
                kw["bias"] = bias
            S.op("act", lambda e: e.activation(**kw), reads, writes)

        def TS(eng, out_, in0, s1, s2, op0, op1, reads, writes):
            if s2 is None:
                S.op(eng, lambda e: e.tensor_scalar(out=out_, in0=in0, scalar1=s1, scalar2=None, op0=op0), reads, writes)
            else:
                S.op(eng, lambda e: e.tensor_scalar(out=out_, in0=in0, scalar1=s1, scalar2=s2, op0=op0, op1=op1), reads, writes)

        def STT(eng, out_, in0, scalar, in1, op0, op1, reads, writes):
            S.op(eng, lambda e: e.scalar_tensor_tensor(out=out_, in0=in0, scalar=scalar, in1=in1, op0=op0, op1=op1), reads, writes)

        def TT(eng, out_, in0, in1, op, reads, writes):
            S.op(eng, lambda e: e.tensor_tensor(out=out_, in0=in0, in1=in1, op=op), reads, writes)

        def CP(eng, out_, in_, reads, writes):
            S.op(eng, lambda e: e.tensor_copy(out=out_, in_=in_), reads, writes)

        def RCP(out_, in_, reads, writes):
            S.op("dve", lambda e: e.reciprocal(out=out_, in_=in_), reads, writes)

        def MSET(eng, ap, val, writes):
            S.op(eng, lambda e: e.memset(ap, val), (), writes)

        def MM(specs, reads, writes):
            fns = []
            for (o, l, r, s0, s1) in specs:
                fns.append(lambda e, o=o, l=l, r=r, s0=s0, s1=s1: e.matmul(o, lhsT=l, rhs=r, start=s0, stop=s1))
            S.pe(fns, reads, writes)

        cast_rr = [0]

        def cast_eng():
            cast_rr[0] += 1
            return "dve"

        def TK(i, qs=(0, 1, 2, 3)):
            return ["T%d_%d" % (i, q) for q in qs]

        def vpk(a, b):
            return ["Vp%d" % i for i in range(a // 768, (b - 1) // 768 + 1)]

        def DBG(name, ap, shape, keys, dt=F32):
            if name not in dbg:
                return
            d = nc.dram_tensor("dbg_" + name, list(shape), dt, kind="ExternalOutput").ap()
            dbg_out[name] = "dbg_" + name
            S.dma("dbg_" + name, [(d, ap)], reads=keys)

        def skew(n, stages):
            for step in range(n + len(stages) - 1):
                for s_i in reversed(range(len(stages))):
                    it = step - s_i
                    if 0 <= it < n:
                        stages[s_i](it)

        def rekey(eng, old, new):
            S.op(eng, lambda e: e.memset(dmy[0:1, 0:1], 0.0), (), list(old) + list(new) + ["dmy"])

        def CAST(eng, out_, in_, reads, writes, mul=None):
            if eng == "act":
                ACT(out_, in_, AF.Copy, reads, writes, scale=(1.0 if mul is None else mul))
            elif mul is None:
                CP(eng, out_, in_, reads, writes)
            else:
                TS(eng, out_, in_, mul, None, ALU.mult, None, reads, writes)

        S.dma("cst", [(cst[:], consts)], writes=["cst"])
        CP("pool", identb[:], cst[:, 0:128], ["cst"], ["identb"])
        CP("pool", mnegb[:], cst[:, 128:256], ["cst"], ["mnegb"])
        MSET("pool", onesb[:], 1.0, ["onesb"])
        MSET("pool", onesf[:], 1.0, ["onesf"])
        MSET("pool", nhalf[:], -0.5, ["nhalf"])
        MSET("pool", epsq[:], RMS_EPS, ["epsq"])

        lw_rr = [0]
        lw_engs = ["dve", "act"]

        def load_w(dst, src, n, keys_dst, extra_reads=(), dst3=None, mul=None):
            tix = lw_rr[0] % 3
            eng = lw_engs[lw_rr[0] % len(lw_engs)] if mul is None else "act"
            lw_rr[0] += 1
            stg = T[tix][:, 0:n]
            if len(src.shape) == 3:
                stg_v = stg.rearrange("p (a b) -> p a b", a=src.shape[1])
            else:
                stg_v = stg
            S.dma("T%d" % tix, [(stg_v, src)], writes=TK(tix))
            if dst3 is not None:
                if eng == "act":
                    CAST(eng, dst3, stg_v, TK(tix) + list(extra_reads), keys_dst, mul)
                else:
                    for a_ in range(src.shape[1]):
                        CAST(eng, dst3[:, a_, :], stg_v[:, a_, :], TK(tix) + list(extra_reads), keys_dst, mul)
            else:
                CAST(eng, dst, stg, TK(tix) + list(extra_reads), keys_dst, mul)

        w_main_v = w_main.rearrange("(c p) n -> p c n", p=128)

        stream_ix = [0]

        def stream_dma(j):
            s_ = stream_ix[0] % 2
            stream_ix[0] += 1
            S.dma("wst%d" % s_, [(wst[s_][:, :, :], w_main_v[:, :, j * 128:(j + 1) * 128])], writes=["wst%d" % s_])
            return s_

        def stream_cast(s_):
            CP(cast_eng(), wbf[s_][:, :, :], wst[s_][:, :, :], ["wst%d" % s_], ["wbf%d" % s_])

        def stream_chunk(j):
            s_ = stream_dma(j)
            stream_cast(s_)
            return s_

        def XK(tc):
            return ["xT%d" % tc]

        def stage1(s_, tc, bank):
            tsl_ = slice(tc * TC, (tc + 1) * TC)
            MM([(ps[bank][:, :], wbf[s_][:, dk, :], xTb[:, dk, tsl_], dk == 0, dk == 7) for dk in range(8)],
               XK(tc) + ["wbf%d" % s_], [PK[bank]])

        cos2 = bufC[:, 0, :]
        sinS = bufC[:, 1, :]
        R = slice(64, 96)
        scr = bufA[:, :, :].rearrange("p a b -> p (a b)").bitcast(F32)
        A0, A1, A2 = scr[:, 0:2048], scr[:, 2048:4096], scr[:, 4096:6144]

        def AK(i):
            return ["kT%d_%d" % (h_, t_) for h_ in (2 * i, 2 * i + 1) for t_ in range(4)]

        def emit_rope_chain():
            posi = A0.bitcast(I32)
            kint = A2.bitcast(I32)
            S.dma("pos", [(posi[R, :], pos.partition_broadcast(32))], writes=AK(0))
            CP("dve", A1[R, :], posi[R, :], AK(0), AK(1))
            TS("dve", A1[R, :], A1[R, :], invf2[R, :], None, ALU.mult, None, AK(1) + ["cst"], AK(1))
            for which, shift in ((1, 0.0),):
                r_ = bufC[R, which, :]
                rk_ = ["bufC%d" % which]
                TS("dve", kint[R, :], A1[R, :], shift, 1.0 / (2 * PI), ALU.add, ALU.mult, AK(1), AK(2))
                CP("dve", A0[R, :], kint[R, :], AK(2), AK(0))
                TS("dve", r_, A1[R, :], shift, None, ALU.add, None, AK(1), rk_)
                STT("dve", r_, A0[R, :], -C1, r_, ALU.mult, ALU.add, AK(0) + rk_, rk_)
                STT("dve", r_, A0[R, :], -C2, r_, ALU.mult, ALU.add, AK(0) + rk_, rk_)
                TS("dve", A0[R, :], r_, PI, 2 * PI, ALU.is_gt, ALU.mult, rk_, AK(0))
                TT("dve", r_, r_, A0[R, :], ALU.subtract, AK(0) + rk_, rk_)
                TS("dve", A0[R, :], r_, -PI, 2 * PI, ALU.is_lt, ALU.mult, rk_, AK(0))
                TT("dve", r_, r_, A0[R, :], ALU.add, AK(0) + rk_, rk_)
                TS("dve", r_, r_, -3.1415925, 3.1415925, ALU.max, ALU.min, rk_, rk_)
            rs_, rc_ = bufC[R, 1, :], bufC[R, 0, :]
            TS("dve", rc_, rs_, PI / 2, None, ALU.add, None, ["bufC1"], ["bufC0"])
            TS("dve", A0[R, :], rc_, PI, 2 * PI, ALU.is_gt, ALU.mult, ["bufC0"], AK(0))
            TT("dve", rc_, rc_, A0[R, :], ALU.subtract, AK(0) + ["bufC0"], ["bufC0"])
            TS("dve", rc_, rc_, -3.1415925, 3.1415925, ALU.max, ALU.min, ["bufC0"], ["bufC0"])

        def emit_rope_finish():
            for which in (1, 0):
                ACT(bufC[R, which, :], bufC[R, which, :], AF.Sin, ["bufC%d" % which], ["bufC%d" % which])
            TS("dve", sinS[R, :], sinS[R, :], sgn[R, :], None, ALU.mult, None, ["bufC1", "cst"], ["bufC1"])
            DBG("cos2", bufC[R, 0, :], [32, S_], ["bufC0"])
            DBG("sinS", bufC[R, 1, :], [32, S_], ["bufC1"])

        emit_rope_chain()
        lw_engs[:] = ["act"]
        wA = WB[:, 0:4096].rearrange("p (c n) -> p c n", c=8)
        for half in range(2):
            load_w(WB[:, half * 2048:(half + 1) * 2048], w_main_v[:, half * 4:(half + 1) * 4, 0:512], 2048, ["WB"])
        wK = WB[:, 4096:5632].rearrange("p (c n) -> p c n", c=8)
        xT_v = xT.rearrange("(c p) s -> p c s", p=128)

        def load_xT(tc):
            tsl_ = slice(tc * TC, (tc + 1) * TC)
            for q4 in range(4):
                s_ = stream_ix[0] % 2
                stream_ix[0] += 1
                stg = wst[s_][:, :, :].rearrange("p a b -> p (a b)").rearrange("p (a b) -> p a b", a=2)
                S.dma("wst%d" % s_, [(stg, xT_v[:, 2 * q4:2 * q4 + 2, tsl_])], writes=["wst%d" % s_])
                if tc == 0:
                    CAST("act", xTb[:, 2 * q4:2 * q4 + 2, tsl_], stg, ["wst%d" % s_], XK(tc))
                else:
                    for a_ in range(2):
                        CP("dve", xTb[:, 2 * q4 + a_, tsl_], stg[:, a_, :], ["wst%d" % s_], XK(tc))

        load_xT(0)
        S.dma("bias", [(bm[:], b_main), (bk[:], b_kpe), (gq[:], g_q), (gkv[:], g_kv),
                       (sg_g[:], sgu_g), (sg_b[:], sgu_b),
                       (bvb[:], b_v.partition_broadcast(128))], writes=["bias"])
        S.dma("bsb", [(bsb[(g % 2) * 64:(g % 2) * 64 + 64, g // 2, :],
                       b_s[g:g + 1, :].partition_broadcast(64)) for g in range(8)], writes=["bsb"])
        TS("pool", bmg[:], bm[:], GK, None, ALU.mult, None, ["bias"], ["bmg"])
        TS("pool", bmh[:], bm[:], 0.5, None, ALU.mult, None, ["bias"], ["bmh"])
        load_w(WB[:, 4096:5632], w_kpe.rearrange("(c p) n -> p c n", p=128), 1536, ["WBk"])
        load_w(wkv[:, 0:512], w_kn, 512, ["wkv"])
        load_w(wkv[:, 512:1024], w_vv, 512, ["wkv"])
        lw_engs[:] = ["dve", "act"]
        S.dma("T2", [(T[2][:, 0:1024], w_sT)], writes=TK(2, (0, 1)))
        for g in range(8):
            TT("pool", wsb[:, g * 128:(g + 1) * 128], T[2][:, g * 128:(g + 1) * 128], tri01, ALU.mult,
               TK(2, (0, 1)) + ["cst"], ["wsb"])


        lw_engs[:] = ["dve", "act"]

        cqn = bufD[:, 0:3, :]
        ckvn = bufD[:, 3, :]
        Vp = bufB[:, :].rearrange("k (t c) -> k t c", t=16)
        MSET("pool", bufB[:, :].rearrange("k (g c) -> k g c", c=192)[:, :, 64:128], 1.0, ["VpOnes"])

        def phaseA_lat(tc):
            tsl = slice(tc * TC, (tc + 1) * TC)
            Rw = T[0][:, :].rearrange("p (j n) -> p j n", j=4)
            SQ = T[1][:, :].rearrange("p (j n) -> p j n", j=4)
            def stats_mm(j):
                if j < 3:
                    MM([(ps[2][:, :], onesf[:, :], SQ[:, j, :], j == 0, j == 2)], TK(1, (j,)) + ["onesf"], [PK[2]])
                else:
                    MM([(ps[3][:, :], onesf[:, :], SQ[:, j, :], True, True)], TK(1, (j,)) + ["onesf"], [PK[3]])

            for j in range(4):
                bank = j % 2
                MM([(ps[bank][:, :], wA[:, dk, j * 128:(j + 1) * 128], xTb[:, dk, tsl], dk == 0, dk == 7)
                    for dk in range(8)], XK(tc) + ["WB"], [PK[bank]])
                if j >= 1:
                    stats_mm(j - 1)
                ACT(Rw[:, j, :], ps[bank][:, :], AF.Identity, [PK[bank], "bias"], TK(0, (j,)), bias=bm[:, j:j + 1])
                ACT(SQ[:, j, :], ps[bank][:, :], AF.Square, [PK[bank], "bias"], TK(1, (j,)), bias=bm[:, j:j + 1])
            stats_mm(3)
            rq = T[2][:, 0:512]
            rkv = T[2][:, 512:1024]
            ACT(rq, ps[2][:, :], AF.Ln, [PK[2], "epsq"], TK(2, (0,)), bias=epsq[:, 0:1], scale=1.0 / 384)
            ACT(rkv, ps[3][:, :], AF.Ln, [PK[3], "epsq"], TK(2, (1,)), bias=epsq[:, 0:1], scale=1.0 / 128)
            ACT(rq, rq, AF.Exp, TK(2, (0,)), TK(2, (0,)), scale=-0.5)
            ACT(rkv, rkv, AF.Exp, TK(2, (1,)), TK(2, (1,)), scale=-0.5)
            STT("dve", ckvn[:, tsl], Rw[:, 3, :], gkv[:, 0:1], rkv, ALU.mult, ALU.mult,
                TK(0, (3,)) + TK(2, (1,)) + ["bias"], ["ckvn%d" % tc])
            for j in range(3):
                STT("dve", cqn[:, j, tsl], Rw[:, j, :], gq[:, j:j + 1], rq, ALU.mult, ALU.mult,
                    TK(0, (j,)) + TK(2, (0,)) + ["bias"], ["cqn%d" % tc])

        def phaseA_kpe(tc):
            tsl = slice(tc * TC, (tc + 1) * TC)
            MM([(ps[4][0:96, :], wK[:, dk, 0:96], xTb[:, dk, tsl], dk == 0, dk == 7) for dk in range(8)],
               XK(tc) + ["WBk"], [PK[4]])
            MM([(ps[5][0:96, :], wK[:, dk, 96:192], xTb[:, dk, tsl], dk == 0, dk == 7) for dk in range(8)],
               XK(tc) + ["WBk"], [PK[5]])
            t1 = T[2][:, 1024:1536]
            t2 = T[2][:, 1536:2048]
            STT("dve", t1[R, :], ps[4][R, :], bk[R, 0:1], cos2[R, tsl], ALU.add, ALU.mult,
                [PK[4], "bias", "bufC0"], TK(2, (2,)))
            STT("dve", t2[R, :], ps[5][R, :], bk[R, 1:2], sinS[R, tsl], ALU.add, ALU.mult,
                [PK[5], "bias", "bufC1"], TK(2, (3,)))
            TT("dve", t1[R, :], t1[R, :], t2[R, :], ALU.add, TK(2, (2, 3)), TK(2, (2,)))
            for h in range(8):
                CAST(("act", "dve", "act", "act", "dve", "act", "act", "dve")[h], bufA[R, h, tsl], t1[R, :],
                     TK(2, (2,)), ["kT%d_%d" % (h, tc)])

        def phaseA_up(tc):
            tsl = slice(tc * TC, (tc + 1) * TC)
            for p in range(4):
                bank = 6 + (p % 2)
                MM([(ps[bank][:, :], wkv[:, p * 128:(p + 1) * 128], ckvn[:, tsl], True, True)],
                   ["wkv", "ckvn%d" % tc], [PK[bank]])
                ACT(bufA[0:64, 2 * p, tsl], ps[bank][0:64, :], AF.Copy, [PK[bank]], ["kT%d_%d" % (2 * p, tc)])
                CP("dve", bufA[0:64, 2 * p + 1, tsl], ps[bank][64:128, :], [PK[bank]], ["kT%d_%d" % (2 * p + 1, tc)])
            for tt in range(4):
                t_ = tc * 4 + tt
                bank = 6 + (tt % 2)
                MM([(ps[bank][:, :], ckvn[:, t_ * 128:(t_ + 1) * 128], wkv[:, 512:1024], True, True)],
                   ["wkv", "ckvn%d" % tc], [PK[bank]])
                src = ps[bank][:, :].rearrange("k (p e d) -> k p e d", p=4, e=2)
                dstv = Vp[:, t_, :].rearrange("k (p c) -> k p c", c=192)
                ACT(dstv[:, :, 0:64], src[:, :, 0, :], AF.Copy, [PK[bank]], ["Vp%d" % t_])
                CP("dve", dstv[:, :, 128:192], src[:, :, 1, :], [PK[bank]], ["Vp%d" % t_])

        for tc in range(NT):
            if tc + 1 < NT:
                load_xT(tc + 1)
            phaseA_lat(tc)
            if tc >= 1:
                phaseA_up(tc - 1)
        phaseA_up(NT - 1)
        emit_rope_finish()
        for tc in range(NT):
            phaseA_kpe(tc)
        for bnk in range(2):
            MM([(ps[bnk][:, :], onesb[:, :], wsb[:, bnk * 512:(bnk + 1) * 512], True, True)], ["onesb", "wsb"], [PK[bnk]])
        for g in range(8):
            p = g // 2
            rows = slice((g % 2) * 64, (g % 2) * 64 + 64)
            STT("dve", bias2[rows, p, :], ps[g // 4][rows, (g % 4) * 128:(g % 4 + 1) * 128], sg_b[rows, p:p + 1],
                bsb[rows, p, :], ALU.mult, ALU.add, [PK[g // 4], "bias", "bsb"], ["bias2"])
        DBG("cqn", bufD[:, 0, :], [128, S_], ["cqn%d" % i for i in range(4)], BF16)
        DBG("ckvn", bufD[:, 3, :], [128, S_], ["ckvn%d" % i for i in range(4)], BF16)
        DBG("kT0", bufA[:, 0, :], [128, S_], ["kT0_%d" % i for i in range(4)], BF16)
        DBG("kT3", bufA[:, 3, :], [128, S_], ["kT3_%d" % i for i in range(4)], BF16)
        DBG("Vp", bufB[:, :], [128, 16 * 768], ["Vp%d" % i for i in range(16)] + ["VpOnes"], BF16)

        wqa = WB[:, 0:2304].rearrange("p (c n) -> p c n", c=3)
        wqb = WB[:, 2304:4608].rearrange("p (c n) -> p c n", c=3)
        for kc in range(3):
            load_w(WB[:, kc * 768:(kc + 1) * 768], w_uqa[kc * 128:(kc + 1) * 128, :], 768, ["WB", "WBk"])
            load_w(WB[:, 2304 + kc * 768:2304 + (kc + 1) * 768], w_uqb[kc * 128:(kc + 1) * 128, :], 768, ["WB", "WBk"])
        qT2 = T[1][:, :].bitcast(BF16)
        ZA = T[0]
        qall = ["qT%d_%d" % (s_, t_) for s_ in range(2) for t_ in range(4)]
        rekey("pool", TK(1), qall)
        bsb_bf = bsb[:, :, :].rearrange("p a b -> p (a b)").bitcast(BF16)
        PT.append(bsb_bf[:, 0:512])
        PT.append(bsb_bf[:, 512:1024])
        rekey("pool", ["bsb"], ["PT4", "PT5"])
        ptix = [0]
        za_slot = {}
        SB_ = (1, 6, 7)

        def emit_za(p, tc):
            tsl = slice(tc * TC, (tc + 1) * TC)
            s_ = za_slot[p]
            zb_ = 2 + (tc % 2)
            stage1(s_, tc, zb_)
            th = wst[s_][:, 0:4, :]
            zv = ZA[:, tsl].rearrange("p (a b) -> p a b", a=4)
            ACT(ZA[:, tsl], ps[zb_][:, :], AF.Identity, [PK[zb_], "bias"], TK(0, (tc,)), bias=bm[:, 4 + p:5 + p])
            ACT(th, zv, AF.Tanh, TK(0, (tc,)), ["wst%d" % s_], scale=0.5)
            STT("dve", zv, th, 1.0, zv, ALU.add, ALU.mult, ["wst%d" % s_] + TK(0, (tc,)), TK(0, (tc,)))

        def emit_qproj(h, tc):
            tsl = slice(tc * TC, (tc + 1) * TC)
            sl_ = h % 2
            qT = qT2[:, sl_ * 2048:(sl_ + 1) * 2048]
            qk = ["qT%d_%d" % (sl_, tc)]
            MM([(ps[2][0:96, :], wqa[:, kc, h * 96:(h + 1) * 96], cqn[:, kc, tsl], kc == 0, kc == 2)
                for kc in range(3)], ["WB", "cqn%d" % tc], [PK[2]])
            MM([(ps[3][0:96, :], wqb[:, kc, h * 96:(h + 1) * 96], cqn[:, kc, tsl], kc == 0, kc == 2)
                for kc in range(3)], ["WB", "cqn%d" % tc], [PK[3]])
            t1 = T[2][:, 1024:1536]
            t2 = T[2][:, 1536:2048]
            TT("dve", t1[R, :], ps[2][R, :], cos2[R, tsl], ALU.mult, [PK[2], "bufC0"], TK(2, (2,)))
            TT("dve", t2[R, :], ps[3][R, :], sinS[R, tsl], ALU.mult, [PK[3], "bufC1"], TK(2, (3,)))
            ACT(qT[0:64, tsl], ps[2][0:64, :], AF.Copy, [PK[2]], qk)
            TT("dve", qT[R, tsl], t1[R, :], t2[R, :], ALU.add, TK(2, (2, 3)), qk)

        SB4 = (1, 6, 7, 3)

        def attn_S(h, c, j, slot):
            sl_ = h % 2
            qT = qT2[:, sl_ * 2048:(sl_ + 1) * 2048]
            c0 = max(0, j - 4 * c) * 128
            sbk = SBK[slot % len(SBK)]
            k3 = slot % 6
            specs = [(ps[sbk][:, c0:512], bufA[0:96, h, j * 128:(j + 1) * 128],
                      qT[0:96, c * 512 + c0:(c + 1) * 512], True, j < 4 * c)]
            rd = ["kT%d_%d" % (h, j // 4), "qT%d_%d" % (sl_, c)]
            if j >= 4 * c:
                specs.append((ps[sbk][:, c0:c0 + 128], identb[:, :], mnegb[:, :], False, True))
                rd = rd + ["identb", "mnegb"]
            MM(specs, rd, [PK[sbk]])
            ACT(PT[k3][:, c0:512], ps[sbk][:, c0:512], AF.Exp, [PK[sbk]], ["PT%d" % k3], scale=QSCALE)

        def attn_PV(h, c, j, slot, ob):
            p, hh = divmod(h, 2)
            c0 = max(0, j - 4 * c) * 128
            k3 = slot % 6
            nj = 4 * c + 4
            vsl = slice(p * 192 + hh * 64, p * 192 + hh * 64 + 128)
            MM([(ps[ob][:, c0:512], Vp[:, j, vsl], PT[k3][:, c0:512], j == 0, j == nj - 1)],
               ["Vp%d" % j, "VpOnes", "PT%d" % k3], [PK[ob]])
            if j == nj - 1:
                orow = slice(0, 64) if hh == 0 else slice(64, 128)
                drow = slice(64, 128) if hh == 0 else slice(0, 64)
                csl = slice(c * 512, (c + 1) * 512)
                e2 = ob - 4
                accS = T[2][:, 0:512] if e2 == 0 else T[2][:, 512:1024]
                ak_ = TK(2, (e2,))
                run_deferred(tag=("norm", e2))
                ACT(accS, ps[ob][:, :], AF.Copy, [PK[ob]], ak_)

                def norm_rest(p=p, hh=hh, c=c, orow=orow, drow=drow, csl=csl, accS=accS, ak_=ak_):
                    RCP(rdn[orow, :], accS[drow, :], ak_, ["rdn"])
                    TT("dve", accS[orow, :], accS[orow, :], rdn[orow, :], ALU.mult, ak_ + ["rdn"], ak_)
                    TT("dve", yaT[orow, p, csl], accS[orow, :], ZA[orow, csl], ALU.mult,
                       ak_ + TK(0, (c,)), ["yaT%d_%d" % (p, c)])
                    if hh == 1 and p + 1 < 4:
                        deferred.append((slot_ctr[0] + 10, ("za", c), lambda p=p, c=c: emit_za(p + 1, c)))

                deferred.append((slot_ctr[0] + 3, ("norm", e2), norm_rest))

        SBK = (0, 1, 6, 7)
        slot_ctr = [0]
        LAG = 4

        za_slot[0] = stream_chunk(4)
        for tc in range(NT):
            emit_za(0, tc)
        for tc in range(NT):
            emit_qproj(0, tc)
        chunks = [(h, c) for h in range(8) for c in (3, 0, 2, 1)]
        deferred = []
        nxt = [0]
        pending = []

        def run_deferred(force=False, tag=None):
            keep = []
            items = list(deferred)
            del deferred[:]
            for it_ in items:
                due, tg, fn = it_
                if force or (tag is not None and tg == tag) or (tag is None and due <= slot_ctr[0]):
                    fn()
                else:
                    keep.append(it_)
            deferred[:0] = keep

        w_v_v = w_v.rearrange("(c p) n -> p c n", p=128)

        def prefetch_wv():
            def issue(i):
                s_ = stream_ix[0] % 2
                stream_ix[0] += 1
                stg = wst[s_][:, :, :].rearrange("p a b -> p (a b)")
                S.dma("wst%d" % s_, [(stg.rearrange("p (a b) -> p a b", a=2), w_v_v[:, 2 * i:2 * i + 2, :])],
                      writes=["wst%d" % s_])

                def cast(i=i, s_=s_, stg=stg):
                    CP("dve", WB[:, i * 1024:(i + 1) * 1024], stg, ["wst%d" % s_], ["WB", "WBk"])
                    if i + 2 < 4:
                        issue(i + 2)
                deferred.append((slot_ctr[0] + 8, ("wv", i), cast))
            issue(0)
            issue(1)

        def start_stream(ob):
            if nxt[0] >= len(chunks):
                return None
            h, c = chunks[nxt[0]]
            k = nxt[0] % 4
            nxt[0] += 1
            p, hh = divmod(h, 2)
            if k == 0 and hh == 1 and p + 1 < 4:
                za_slot[p + 1] = stream_chunk(4 + p + 1)
            if h + 1 < 8:
                emit_qproj(h + 1, (3, 0, 2, 1)[k])
            if h == 7 and k == 0:
                prefetch_wv()
            return dict(h=h, c=c, j=0, nj=4 * c + 4, ob=ob)

        streams = [start_stream(4), start_stream(5)]
        while any(s is not None for s in streams):
            for si in range(2):
                s = streams[si]
                if s is None:
                    continue
                slot = slot_ctr[0]
                slot_ctr[0] += 1
                run_deferred()
                attn_S(s["h"], s["c"], s["j"], slot)
                pending.append((s["h"], s["c"], s["j"], slot, s["ob"]))
                if len(pending) > LAG:
                    attn_PV(*pending.pop(0))
                s["j"] += 1
                if s["j"] == s["nj"]:
                    while any(pp[4] == s["ob"] for pp in pending):
                        attn_PV(*pending.pop(0))
                    streams[si] = start_stream(s["ob"])
        while pending:
            attn_PV(*pending.pop(0))
        while deferred:
            run_deferred(force=True)
        DBG("qT3", qT2[:, 2048:4096], [128, S_], ["qT1_%d" % t_ for t_ in range(4)], BF16)
        rekey("pool", qall, TK(1))
        DBG("yaT", yaT[:, 1, :], [128, S_], ["yaT1_%d" % i for i in range(4)], BF16)

        wV = WB[:, 0:4096].rearrange("p (c n) -> p c n", c=8)
        vn = bufB[:, 0:8192].rearrange("k (t f) -> k t f", t=16)
        VSETS = [((T[i][:, (2 * k) * 512:(2 * k + 1) * 512], TK(i, (2 * k,))),
                  (T[i][:, (2 * k + 1) * 512:(2 * k + 2) * 512], TK(i, (2 * k + 1,))))
                 for i in range(3) for k in range(2)]

        def vset(t):
            (vb, kvb), (sq, ksq) = VSETS[t % 6]
            e4 = t % 4
            st6 = small[:, e4 * 6:e4 * 6 + 6]
            mv = small[:, 24 + e4 * 2:26 + e4 * 2]
            rs = small[:, 32 + e4:33 + e4]
            return vb, kvb, sq, ksq, 2 + e4, st6, mv, rs, ["smallV%d" % e4]

        def v1(t):
            vb, kvb, sq, ksq, bank, st6, mv, rs, sk = vset(t)
            MM([(ps[bank][:, :], xTb[:, dk, t * 128:(t + 1) * 128], wV[:, dk, :], dk == 0, dk == 7) for dk in range(8)],
               XK(t // 4) + ["WB"], [PK[bank]])
            TT("dve", vb, ps[bank][:, :], bvb[:, :], ALU.add, [PK[bank], "bias"], kvb)
            ACT(sq, vb, AF.Square, kvb, ksq, scale=GK)

        def v2(t):
            vb, kvb, sq, ksq, bank, st6, mv, rs, sk = vset(t)
            STT("dve", sq, sq, 1.0, vb, ALU.add, ALU.mult, kvb + ksq, ksq)
            ACT(sq, sq, AF.Tanh, ksq, ksq, scale=GS / 2)

        def v3(t):
            vb, kvb, sq, ksq, bank, st6, mv, rs, sk = vset(t)
            STT("dve", vb, sq, 1.0, vb, ALU.add, ALU.mult, kvb + ksq, kvb)
            S.op("dve", lambda en, o=st6, i=vb: en.bn_stats(out=o, in_=i), kvb, sk)
            S.op("dve", lambda en, o=mv, i=st6: en.bn_aggr(out=o, in_=i), sk, sk)
            TS("pool", rs, mv[:, 1:2], 4.0 * LN_EPS, None, ALU.add, None, sk, sk)
            TT("pool", rs, rs, nhalf[:, 0:1], ALU.pow, sk + ["nhalf"], sk)

        def v4(t):
            vb, kvb, sq, ksq, bank, st6, mv, rs, sk = vset(t)
            nm_ = small[:, 40 + (t % 4):41 + (t % 4)]
            STT("dve", nm_, mv[:, 0:1], -1.0, rs, ALU.mult, ALU.mult, sk, sk)
            ACT(vn[:, t, :], vb, AF.Identity, kvb + sk, vpk(512 * t, 512 * t + 512), bias=nm_, scale=rs)

        skew(16, [v1, v2, v3, v4])
        DBG("vn", bufB[:, 0:8192], [128, 8192], vpk(0, 8192), BF16)
        def Q(i, q):
            return T[i][:, q * 512:(q + 1) * 512], TK(i, (q,))
        CSETS = [dict(ub=Q(0, 0), sq=Q(0, 1), sg=Q(0, 2), zb=Q(0, 3), mt=Q(1, 0), thz=Q(1, 1)),
                 dict(ub=Q(1, 2), sq=Q(1, 3), sg=Q(2, 0), zb=Q(2, 1), mt=Q(2, 2), thz=Q(2, 3))]
        pair_slots = {}
        pair_dma = {}
        w_oa_v = w_oa.rearrange("(c p) n -> p c n", p=128)
        w_ob_v = w_ob.rearrange("(c p) n -> p c n", p=128)
        pre_pieces = []
        for kc in range(4):
            for hf in range(2):
                pre_pieces.append((WB[:, kc * 1024 + hf * 512:kc * 1024 + (hf + 1) * 512], w_oa_v[:, kc, hf * 512:(hf + 1) * 512], 0.5))
        for kc in range(4):
            for hf in range(2):
                pre_pieces.append((WB[:, 4096 + kc * 1024 + hf * 512:4096 + kc * 1024 + (hf + 1) * 512],
                                   w_ob_v[:, kc, hf * 512:(hf + 1) * 512], 0.25))

        def piece_dma(i, pieces):
            dst, src, mul = pieces[i]
            S.dma("rdn", [(rdn[:, :], src)], writes=["rdn"])

        def piece_cast(i, pieces, keys):
            dst, src, mul = pieces[i]
            CAST("act", dst, rdn[:, :], ["rdn"], keys, mul)
        CS3 = CSETS + [None]

        def pset(it):
            cs = CSETS[it % 2]
            return (cs["ub"], cs["sq"], cs["sg"], cs["zb"], cs["mt"], cs["thz"])

        def pA(it):
            p, tc = divmod(it, NT)
            if tc == 0:
                if p == 0:
                    pair_dma[0] = (stream_dma(8), stream_dma(12))
                su_, sz_ = pair_dma[p]
                stream_cast(su_)
                stream_cast(sz_)
                pair_slots[p] = (su_, sz_)
                if p + 1 < 4:
                    pair_dma[p + 1] = (stream_dma(8 + p + 1), stream_dma(12 + p + 1))
            su, sz = pair_slots[p]
            ju = 8 + p
            (ub, kub), (sq, ksq), (sg, ksg), (zb, kzb), (mt, kmt), (thz, kthz) = pset(it)
            e = it % 2
            bu, bz = e, 2 + e
            piece_dma(it, pre_pieces)
            stage1(su, tc, bu)
            stage1(sz, tc, bz)
            ACT(ub, ps[bu][:, :], AF.Identity, [PK[bu], "bias"], kub, bias=bm[:, ju:ju + 1])
            ACT(sq, ps[bu][:, :], AF.Square, [PK[bu], "bmg"], ksq, bias=bmg[:, ju:ju + 1], scale=GK)
            ACT(zb, ps[bz][:, :], AF.Identity, [PK[bz], "bias"], kzb, bias=bm[:, 12 + p:13 + p])
            ACT(thz, ps[bz][:, :], AF.Tanh, [PK[bz], "bmh"], kthz, bias=bmh[:, 12 + p:13 + p], scale=0.5)
            STT("dve", sq, sq, 1.0, ub, ALU.add, ALU.mult, ksq + kub, ksq)
            ACT(sg, sq, AF.Tanh, ksq, ksg, scale=GS / 2)
            STT("dve", zb, thz, 1.0, zb, ALU.add, ALU.mult, kzb + kthz, kzb)

        def pB(it):
            p, tc = divmod(it, NT)
            (ub, kub), (sq, ksq), (sg, ksg), (zb, kzb), (mt, kmt), (thz, kthz) = pset(it)
            piece_cast(it, pre_pieces, ["WB"])
            for half in range(2):
                cc0 = tc * 4 + 2 * half
                bank = 4 + 2 * (it % 2) + half
                MM([(ps[bank][:, k2 * 256:(k2 + 1) * 256], vn[:, cc0 + k2, p * 128:(p + 1) * 128],
                     wsb[:, 2 * p * 128:(2 * p + 2) * 128], True, True) for k2 in range(2)],
                   vpk(512 * cc0, 512 * cc0 + 1024) + ["wsb"], [PK[bank]])
                for rows, cs_ in ((slice(0, 64), slice(0, 128)), (slice(64, 128), slice(128, 256))):
                    in0 = ps[bank][rows, :].rearrange("f (c x) -> f c x", c=2)[:, :, cs_]
                    out_ = mt[rows, half * 256:(half + 1) * 256].rearrange("f (c x) -> f c x", c=2)
                    STT("dve", out_, in0, sg_g[rows, p:p + 1], bias2[rows, p:p + 1, :].broadcast_to([64, 2, 128]),
                        ALU.mult, ALU.add, [PK[bank], "bias", "bias2"], kmt)
            STT("dve", ub, sg, 1.0, ub, ALU.add, ALU.mult, kub + ksg, kub)

        def pC(it):
            p, tc = divmod(it, NT)
            tsl = slice(tc * TC, (tc + 1) * TC)
            (ub, kub), (sq, ksq), (sg, ksg), (zb, kzb), (mt, kmt), (thz, kthz) = pset(it)
            TT("dve", mt, mt, ub, ALU.mult, kmt + kub, kmt)
            TT("dve", ybT[:, p, tsl], mt, zb, ALU.mult, kmt + kzb, ["ybT%d_%d" % (p, tc), "bufC%d" % (p // 2)])

        skew(16, [pA, pB, pC])
        DBG("ybT", ybT[:, 1, :], [128, S_], ["ybT1_%d" % i for i in range(4)], BF16)

        woa = WB[:, 0:4096].rearrange("p (c n) -> p c n", c=4)
        wob = WB[:, 4096:8192].rearrange("p (c n) -> p c n", c=4)
        wOut = bufB[:, 0:8192].rearrange("p (c n) -> p c n", c=8)
        w_out_v = w_out.rearrange("(c p) n -> p c n", p=128)
        out_pieces = []
        for kc in range(8):
            for hf in range(2):
                out_pieces.append((bufB[:, kc * 1024 + hf * 512:kc * 1024 + (hf + 1) * 512], w_out_v[:, kc, hf * 512:(hf + 1) * 512], 0.5))
        d_it = [0]
        wslotD = [(wbf[0][:, :, :], "wbf0"), (wbf[1][:, :, :], "wbf1"),
                  (wkv[:, :].rearrange("p (c n) -> p c n", c=8), "wkv"),
                  (wsb[:, :].rearrange("p (c n) -> p c n", c=8), "wsb")]

        def castD(stg_slot, w3, wkey):
            CP("dve", w3, wst[stg_slot][:, :, :], ["wst%d" % stg_slot], [wkey])

        def stageD(w3, wkey, tc_, bank):
            tsl_ = slice(tc_ * TC, (tc_ + 1) * TC)
            MM([(ps[bank][:, :], w3[:, dk, :], xTb[:, dk, tsl_], dk == 0, dk == 7) for dk in range(8)],
               XK(tc_) + [wkey], [PK[bank]])

        stg_d = (stream_dma(16), stream_dma(24))
        castD(stg_d[0], *wslotD[0])
        castD(stg_d[1], *wslotD[1])
        for m in range(8):
            (wa3, wak), (wb3, wbk) = wslotD[2 * (m % 2)], wslotD[2 * (m % 2) + 1]
            if m + 1 < 8:
                stg_d = (stream_dma(16 + m + 1), stream_dma(24 + m + 1))
            SA = T[0]
            SBt = T[1]
            for tc in range(NT):
                tsl = slice(tc * TC, (tc + 1) * TC)
                di = d_it[0]
                d_it[0] += 1
                if 1 <= di <= 16:
                    dst_, _, _ = out_pieces[di - 1]
                    o0 = (di - 1) * 512
                    piece_cast(di - 1, out_pieces, vpk(o0, o0 + 512))
                if di < 16:
                    piece_dma(di, out_pieces)
                if tc == 2 and m + 1 < 8:
                    castD(stg_d[0], *wslotD[2 * ((m + 1) % 2)])
                    castD(stg_d[1], *wslotD[2 * ((m + 1) % 2) + 1])
                stageD(wa3, wak, tc, 0)
                ACT(SA[:, tsl], ps[0][:, :], AF.Tanh, [PK[0], "bmh"], TK(0, (tc,)), bias=bmh[:, 16 + m:17 + m], scale=0.5)
                stageD(wb3, wbk, tc, 1)
                ACT(SBt[:, tsl], ps[1][:, :], AF.Tanh, [PK[1], "bmh"], TK(1, (tc,)), bias=bmh[:, 24 + m:25 + m], scale=0.5)
                e = tc % 2
                ba, bb = 2 + 2 * e, 3 + 2 * e
                MM([(ps[ba][:, :], woa[:, kc, m * 128:(m + 1) * 128], yaT[:, kc, tsl], kc == 0, kc == 3) for kc in range(4)],
                   ["WB"] + ["yaT%d_%d" % (kc, tc) for kc in range(4)], [PK[ba]])
                MM([(ps[bb][:, :], wob[:, kc, m * 128:(m + 1) * 128], ybT[:, kc, tsl], kc == 0, kc == 3) for kc in range(4)],
                   ["WB", "bufC0", "bufC1"] + ["ybT%d_%d" % (kc, tc) for kc in range(4)], [PK[bb]])
                ta = T[2][:, (2 * e) * 512:(2 * e + 1) * 512]
                tb = T[2][:, (2 * e + 1) * 512:(2 * e + 2) * 512]
                STT("dve", ta, SA[:, tsl], 1.0, ps[ba][:, :], ALU.add, ALU.mult, [PK[ba]] + TK(0, (tc,)), TK(2, (2 * e,)))
                STT("dve", tb, SBt[:, tsl], 1.0, ps[bb][:, :], ALU.add, ALU.mult, [PK[bb]] + TK(1, (tc,)), TK(2, (2 * e + 1,)))
                TT("dve", bufA[:, m, tsl], ta, tb, ALU.add, TK(2, (2 * e, 2 * e + 1)), ["kT%d_%d" % (m, tc)])
        DBG("mergedT", bufA[:, 2, :], [128, S_], ["kT2_%d" % i for i in range(4)], BF16)

        lnv = bufD[:, :, :].rearrange("p a b -> p (a b)").bitcast(F32)
        lnG = lnv[:, 0:1024]
        lnB = lnv[:, 1024:2048]
        dkeys = ["cqn%d" % i for i in range(4)] + ["ckvn%d" % i for i in range(4)]
        S.dma("lnp", [(lnG, ln_g.partition_broadcast(128)), (lnB, ln_b.partition_broadcast(128))], writes=dkeys)
        rekey("pool", ["xT%d" % i for i in range(4)], ["xs%d" % i for i in range(8)])
        xsl = xTb[:, :, :].rearrange("p a b -> p (a b)").bitcast(F32)
        OTs = [T[1][:, 0:1024], T[1][:, 1024:2048], T[2][:, 0:1024], T[2][:, 1024:2048]]
        OTk = [TK(1, (0, 1)), TK(1, (2, 3)), TK(2, (0, 1)), TK(2, (2, 3))]

        def emit_xload(t):
            sl_ = t % 8
            S.dma("xs%d" % sl_, [(xsl[:, sl_ * 1024:(sl_ + 1) * 1024], x[t * 128:(t + 1) * 128, :])], writes=["xs%d" % sl_])

        for t in range(8):
            emit_xload(t)
        def eA(t):
            e = t % 2
            sl_ = t % 8
            o4 = t % 4
            XT = xsl[:, sl_ * 1024:(sl_ + 1) * 1024]
            OT = OTs[o4]
            r = T[0][:, e * 1024:(e + 1) * 1024]
            rk = TK(0, (2 * e, 2 * e + 1))
            st12 = small[:, 32 + e * 12:32 + e * 12 + 12]
            mv = small[:, 56 + e * 2:58 + e * 2]
            rs = small[:, 60 + e:61 + e]
            nmr = small[:, 62 + e:63 + e]
            sk = ["smallE%d" % e]
            for half in range(2):
                bank = 2 * e + half
                hs = slice(half * 512, (half + 1) * 512)
                MM([(ps[bank][:, :], bufA[:, kc, t * 128:(t + 1) * 128], wOut[:, kc, hs], kc == 0, kc == 7)
                    for kc in range(8)], ["kT%d_%d" % (kc, t // 4) for kc in range(8)] + vpk(0, 8192), [PK[bank]])
                STT("dve", r[:, hs], XT[:, hs], ALPHA, ps[bank][:, :], ALU.mult, ALU.add, ["xs%d" % sl_, PK[bank]], TK(0, (2 * e + half,)))
                S.op("dve", lambda en, o=st12[:, half * 6:half * 6 + 6], i=r[:, hs]: en.bn_stats(out=o, in_=i),
                     TK(0, (2 * e + half,)), sk)

        def eB(t):
            e = t % 2
            sl_ = t % 8
            o4 = t % 4
            XT = xsl[:, sl_ * 1024:(sl_ + 1) * 1024]
            OT = OTs[o4]
            r = T[0][:, e * 1024:(e + 1) * 1024]
            rk = TK(0, (2 * e, 2 * e + 1))
            st12 = small[:, 32 + e * 12:32 + e * 12 + 12]
            mv = small[:, 56 + e * 2:58 + e * 2]
            rs = small[:, 60 + e:61 + e]
            nmr = small[:, 62 + e:63 + e]
            sk = ["smallE%d" % e]
            S.op("dve", lambda en, o=mv, i=st12: en.bn_aggr(out=o, in_=i), sk, sk)
            TS("pool", rs, mv[:, 1:2], LN_EPS, None, ALU.add, None, sk, sk)
            TT("pool", rs, rs, nhalf[:, 0:1], ALU.pow, sk + ["nhalf"], sk)
            STT("dve", nmr, mv[:, 0:1], -1.0, rs, ALU.mult, ALU.mult, sk, sk)
            ACT(OT, r, AF.Identity, rk + sk, OTk[o4], bias=nmr, scale=rs)

        def eC(t):
            e = t % 2
            sl_ = t % 8
            o4 = t % 4
            XT = xsl[:, sl_ * 1024:(sl_ + 1) * 1024]
            OT = OTs[o4]
            r = T[0][:, e * 1024:(e + 1) * 1024]
            rk = TK(0, (2 * e, 2 * e + 1))
            st12 = small[:, 32 + e * 12:32 + e * 12 + 12]
            mv = small[:, 56 + e * 2:58 + e * 2]
            rs = small[:, 60 + e:61 + e]
            nmr = small[:, 62 + e:63 + e]
            sk = ["smallE%d" % e]
            TT("dve", OT, OT, lnG, ALU.mult, OTk[o4] + ["cqn0"], OTk[o4])
            TT("dve", OT, OT, lnB, ALU.add, OTk[o4] + ["cqn0"], OTk[o4])
            S.dma("ot%d" % o4, [(out[t * 128:(t + 1) * 128, :], OT)], reads=OTk[o4])
            if t + 8 < 16:
                emit_xload(t + 8)

        skew(16, [eA, eB, eC])
        S.wait_all_dma()
        S.emit()
    return nc, dbg_out


def _prep(x, positions, w_in, b_in, g_q, w_uq, g_kv, w_ukv, w_oa, sgu_ln_g, sgu_ln_b, w_s, b_s,
          w_ob, w_out, ln_g, ln_b):
    f = lambda a: np.ascontiguousarray(np.asarray(a), dtype=np.float32)
    W = f(w_in)[0]
    b = f(b_in)[0]
    cols = np.r_[0:512, 544:1568, 2080:4640]
    perm = np.r_[528:544, 512:528]
    w_kpe = np.zeros((D_, 2, 96), np.float32)
    w_kpe[:, 0, 64:96] = W[:, 512:544]
    w_kpe[:, 1, 64:96] = W[:, perm]
    b_kpe = np.zeros((128, 2), np.float32)
    b_kpe[64:96, 0] = b[512:544]
    b_kpe[64:96, 1] = b[perm]
    wq = f(w_uq)[0]
    w_uqb = np.zeros((384, 8, 96), np.float32)
    w_uqb[:, :, 64:96] = wq[:, :, 64 + np.r_[16:32, 0:16]]
    wkv_ = f(w_ukv)[0]
    consts = np.zeros((128, 388), np.float32)
    consts[:, 0:128] = np.eye(128, dtype=np.float32)
    kk = np.arange(128)[:, None]
    qq = np.arange(128)[None, :]
    consts[:, 128:256] = np.where(kk <= qq, 0.0, -30000.0)
    consts[:, 256:384] = np.where(kk <= qq, 1.0, 0.0)
    inv_freq = (np.float32(10000.0) ** (-np.arange(0, 32, 2, dtype=np.float32) / np.float32(32))).astype(np.float32)
    consts[64:96, 384] = np.concatenate([inv_freq, inv_freq])
    consts[64:80, 385] = -1.0
    consts[80:96, 385] = 1.0
    shared = {
        "w_main": np.ascontiguousarray(W[:, cols]),
        "b_main": np.ascontiguousarray(b[cols].reshape(32, 128).T),
        "w_v": np.ascontiguousarray(W[:, 1568:2080]),
        "b_v": np.ascontiguousarray(b[1568:2080][None, :]),
        "w_kpe": np.ascontiguousarray(w_kpe.reshape(D_, 192)),
        "b_kpe": b_kpe,
        "w_uqa": np.ascontiguousarray(wq.reshape(384, 768)),
        "w_uqb": np.ascontiguousarray(w_uqb.reshape(384, 768)),
        "g_q": np.ascontiguousarray(f(g_q)[0].reshape(3, 128).T),
        "g_kv": np.ascontiguousarray(f(g_kv)[0].reshape(1, 128).T),
        "w_kn": np.ascontiguousarray(wkv_[:, :, 0:64].reshape(128, 512)),
        "w_vv": np.ascontiguousarray(wkv_[:, :, 64:128].reshape(128, 512)),
        "w_oa": f(w_oa)[0],
        "w_ob": f(w_ob)[0],
        "w_out": f(w_out)[0],
        "sgu_g": np.ascontiguousarray(f(sgu_ln_g)[0].reshape(4, 128).T),
        "sgu_b": np.ascontiguousarray(f(sgu_ln_b)[0].reshape(4, 128).T),
        "w_sT": np.ascontiguousarray(np.transpose(f(w_s)[0], (2, 0, 1)).reshape(128, 1024)),
        "b_s": f(b_s)[0],
        "ln_g": f(ln_g)[0][None, :],
        "ln_b": f(ln_b)[0][None, :],
        "consts": consts,
    }
    xs = f(x)
    ps_ = np.ascontiguousarray(np.asarray(positions), dtype=np.int32)
    in_maps = []
    for bi in range(xs.shape[0]):
        m = dict(shared)
        m["x"] = xs[bi]
        m["xT"] = np.ascontiguousarray(xs[bi].T)
        m["pos"] = ps_[bi][None, :]
        in_maps.append(m)
    return in_maps


def kernel(**inputs):
    in_maps = _prep(**inputs)
    nc, _ = build()
    res = run_bass_kernel_spmd(nc, in_maps, core_ids=list(range(8)))
    return np.stack([np.asarray(r["out"], dtype=np.float32) for r in res.results], axis=0)
```
